# Optimizing a Trainium2 kernel written in Bass

```python
import math
import jax
import jax.numpy as jnp
from jax import lax
import numpy as np

D_MODEL = 1024
BATCH = 32
SEQ = 256
DEPTH = 1
DEC_BATCH = 2
DEC_SEQ = 1024
PAST_LEN = 256

GRID_W = 64
ATT_WIDTH = D_MODEL // 2
HY_WIDTH = D_MODEL - ATT_WIDTH
MIX_WIDTH = ATT_WIDTH + HY_WIDTH
DA_HEADS = 4
DA_HEAD_DIM = ATT_WIDTH // (2 * DA_HEADS)
HY_ORDER = 2
HY_EMB_DIM = 33
HY_BANDS = (HY_EMB_DIM - 1) // 2
HY_FILTER_WIDTH = 64
HY_DECAY_TARGET = 1e-2
HY_SHORT_PCT = 0.3
HY_LONG_PCT = 1.5
IN_COLS = 3 * ATT_WIDTH + 3 * HY_WIDTH
D_FF = -(-8 * D_MODEL // (3 * 256)) * 256
DEEPNORM_ALPHA = (2.0 * DEPTH) ** 0.25
DEEPNORM_BETA = (8.0 * DEPTH) ** -0.25
ROPE_BASE = 10000.0
LN_EPS = 1e-5
Q_BLOCK = 128

kernel_name = 'hybrid_diffattn_hyena_prefix_dit_step'


def layer_norm(x, g=None, b=None):
    xf = x.astype(jnp.float32)
    mu = jnp.mean(xf, axis=-1, keepdims=True)
    var = jnp.mean(jnp.square(xf - mu), axis=-1, keepdims=True)
    y = (xf - mu) * lax.rsqrt(var + LN_EPS)
    if g is not None:
        y = y * g.astype(jnp.float32) + b.astype(jnp.float32)
    return y.astype(x.dtype)


def rms_norm(x, g):
    xf = x.astype(jnp.float32)
    return xf * lax.rsqrt(jnp.mean(xf * xf, axis=-1, keepdims=True) + LN_EPS) * g.astype(jnp.float32)


def adaln_params(cvec, w, b):
    m = jax.nn.silu(cvec) @ w + b
    return jnp.split(m[..., None, :], 6, axis=-1)


def axial_rope_tables(n):
    rows = n // GRID_W
    pos = jnp.arange(rows * GRID_W)
    row = (pos // GRID_W).astype(jnp.float32)
    col = (pos % GRID_W).astype(jnp.float32)
    half = DA_HEAD_DIM // 2
    inv = ROPE_BASE ** (-jnp.arange(0, half, 2, dtype=jnp.float32) / half)
    ang = jnp.stack([row[:, None] * inv, col[:, None] * inv], axis=1)
    return jnp.cos(ang), jnp.sin(ang)


def apply_axial_rope(x, cos, sin):
    xs = x.astype(jnp.float32).reshape(x.shape[:-1] + (2, 2, DA_HEAD_DIM // 4))
    x1, x2 = xs[..., 0, :], xs[..., 1, :]
    c_, s_ = cos[None, :, None], sin[None, :, None]
    out = jnp.stack([x1 * c_ - x2 * s_, x1 * s_ + x2 * c_], axis=-2)
    return out.reshape(x.shape).astype(x.dtype)


def diff_attention(q, k, v, lam):
    B, n = q.shape[:2]
    nblk = n // Q_BLOCK
    qb = q.reshape(B, nblk, Q_BLOCK, 2 * DA_HEADS, DA_HEAD_DIM).transpose(1, 0, 2, 3, 4)
    kf = k.astype(jnp.float32)
    vf = v.astype(jnp.float32)
    scale = DA_HEAD_DIM ** -0.5

    def block(qblk):
        s = jnp.einsum('bqhd,bkhd->bhqk', qblk.astype(jnp.float32), kf) * scale
        p = jax.nn.softmax(s, axis=-1).reshape(B, DA_HEADS, 2, Q_BLOCK, -1)
        a = p[:, :, 0] - lam * p[:, :, 1]
        return jnp.einsum('bhqk,bkhe->bqhe', a, vf)

    o = lax.map(block, qb)
    return o.transpose(1, 0, 2, 3, 4).reshape(B, n, DA_HEADS, 2 * DA_HEAD_DIM)


def short_conv(u, w, b):
    n = u.shape[1]
    up = jnp.pad(u, ((0, 0), (1, 1), (0, 0)))
    return up[:, :n] * w[0] + up[:, 1:n + 1] * w[1] + up[:, 2:] * w[2] + b


def hyena_filters(n, w1, b1, w2, b2, freq, w3, decay):
    f32 = jnp.float32
    pos = jnp.arange(n, dtype=f32)
    t = (pos / (n - 1))[:, None]
    bands = jnp.linspace(1e-4, HY_BANDS - 1, HY_BANDS, dtype=f32)
    ang = (2.0 * math.pi / n) * pos[:, None] * bands
    z = jnp.concatenate([t, jnp.cos(ang), -jnp.sin(ang)], axis=-1)
    fr = freq.astype(f32)
    hdn = jnp.sin(fr * (z @ w1.astype(f32) + b1.astype(f32)))
    hdn = jnp.sin(fr * (hdn @ w2.astype(f32) + b2.astype(f32)))
    filt = (hdn @ w3.astype(f32)).reshape(n, HY_ORDER, 2, HY_WIDTH)
    return filt * jnp.exp(-t[:, :, None, None] * jnp.abs(decay.astype(f32)))


def bidir_long_conv(u, h_fwd, h_bwd, skip):
    n = u.shape[1]
    circ = jnp.concatenate([h_fwd, jnp.zeros_like(h_fwd[:1]), h_bwd[:0:-1]], axis=0)
    hf = jnp.fft.rfft(circ, axis=0)
    uf = jnp.fft.rfft(u, n=2 * n, axis=1)
    y = jnp.fft.irfft(uf * hf[None], n=2 * n, axis=1)[:, :n]
    return y + u * skip


def hyena_mixer(u, conv_w, conv_b, w1, b1, w2, b2, freq, w3, decay, skip):
    f32 = jnp.float32
    n = u.shape[1]
    uc = short_conv(u.astype(f32), conv_w.astype(f32), conv_b.astype(f32))
    v, x1, x2 = jnp.split(uc, 3, axis=-1)
    filt = hyena_filters(n, w1, b1, w2, b2, freq, w3, decay)
    sk = skip.astype(f32)
    z = x1 * bidir_long_conv(v, filt[:, 0, 0], filt[:, 0, 1], sk[0])
    return x2 * bidir_long_conv(z, filt[:, 1, 0], filt[:, 1, 1], sk[1])


def setup_inputs(seed: int = 0) -> dict:
    key = jax.random.key(seed)
    ks = jax.random.split(key, 32)
    f32 = jnp.float32

    def nrm(k, shape, s):
        return s * jax.random.normal(k, shape, f32)

    decay_lo = -math.log(HY_DECAY_TARGET) / HY_LONG_PCT
    decay_hi = -math.log(HY_DECAY_TARGET) / HY_SHORT_PCT
    base_decay = jnp.broadcast_to(jnp.linspace(decay_lo, decay_hi, HY_WIDTH, dtype=f32),
                                  (DEPTH, HY_ORDER, 2, HY_WIDTH))
    return {
        'x_prompt': nrm(ks[0], (BATCH, SEQ, D_MODEL), 1.0),
        'x_sample': nrm(ks[1], (DEC_BATCH, DEC_SEQ, D_MODEL), 1.0),
        'cache_k': nrm(ks[2], (DEC_BATCH, DEPTH, PAST_LEN, 2 * DA_HEADS, DA_HEAD_DIM), 1.0),
        'cache_v': nrm(ks[3], (DEC_BATCH, DEPTH, PAST_LEN, DA_HEADS, 2 * DA_HEAD_DIM), 1.0),
        'c': nrm(ks[4], (DEC_BATCH, D_MODEL), 1.0),
        'c_ctx': nrm(ks[5], (D_MODEL,), 1.0),
        'mod_w': nrm(ks[6], (DEPTH, D_MODEL, 6 * D_MODEL), 0.5 * D_MODEL ** -0.5),
        'mod_b': nrm(ks[7], (DEPTH, 6 * D_MODEL), 0.01),
        'w_in': nrm(ks[8], (DEPTH, D_MODEL, IN_COLS), D_MODEL ** -0.5),
        'da_lq1': nrm(ks[9], (DEPTH, DA_HEAD_DIM), 0.1),
        'da_lk1': nrm(ks[10], (DEPTH, DA_HEAD_DIM), 0.1),
        'da_lq2': nrm(ks[11], (DEPTH, DA_HEAD_DIM), 0.1),
        'da_lk2': nrm(ks[12], (DEPTH, DA_HEAD_DIM), 0.1),
        'da_subln': 1.0 + nrm(ks[13], (DEPTH, 2 * DA_HEAD_DIM), 0.01),
        'hy_conv_w': nrm(ks[14], (DEPTH, 3, 3 * HY_WIDTH), 3 ** -0.5),
        'hy_conv_b': nrm(ks[15], (DEPTH, 3 * HY_WIDTH), 0.01),
        'hy_w1': nrm(ks[16], (DEPTH, HY_EMB_DIM, HY_FILTER_WIDTH), HY_EMB_DIM ** -0.5),
        'hy_b1': nrm(ks[17], (DEPTH, HY_FILTER_WIDTH), 0.01),
        'hy_w2': nrm(ks[18], (DEPTH, HY_FILTER_WIDTH, HY_FILTER_WIDTH), HY_FILTER_WIDTH ** -0.5),
        'hy_b2': nrm(ks[19], (DEPTH, HY_FILTER_WIDTH), 0.01),
        'hy_freq': 1.0 + nrm(ks[20], (DEPTH, HY_FILTER_WIDTH), 0.01),
        'hy_w3': nrm(ks[21], (DEPTH, HY_FILTER_WIDTH, HY_ORDER * 2 * HY_WIDTH), 0.1 * HY_FILTER_WIDTH ** -0.5),
        'hy_decay': base_decay + nrm(ks[22], (DEPTH, HY_ORDER, 2, HY_WIDTH), 0.1),
        'hy_skip': nrm(ks[23], (DEPTH, HY_ORDER, HY_WIDTH), 1.0),
        'w_out': nrm(ks[24], (DEPTH, MIX_WIDTH, D_MODEL), DEEPNORM_BETA * MIX_WIDTH ** -0.5),
        'ln1_g': 1.0 + nrm(ks[25], (DEPTH, D_MODEL), 0.01),
        'ln1_b': nrm(ks[26], (DEPTH, D_MODEL), 0.01),
        'w_up': nrm(ks[27], (DEPTH, D_MODEL, 2 * D_FF), D_MODEL ** -0.5),
        'w_down': nrm(ks[28], (DEPTH, D_FF, D_MODEL), DEEPNORM_BETA * D_FF ** -0.5),
        'ln2_g': 1.0 + nrm(ks[29], (DEPTH, D_MODEL), 0.01),
        'ln2_b': nrm(ks[30], (DEPTH, D_MODEL), 0.01),
    }


def reference(x_prompt, x_sample, cache_k, cache_v, c, c_ctx, mod_w, mod_b, w_in,
              da_lq1, da_lk1, da_lq2, da_lk2, da_subln, hy_conv_w, hy_conv_b,
              hy_w1, hy_b1, hy_w2, hy_b2, hy_freq, hy_w3, hy_decay, hy_skip,
              w_out, ln1_g, ln1_b, w_up, w_down, ln2_g, ln2_b):

    def layer(x, cvec, ctx_k, ctx_v, li, latent):
        B, n, _ = x.shape
        sh1, sc1, g1, sh2, sc2, g2 = adaln_params(cvec, mod_w[li], mod_b[li])
        h = layer_norm(x) * (1 + sc1) + sh1
        proj = h @ w_in[li]
        q, k, v, hy_in = jnp.split(proj, [ATT_WIDTH, 2 * ATT_WIDTH, 3 * ATT_WIDTH], axis=-1)
        q = q.reshape(B, n, 2 * DA_HEADS, DA_HEAD_DIM)
        k = k.reshape(B, n, 2 * DA_HEADS, DA_HEAD_DIM)
        v = v.reshape(B, n, DA_HEADS, 2 * DA_HEAD_DIM)
        if latent:
            cos, sin = axial_rope_tables(n)
            q = apply_axial_rope(q, cos, sin)
            keys = jnp.concatenate([ctx_k, apply_axial_rope(k, cos, sin)], axis=1)
            vals = jnp.concatenate([ctx_v, v], axis=1)
        else:
            keys, vals = k, v
        lam_init = 0.8 - 0.6 * math.exp(-0.3 * li)
        lam = (jnp.exp(jnp.sum(da_lq1[li].astype(jnp.float32) * da_lk1[li].astype(jnp.float32)))
               - jnp.exp(jnp.sum(da_lq2[li].astype(jnp.float32) * da_lk2[li].astype(jnp.float32)))
               + lam_init)
        att = diff_attention(q, keys, vals, lam)
        att = (rms_norm(att, da_subln[li]) * (1 - lam_init)).reshape(B, n, ATT_WIDTH).astype(x.dtype)
        hy = hyena_mixer(hy_in, hy_conv_w[li], hy_conv_b[li], hy_w1[li], hy_b1[li], hy_w2[li],
                         hy_b2[li], hy_freq[li], hy_w3[li], hy_decay[li], hy_skip[li]).astype(x.dtype)
        mix = jnp.concatenate([att, hy], axis=-1) @ w_out[li]
        x = layer_norm(DEEPNORM_ALPHA * x + g1 * mix, ln1_g[li], ln1_b[li])
        h = layer_norm(x) * (1 + sc2) + sh2
        gate, up = jnp.split(h @ w_up[li], 2, axis=-1)
        ffn = (jax.nn.silu(gate) * up) @ w_down[li]
        x = layer_norm(DEEPNORM_ALPHA * x + g2 * ffn, ln2_g[li], ln2_b[li])
        return x, k, v

    y_prompt = x_prompt
    ks_new, vs_new = [], []
    for li in range(DEPTH):
        y_prompt, k_l, v_l = layer(y_prompt, c_ctx, None, None, li, False)
        ks_new.append(k_l)
        vs_new.append(v_l)
    new_cache_k = jnp.stack(ks_new, axis=1)
    new_cache_v = jnp.stack(vs_new, axis=1)

    y_sample = x_sample
    for li in range(DEPTH):
        y_sample, _, _ = layer(y_sample, c, cache_k[:, li], cache_v[:, li], li, True)

    return (y_prompt, y_sample, new_cache_k, new_cache_v)
```

```python
import math
from contextlib import ExitStack
import numpy as np
import ml_dtypes
import concourse.bass as bass
import concourse.mybir as mybir
from concourse.bass_utils import run_bass_kernel_spmd

F32 = mybir.dt.float32
BF16 = mybir.dt.bfloat16
AF = mybir.ActivationFunctionType
ALU = mybir.AluOpType
AX = mybir.AxisListType

D = 1024
NCORE = 8
NDS = 12
LAM_INIT = 0.8 - 0.6 * math.exp(-0.3 * 0)
ALPHA = 2.0 ** 0.25
EPS = 1e-5
DFF = 2816
TWO_PI = 2.0 * math.pi


class KB:
    def __init__(self):
        self.nc = bass.Bass("TRN2", target_bir_lowering=False)
        nc = self.nc
        self.es = ExitStack()
        self.E = {'pe': nc.tensor, 'act': nc.scalar, 'dve': nc.vector, 'pool': nc.gpsimd, 'sp': nc.sync}
        self.sem = {}
        self.cnt = {}
        for e in self.E:
            nm = 'c_' + e
            self.sem[nm] = self.es.enter_context(nc.semaphore(nm))
            self.cnt[nm] = 0
        self.dq = {'sp': [], 'pool': []}
        for q in self.dq:
            for i in range(NDS):
                nm = f'd_{q}{i}'
                self.sem[nm] = self.es.enter_context(nc.semaphore(nm))
                self.cnt[nm] = 0
                self.dq[q].append(nm)
        self.dqi = {'sp': 0, 'pool': 0}
        self.waited = {e: {} for e in self.E}
        self.lw = {}
        self.rd = {}
        self.nps = 0
        self.dumps = []

    def sb(self, name, shape, dt, scope=None):
        return (scope or self.es).enter_context(self.nc.sbuf_tensor("s_" + name, list(shape), dt))

    def ps(self, name, shape, dt):
        return self.es.enter_context(self.nc.psum_tensor("p_" + name, list(shape), dt))

    def _wait(self, e, evs):
        need = {}
        for ev in evs:
            if ev is None:
                continue
            s, v = ev
            if need.get(s, 0) < v:
                need[s] = v
        for s, v in need.items():
            if self.waited[e].get(s, 0) < v:
                self.E[e].wait_ge(self.sem[s], v)
                self.waited[e][s] = v

    def _deps(self, e, r, w):
        own = 'c_' + e
        evs = []
        for k in r:
            evs.append(self.lw.get(k))
        for k in w:
            for ev in [self.lw.get(k)] + list(self.rd.get(k, {}).items()):
                if ev is not None and not (e == 'pe' and ev[0] == own):
                    evs.append(ev)
        return evs

    def _commit(self, ev, r, w):
        for k in r:
            d = self.rd.setdefault(k, {})
            if d.get(ev[0], 0) < ev[1]:
                d[ev[0]] = ev[1]
        for k in w:
            self.lw[k] = ev
            self.rd[k] = {}

    def op(self, e, fn, r=(), w=()):
        evs = self._deps(e, r, w)
        if e != 'pe':
            claim = [k for k in r if isinstance(k, tuple) and len(k) == 2 and k[0] in ('pb', 'pbT') and k not in w]
            own = 'c_' + e
            for k in claim:
                for ev in [self.lw.get(k)] + list(self.rd.get(k, {}).items()):
                    if ev is not None and ev[0] != own:
                        evs.append(ev)
            w = list(w) + claim
        self._wait(e, evs)
        ins = fn()
        s = 'c_' + e
        self.cnt[s] += 1
        ins.then_inc(self.sem[s], 1)
        self._commit((s, self.cnt[s]), r, w)

    def mmg(self, fns, r=(), w=()):
        self._wait('pe', self._deps('pe', r, w))
        ins = None
        for fn in fns:
            ins = fn()
        s = 'c_pe'
        self.cnt[s] += 1
        ins.then_inc(self.sem[s], 1)
        self._commit((s, self.cnt[s]), r, w)

    def dma(self, q, out, in_, r=(), w=()):
        nm = self.dq[q][self.dqi[q]]
        self.dqi[q] = (self.dqi[q] + 1) % NDS
        evs = self._deps(q, r, w)
        if self.cnt[nm] > 0:
            evs.append((nm, self.cnt[nm]))
        self._wait(q, evs)
        ins = self.E[q].dma_start(out=out, in_=in_)
        self.cnt[nm] += 16
        ins.then_inc(self.sem[nm], 16)
        self._commit((nm, self.cnt[nm]), r, w)

    def barrier(self):
        for e in self.E:
            self._wait(e, [(s, v) for s, v in self.cnt.items() if v > 0])
        self.lw = {}
        self.rd = {}

    def finish(self):
        self._wait('sp', [(s, v) for s, v in self.cnt.items() if v > 0])

    def bank(self):
        i = self.nps % 8
        self.nps += 1
        return i


def _dft_tables(n):
    s = np.arange(n, dtype=np.float64)
    th = np.pi / n
    ang = th * np.outer(s, s)
    cf = np.cos(ang)
    bf = np.sin(ang)
    bf[:, 0] = (-1.0) ** s
    bft = bf.T.copy()
    wk = np.full((n,), 1.0 / n)
    wk[0] = 1.0 / (2 * n)
    return cf, bf, bft, wk


def _z_table(n):
    pos = np.arange(n, dtype=np.float32)
    t = (pos / np.float32(n - 1))[:, None]
    bands = np.linspace(1e-4, 16 - 1, 16, dtype=np.float32)
    ang = (np.float32(2.0 * math.pi / n) * pos[:, None] * bands).astype(np.float32)
    z = np.concatenate([t, np.cos(ang), -np.sin(ang)], axis=-1).astype(np.float32)
    return z, t[:, 0].astype(np.float32)


def _rope_tables_unused(n):
    pos = np.arange(n)
    row = (pos // 64).astype(np.float32)
    col = (pos % 64).astype(np.float32)
    inv = (10000.0 ** (-np.arange(0, 32, 2, dtype=np.float32) / 32)).astype(np.float32)
    p = np.arange(128)
    d = p % 64
    a = d // 32
    r = (d % 32) // 16
    i = d % 16
    posa = np.where(a[:, None] == 0, row[None, :], col[None, :])
    ang = (posa * inv[i][:, None]).astype(np.float32)
    cos = np.cos(ang).astype(np.float32)
    sin = np.sin(ang).astype(np.float32) * np.where(r == 0, -1.0, 1.0)[:, None].astype(np.float32)
    return cos, sin


_CONST_CACHE = {}


def _consts():
    if _CONST_CACHE:
        return _CONST_CACHE
    bf = ml_dtypes.bfloat16
    c = {}
    for n in (256, 1024):
        cf, bfm, bft, wk = _dft_tables(n)
        c[f'cf{n}'] = cf.astype(np.float32).astype(bf)
        c[f'bf{n}'] = bfm.astype(np.float32).astype(bf)
        c[f'bft{n}'] = bft.astype(np.float32).astype(bf)
        nt = n // 128
        c[f'wk{n}'] = np.ascontiguousarray(wk.reshape(nt, 128).T).astype(np.float32)
        z, t = _z_table(n)
        c[f'zt{n}'] = np.ascontiguousarray(z.T).astype(np.float32)
        c[f'nt{n}'] = np.ascontiguousarray((-t).reshape(nt, 128).T).astype(np.float32)
    pos = np.arange(1024)
    row = (pos // 64).astype(np.float32); col = (pos % 64).astype(np.float32)
    inv = (10000.0 ** (-np.arange(0, 32, 2, dtype=np.float32) / 32)).astype(np.float32)
    d = np.arange(64); a = d // 32; r = (d % 32) // 16; ii = d % 16
    posa = np.where(a[None, :] == 0, row[:, None], col[:, None]).astype(np.float32)
    ang = (posa * inv[ii][None, :]).astype(np.float32)
    tab = np.stack([np.cos(ang), np.sin(ang) * np.where(r == 0, -1.0, 1.0)[None, :]], axis=1).astype(np.float32)
    c['rope'] = tab
    c['identb'] = np.eye(128, dtype=np.float32).astype(bf)
    c['identf'] = np.eye(128, dtype=np.float32)
    sel = np.zeros((2, 2, 128), np.float32)
    sel[0, 0, :] = 1.0
    sel[1, 1, :] = 1.0
    c['sel2'] = sel
    _CONST_CACHE.update(c)
    return c


def build(stop_after=99, debug=()):
    kb = KB()
    nc = kb.nc
    G = kb.es

    def din(name, shape, dt=F32):
        return nc.dram_tensor(name, list(shape), dt, kind="ExternalInput").ap()

    def dout(name, shape):
        return nc.dram_tensor(name, list(shape), F32, kind="ExternalOutput").ap()

    xpD = din('xp', [1024, 1024]); xsD = din('xs', [1024, 1024]); xoD = din('xo', [258, 1024])
    ckD = din('ck', [256, 512]); cvD = din('cv', [256, 512]); cvecD = din('cvec', [2, 1024])
    modwD = din('mod_w', [1024, 6144]); modbD = din('mod_b', [6144]); winD = din('w_in', [1024, 3072])
    lqD = [din(n, [64]) for n in ('lq1', 'lk1', 'lq2', 'lk2')]
    sublnD = din('subln', [128])
    cwD = din('conv_w', [3, 1536]); cbD = din('conv_b', [1536])
    hw1D = din('hw1', [33, 64]); hb1D = din('hb1', [64]); hw2D = din('hw2', [64, 64]); hb2D = din('hb2', [64])
    hfD = din('hfreq', [64]); hw3D = din('hw3', [64, 2048]); hdecD = din('hdecay', [2048]); hskD = din('hskip', [1024])
    woutD = din('w_out', [1024, 1024]); ln1gD = din('ln1_g', [1024]); ln1bD = din('ln1_b', [1024])
    wupD = din('w_up', [1024, 5632]); wdnD = din('w_down', [2816, 1024]); ln2gD = din('ln2_g', [1024]); ln2bD = din('ln2_b', [1024])
    cfD = {n: din(f'cf{n}', [n, n], BF16) for n in (256, 1024)}
    bfD = {n: din(f'bf{n}', [n, n], BF16) for n in (256, 1024)}
    bftD = {n: din(f'bft{n}', [n, n], BF16) for n in (256, 1024)}
    wkD = {n: din(f'wk{n}', [128, n // 128]) for n in (256, 1024)}
    ztD = {n: din(f'zt{n}', [33, n]) for n in (256, 1024)}
    ntD = {n: din(f'nt{n}', [128, n // 128]) for n in (256, 1024)}
    cfoD = din('cfo', [1024, 256], BF16); bftoD = din('bfto', [1024, 256], BF16)
    ropeSD = din('ropeS', [128, 8, 2, 64]); ropeOD = din('ropeO', [128, 2, 2, 64])
    hmaskD = din('hmask', [128, 2])
    identbD = din('identb', [128, 128], BF16); identfD = din('identf', [128, 128]); sel2D = din('sel2', [2, 2, 128])
    ypD = dout('yp', [1024, 1024]); yoD = dout('yo', [256, 1024]); nkD = dout('nk', [1024, 512]); nvD = dout('nv', [1024, 512])
    spP = {(n, o): nc.dram_tensor(f'spP{n}_{o}', [128, n // 128, 512], BF16).ap() for n in (256, 1024) for o in (0, 1)}
    spQ = {(n, o): nc.dram_tensor(f'spQ{n}_{o}', [128, n // 128, 512], BF16).ap() for n in (256, 1024) for o in (0, 1)}
    spN = {(n, o): nc.dram_tensor(f'spN{n}_{o}', [1, 512], F32).ap() for n in (256, 1024) for o in (0, 1)}
    dbg = {}

    def dump(name, ap, shape, rkey):
        if name in debug:
            kb.barrier()
            d = nc.dram_tensor('dbg_' + name, list(shape), F32, kind="ExternalOutput").ap()
            if ap.dtype == F32:
                kb.dma('sp', d, ap, r=[rkey])
            else:
                with ExitStack() as ds:
                    tmp = kb.sb('dt_' + name, list(shape), F32, ds)
                    kb.op('dve', lambda: nc.vector.tensor_copy(tmp[:], ap), r=[rkey], w=['dtmp'])
                    kb.dma('sp', d, tmp[:], r=['dtmp'])
                    kb.barrier()

    V = lambda fn, r=(), w=(): kb.op('dve', fn, r, w)
    A = lambda fn, r=(), w=(): kb.op('act', fn, r, w)
    P = lambda fn, r=(), w=(): kb.op('pool', fn, r, w)
    T = lambda fn, r=(), w=(): kb.op('pe', fn, r, w)
    MM = nc.tensor.matmul
    cnt = [0]

    def alt(fa, fv, r, w):
        cnt[0] += 1
        if fa is None:
            V(fv, r, w)
        elif cnt[0] % 2:
            A(fa, r, w)
        else:
            V(fv, r, w)

    def pipeline_gen(units, depth=1, oldest_first=False, order=None):
        n = len(units)
        S = max(len(u) for u in units)
        for step in range(n + (S - 1) * depth):
            for st_ in (order if order is not None else (reversed(range(S)) if oldest_first else range(S))):
                u = step - st_ * depth
                if 0 <= u < n and st_ < len(units[u]):
                    units[u][st_]()
            yield step

    def pipeline(units, depth=1, oldest_first=False, order=None):
        for _ in pipeline_gen(units, depth, oldest_first, order):
            pass

    NPB = 6
    pb = [kb.ps(f'pb{i}', [128, 512], F32) for i in range(NPB)]
    pbb = [kb.ps(f'pbT{i}', [128, 1024], BF16) for i in range(2)]
    nbk = [0, 0]

    def bank():
        i = nbk[0] % NPB
        nbk[0] += 1
        return i, ('pb', i)

    def tbank():
        i = nbk[1] % 2
        nbk[1] += 1
        return i, ('pbT', i)

    identb = kb.sb('identb', [128, 128], BF16); identf = kb.sb('identf', [128, 128], F32)
    sel2 = kb.sb('sel2', [2, 2, 128], F32)
    epst = kb.sb('epst', [128, 1], F32); negpi = kb.sb('negpi', [128, 1], F32)
    modT = kb.sb('modT', [128, 48, 2], F32)
    gbc = {(c, g): kb.sb(f'gbc{c}{g}', [128, 1024], F32) for c in (0, 1) for g in (0, 1)}
    nlam = kb.sb('nlam', [128, 1], F32)
    sublnbc = kb.sb('sublnbc', [128, 128], F32)
    convp = kb.sb('convp', [128, 12, 4], F32)
    NR = 3
    ring = [kb.sb(f'ring{i}', [128, 8, 512], BF16) for i in range(NR)]
    kb.dma('sp', identb[:], identbD, w=['identb'])
    kb.dma('sp', identf[:], identfD, w=['identf'])
    kb.dma('sp', sel2[:], sel2D, w=['sel2'])
    V(lambda: nc.vector.memset(epst[:], EPS), w=['epst'])
    V(lambda: nc.vector.memset(negpi[:], -math.pi), w=['negpi'])

    plan = []
    for b in range(12):
        plan.append(modwD.rearrange('(kc p) c -> p kc c', p=128)[:, :, b * 512:(b + 1) * 512])
    for b in (0, 1, 2, 5, 3, 4):
        plan.append(winD.rearrange('(kc p) c -> p kc c', p=128)[:, :, b * 512:(b + 1) * 512])
    for b in range(2):
        plan.append(woutD.rearrange('(kc p) c -> p kc c', p=128)[:, :, b * 512:(b + 1) * 512])
    issued = [0]

    def acquire(i):
        while issued[0] < min(len(plan), i + NR):
            j = issued[0]
            kb.dma('pool', ring[j % NR][:], plan[j], w=[('ring', j % NR)])
            issued[0] += 1
        return ring[i % NR], ('ring', i % NR)

    with ExitStack() as sc:
        crow = kb.sb('crow', [2, 1024], F32, sc); srow = kb.sb('srow', [2, 1024], F32, sc)
        sT = kb.sb('sT', [128, 16], BF16, sc)
        mbb = [kb.sb(f'mbb{i}', [2, 512], F32, sc) for i in range(2)]; mrow = [kb.sb(f'mrow{i}', [2, 512], F32, sc) for i in range(2)]
        lq = kb.sb('lq', [128, 4, 64], F32, sc); lpr = kb.sb('lpr', [128, 2, 64], F32, sc); ls = kb.sb('ls', [128, 2], F32, sc)
        cwrow = kb.sb('cwrow', [4, 1536], F32, sc)

        def p1_prologue():
            kb.dma('sp', crow[:], cvecD, w=['crow'])
            A(lambda: nc.scalar.activation(srow[:], crow[:], AF.Silu), r=['crow'], w=['srow'])
            i, k = bank()
            for kc in range(8):
                T(lambda: nc.tensor.transpose(pb[i][:, kc * 2:kc * 2 + 2], srow[0:2, kc * 128:(kc + 1) * 128], identf[0:2, 0:2]), r=['srow', 'identf'], w=[k])
            V(lambda: nc.vector.tensor_copy(sT[:], pb[i][:, 0:16]), r=[k], w=['sT'])

        def p1_block(blk):
            def f():
                slot, sk = acquire(blk)
                mb_, mr_ = mbb[blk % 2], mrow[blk % 2]
                i, k = bank()
                kb.mmg([(lambda kc=kc: MM(pb[i][0:2, :], lhsT=sT[:, kc * 2:kc * 2 + 2], rhs=slot[:, kc, :], start=(kc == 0), stop=(kc == 7))) for kc in range(8)], r=['sT', sk], w=[k])
                kb.dma('sp', mb_[:], modbD[blk * 512:(blk + 1) * 512].partition_broadcast(2), w=[('mbb', blk % 2)])
                V(lambda: nc.vector.tensor_tensor(mr_[:], pb[i][0:2, :], mb_[:], ALU.add), r=[k, ('mbb', blk % 2)], w=[('mrow', blk % 2)])
                j, kj = bank()
                for q in range(4):
                    T(lambda: nc.tensor.transpose(pb[j][:, q * 2:q * 2 + 2], mr_[0:2, q * 128:(q + 1) * 128], identf[0:2, 0:2]), r=[('mrow', blk % 2), 'identf'], w=[kj])
                V(lambda: nc.vector.tensor_copy(modT[:, blk * 4:(blk + 1) * 4, :], pb[j][:, 0:8].rearrange('p (a b) -> p a b', b=2)), r=[kj], w=['modT'])
                if blk in (4, 5, 10, 11):
                    g = 0 if blk < 6 else 1
                    half = blk % 2
                    for cvi in (0, 1):
                        i2, k2 = bank()
                        T(lambda: MM(pb[i2][:, :], lhsT=sel2[0:2, cvi, :], rhs=mr_[0:2, :], start=True, stop=True), r=['sel2', ('mrow', blk % 2)], w=[k2])
                        A(lambda: nc.scalar.copy(gbc[(cvi, g)][:, half * 512:(half + 1) * 512], pb[i2][:, :]), r=[k2], w=[('gbc', cvi, g)])
            return f

        def p1_epilogue():
            V(lambda: nc.vector.tensor_scalar(modT[:, 8:16, :], modT[:, 8:16, :], 1.0, None, ALU.add), r=['modT'], w=['modT'])
            V(lambda: nc.vector.tensor_scalar(modT[:, 32:40, :], modT[:, 32:40, :], 1.0, None, ALU.add), r=['modT'], w=['modT'])
            for q in range(4):
                kb.dma('sp', lq[:, q, :], lqD[q].partition_broadcast(128), w=['lq'])
            V(lambda: nc.vector.tensor_tensor(lpr[:, 0, :], lq[:, 0, :], lq[:, 1, :], ALU.mult), r=['lq'], w=['lpr'])
            V(lambda: nc.vector.tensor_tensor(lpr[:, 1, :], lq[:, 2, :], lq[:, 3, :], ALU.mult), r=['lq'], w=['lpr'])
            V(lambda: nc.vector.reduce_sum(ls[:], lpr[:], axis=AX.X), r=['lpr'], w=['ls'])
            A(lambda: nc.scalar.activation(ls[:], ls[:], AF.Exp), r=['ls'], w=['ls'])
            V(lambda: nc.vector.scalar_tensor_tensor(nlam[:], ls[:, 1:2], -LAM_INIT, ls[:, 0:1], ALU.add, ALU.subtract), r=['ls'], w=['nlam'])
            kb.dma('sp', sublnbc[:], sublnD.partition_broadcast(128), w=['sublnbc'])
            V(lambda: nc.vector.tensor_scalar(sublnbc[:], sublnbc[:], 1.0 - LAM_INIT, None, ALU.mult), r=['sublnbc'], w=['sublnbc'])
            kb.dma('sp', cwrow[0:3, :], cwD, w=['cwrow'])
            kb.dma('sp', cwrow[3:4, :], cbD.unsqueeze(0), w=['cwrow'])
            i, k = bank()
            for ct in range(12):
                T(lambda: nc.tensor.transpose(pb[i][:, ct * 4:ct * 4 + 4], cwrow[0:4, ct * 128:(ct + 1) * 128], identf[0:4, 0:4]), r=['cwrow', 'identf'], w=[k])
            V(lambda: nc.vector.tensor_copy(convp[:], pb[i][:, 0:48].rearrange('p (a b) -> p a b', b=4)), r=[k], w=['convp'])

        p1_units = [p1_prologue] + [p1_block(b_) for b_ in range(12)] + [p1_epilogue]
        p1_next = [0]

        def p1_tick(n_=1):
            for _ in range(n_):
                if p1_next[0] < len(p1_units):
                    p1_units[p1_next[0]]()
                    p1_next[0] += 1

        w1s = kb.sb('w1s', [33, 64], F32, sc); w2s = kb.sb('w2s', [64, 64], F32, sc)
        w3f = kb.sb('w3f', [64, 2048], F32, sc); w3b = kb.sb('w3b', [64, 2048], BF16, sc)
        frow = kb.sb('frow', [3, 64], F32, sc); fmv = kb.sb('fmv', [64, 4], F32, sc)
        decbc = kb.sb('decbc', [128, 2048], F32, sc); skrow = kb.sb('skrow', [1, 1024], F32, sc)
        kb.dma('sp', w1s[:], hw1D, w=['w1s']); kb.dma('sp', w2s[:], hw2D, w=['w2s'])
        kb.dma('sp', w3f[:], hw3D, w=['w3f'])
        kb.dma('sp', frow[0:1, :], hfD.unsqueeze(0), w=['frow'])
        kb.dma('sp', frow[1:2, :], hb1D.unsqueeze(0), w=['frow'])
        kb.dma('sp', frow[2:3, :], hb2D.unsqueeze(0), w=['frow'])
        kb.dma('sp', decbc[:], hdecD.partition_broadcast(128), w=['decbc'])
        kb.dma('sp', skrow[:], hskD.unsqueeze(0), w=['skrow'])
        p1_tick(2)
        V(lambda: nc.vector.tensor_copy(w3b[:], w3f[:]), r=['w3f'], w=['w3b'])
        A(lambda: nc.scalar.activation(decbc[:], decbc[:], AF.Abs), r=['decbc'], w=['decbc'])
        i, k = bank()
        T(lambda: nc.tensor.transpose(pb[i][0:64, 0:3], frow[0:3, :], identf[0:3, 0:3]), r=['frow', 'identf'], w=[k])
        V(lambda: nc.vector.tensor_copy(fmv[:, 0:3], pb[i][0:64, 0:3]), r=[k], w=['fmv'])
        V(lambda: nc.vector.tensor_scalar(fmv[:, 1:3], fmv[:, 1:3], fmv[:, 0:1], None, ALU.mult), r=['fmv'], w=['fmv'])
        for n in (1024, 256):
            NT = n // 128
            with ExitStack() as s2:
                cf = kb.sb(f'f_cf{n}', [128, NT, n], BF16, s2); bfm = kb.sb(f'f_bf{n}', [128, NT, n], BF16, s2)
                kb.dma('sp', cf[:], cfD[n].rearrange('(st p) k -> p st k', p=128), w=['cf'])
                kb.dma('sp', bfm[:], bfD[n].rearrange('(st p) k -> p st k', p=128), w=['bf'])
                zt = kb.sb(f'zt{n}', [33, n], F32, s2); ntt = kb.sb(f'ntt{n}', [128, NT], F32, s2)
                wkt = kb.sb(f'wkt{n}', [128, NT], F32, s2)
                kb.dma('sp', zt[:], ztD[n], w=['zt']); kb.dma('sp', ntt[:], ntD[n], w=['ntt'])
                kb.dma('sp', wkt[:], wkD[n], w=['wkt'])
                arg = kb.sb(f'arg{n}', [64, n], F32, s2); h1 = kb.sb(f'h1{n}', [64, n], F32, s2)
                h2b = kb.sb(f'h2b{n}', [64, n], BF16, s2)
                rr = kb.sb(f'rr{n}', [64, 512], F32, s2); rr2 = kb.sb(f'rr2{n}', [64, 512], F32, s2)
                for (wl, wlk, src, srck, dst, dstk, bcol) in ((w1s, 'w1s', zt, 'zt', h1, 'h1', 1), (w2s, 'w2s', h1, 'h1', h2b, 'h2b', 2)):
                    for c0 in range(0, n, 512):
                        cw = min(512, n - c0)
                        i, k = bank()
                        T(lambda: MM(pb[i][0:64, 0:cw], lhsT=wl[:], rhs=src[:, c0:c0 + cw], start=True, stop=True), r=[wlk, srck], w=[k])
                        V(lambda: nc.vector.tensor_scalar(arg[:, c0:c0 + cw], pb[i][0:64, 0:cw], fmv[:, 0:1], fmv[:, bcol:bcol + 1], ALU.mult, ALU.add), r=[k, 'fmv'], w=[('arg', c0)])
                        V(lambda: nc.vector.tensor_scalar(rr[:, 0:cw], arg[:, c0:c0 + cw], math.pi, -TWO_PI, ALU.is_gt, ALU.mult), r=[('arg', c0)], w=['rr'])
                        V(lambda: nc.vector.tensor_scalar(rr2[:, 0:cw], arg[:, c0:c0 + cw], -math.pi, TWO_PI, ALU.is_lt, ALU.mult), r=[('arg', c0)], w=['rr2'])
                        V(lambda: nc.vector.tensor_tensor(rr[:, 0:cw], rr[:, 0:cw], rr2[:, 0:cw], ALU.add), r=['rr', 'rr2'], w=['rr'])
                        V(lambda: nc.vector.tensor_tensor(arg[:, c0:c0 + cw], arg[:, c0:c0 + cw], rr[:, 0:cw], ALU.add), r=[('arg', c0), 'rr'], w=[('arg', c0)])
                        A(lambda: nc.scalar.activation(dst[:, c0:c0 + cw], arg[:, c0:c0 + cw], AF.Sin), r=[('arg', c0)], w=[dstk])
                    p1_tick()
                hsd = kb.sb(f'hsd{n}', [128, NT, 512], BF16, s2); hdd = kb.sb(f'hdd{n}', [128, NT, 512], BF16, s2)
                dEs = [kb.sb(f'dE{n}_{q}', [128, 1024], F32, s2) for q in range(2)]
                f0s = [kb.sb(f'f0{n}_{q}', [128, 512], F32, s2) for q in range(2)]; f1s = [kb.sb(f'f1{n}_{q}', [128, 512], F32, s2) for q in range(2)]
                Pq = kb.sb(f'Pq{n}', [128, NT, 512], BF16, s2); Qq = kb.sb(f'Qq{n}', [128, NT, 512], BF16, s2)
                pnr = kb.sb(f'pnr{n}', [1, 512], F32, s2)
                for o in (0, 1):
                    for st in range(NT):
                        q = st % 2
                        dE, f0, f1 = dEs[q], f0s[q], f1s[q]
                        A(lambda: nc.scalar.activation(dE[:], decbc[:, o * 1024:(o + 1) * 1024], AF.Exp, scale=ntt[:, st:st + 1]), r=['decbc', 'ntt'], w=[('dE', q)])
                        ia, ka = bank(); ib, kbk = bank()
                        T(lambda: MM(pb[ia][:, :], lhsT=h2b[:, st * 128:(st + 1) * 128], rhs=w3b[:, o * 1024:o * 1024 + 512], start=True, stop=True), r=['h2b', 'w3b'], w=[ka])
                        T(lambda: MM(pb[ib][:, :], lhsT=h2b[:, st * 128:(st + 1) * 128], rhs=w3b[:, o * 1024 + 512:o * 1024 + 1024], start=True, stop=True), r=['h2b', 'w3b'], w=[kbk])
                        V(lambda: nc.vector.tensor_tensor(f0[:], pb[ia][:, :], dE[:, 0:512], ALU.mult), r=[ka, ('dE', q)], w=[('f0', q)])
                        V(lambda: nc.vector.tensor_tensor(f1[:], pb[ib][:, :], dE[:, 512:1024], ALU.mult), r=[kbk, ('dE', q)], w=[('f1', q)])
                        if st == 0:
                            V(lambda: nc.vector.memset(f1[0:1, :], 0.0), r=[('f1', q)], w=[('f1', q)])
                            V(lambda: nc.vector.tensor_tensor(f0[0:1, :], f0[0:1, :], skrow[0:1, o * 512:(o + 1) * 512], ALU.add), r=[('f0', q), 'skrow'], w=[('f0', q)])
                        P(lambda: nc.gpsimd.tensor_tensor(hsd[:, st, :], f0[:], f1[:], ALU.add), r=[('f0', q), ('f1', q)], w=[('hsd', st)])
                        V(lambda: nc.vector.tensor_tensor(hdd[:, st, :], f0[:], f1[:], ALU.subtract), r=[('f0', q), ('f1', q)], w=[('hdd', st)])
                        if st % 2 == 1:
                            p1_tick()
                    hs_keys = [('hsd', st) for st in range(NT)]; hd_keys = [('hdd', st) for st in range(NT)]
                    for kt in range(NT):
                        ia, ka = bank(); ib, kbk = bank()
                        kb.mmg([(lambda st=st: MM(pb[ia][:, :], lhsT=cf[:, st, kt * 128:(kt + 1) * 128], rhs=hsd[:, st, :], start=(st == 0), stop=(st == NT - 1))) for st in range(NT)], r=['cf'] + hs_keys, w=[ka])
                        kb.mmg([(lambda st=st: MM(pb[ib][:, :], lhsT=bfm[:, st, kt * 128:(kt + 1) * 128], rhs=hdd[:, st, :], start=(st == 0), stop=(st == NT - 1))) for st in range(NT)], r=['bf'] + hd_keys, w=[kbk])
                        A(lambda: nc.scalar.activation(Pq[:, kt, :], pb[ia][:, :], AF.Identity, scale=wkt[:, kt:kt + 1]), r=[ka, 'wkt'], w=[('Pq', kt)])
                        V(lambda: nc.vector.tensor_scalar(Qq[:, kt, :], pb[ib][:, :], wkt[:, kt:kt + 1], None, ALU.mult), r=[kbk, 'wkt'], w=[('Qq', kt)])
                        if kt == 0:
                            V(lambda: nc.vector.memset(Qq[0:1, 0, :], 0.0), r=[('Qq', 0)], w=[('Qq', 0)])
                        if kt % 2 == 1:
                            p1_tick()
                    i, k = bank()
                    kb.mmg([(lambda st=st: MM(pb[i][0:1, :], lhsT=bfm[:, st, 0:1], rhs=hsd[:, st, :], start=(st == 0), stop=(st == NT - 1))) for st in range(NT)], r=['bf'] + hs_keys, w=[k])
                    V(lambda: nc.vector.tensor_scalar(pnr[:], pb[i][0:1, :], 1.0 / (2 * n), None, ALU.mult), r=[k], w=['pnr'])
                    kb.dma('sp', spP[(n, o)], Pq[:], r=[('Pq', kt) for kt in range(NT)], w=[('spP', n, o)])
                    kb.dma('sp', spQ[(n, o)], Qq[:], r=[('Qq', kt) for kt in range(NT)], w=[('spQ', n, o)])
                    kb.dma('sp', spN[(n, o)], pnr[:], r=['pnr'], w=[('spN', n, o)])
                if n == 256:
                    p1_tick(100)
                kb.barrier()
        dump('modT', modT[:], [128, 48, 2], 'modT')
        kb.barrier()
    if stop_after <= 1:
        kb.finish()
        return kb

    S_mix = ExitStack(); S_hy = ExitStack(); S_hyP = ExitStack(); S_at = ExitStack(); S_h = ExitStack()
    hTP = kb.sb('hTP', [128, 8, 1024], BF16, S_mix); hTO = kb.sb('hTO', [128, 8, 264], BF16, S_mix)
    mixP = hTP; mixO = hTO
    vS = kb.sb('vS', [128, 8, 512], BF16, S_hy); x1S = kb.sb('x1S', [128, 8, 512], BF16, S_hy); x2O = kb.sb('x2O', [128, 4, 256], BF16, S_hy)
    vP = kb.sb('vP', [128, 8, 512], BF16, S_hyP); x1P = kb.sb('x1P', [128, 8, 512], BF16, S_hyP); x2P = kb.sb('x2P', [128, 4, 1024], BF16, S_hyP)
    QTP = kb.sb('QTP', [128, 4, 1024], BF16, S_at); KTP = kb.sb('KTP', [128, 4, 1024], BF16, S_at)
    VP = kb.sb('VP', [128, 8, 4, 130], BF16, S_at)
    QTO = kb.sb('QTO', [128, 4, 256], BF16, S_at); KTS = kb.sb('KTS', [128, 4, 1280], BF16, S_at)
    VS = kb.sb('VS', [128, 10, 4, 130], BF16, S_at)
    hTS = kb.sb('hTS', [128, 8, 1024], BF16, S_h)

    def ln_rstd(mv_ap, rstd_ap, lnv_ap, npart, rk, wk_):
        A(lambda: nc.scalar.activation(lnv_ap, mv_ap, AF.Ln, bias=epst[0:npart, 0:1], scale=1.0), r=[rk, 'epst'], w=[wk_ + 'l'])
        A(lambda: nc.scalar.activation(rstd_ap, lnv_ap, AF.Exp, scale=-0.5), r=[wk_ + 'l'], w=[wk_])

    def transpose_mod(xn, xnk, nt, dst, dstk, t0, cvi, sc_c0, sh_c0, defer=False):
        for hb in range(2):
            kb.mmg([(lambda kc=kc: nc.tensor.transpose(pbb[hb][:, (kc - 4 * hb) * 128:(kc - 4 * hb) * 128 + nt], xn[0:nt, kc * 128:(kc + 1) * 128], identb[0:nt, 0:nt])) for kc in range(4 * hb, 4 * hb + 4)],
                   r=[xnk, 'identb'], w=[('pbT', hb)])

        def evac():
            for kc in range(4):
                A(lambda: nc.scalar.activation(dst[:, kc, t0:t0 + nt], pbb[0][:, kc * 128:kc * 128 + nt], AF.Identity, bias=modT[:, sh_c0 + kc, cvi:cvi + 1], scale=modT[:, sc_c0 + kc, cvi:cvi + 1]), r=[('pbT', 0), 'modT'], w=[(dstk, kc)])
            for kc in range(4, 8):
                V(lambda: nc.vector.tensor_scalar(dst[:, kc, t0:t0 + nt], pbb[1][:, (kc - 4) * 128:(kc - 4) * 128 + nt], modT[:, sc_c0 + kc, cvi:cvi + 1], modT[:, sh_c0 + kc, cvi:cvi + 1], ALU.mult, ALU.add), r=[('pbT', 1), 'modT'], w=[(dstk, kc)])
        if defer:
            return evac
        evac()

    with ExitStack() as sc:
        NB = 4
        xt = [kb.sb(f'xt{i}', [128, 1024], F32, sc) for i in range(NB)]
        xn = [kb.sb(f'xn{i}', [128, 1024], BF16, sc) for i in range(NB)]
        st = [kb.sb(f'st{i}', [128, 12], F32, sc) for i in range(NB)]
        mv = [kb.sb(f'mv{i}', [128, 4], F32, sc) for i in range(NB)]
        units2 = []

        def unit_ln1(xD, t0, nt, cvi, hT, hk, b):
            X, N, S_, M = xt[b], xn[b], st[b], mv[b]

            def sL():
                kb.dma('sp', X[0:nt, :], xD[t0:t0 + nt, :], w=[('xt', b)])

            def s0():
                V(lambda: nc.vector.bn_stats(S_[0:nt, 0:6], X[0:nt, 0:512]), r=[('xt', b)], w=[('st', b, 0)])
                V(lambda: nc.vector.bn_stats(S_[0:nt, 6:12], X[0:nt, 512:1024]), r=[('xt', b)], w=[('st', b, 1)])
                V(lambda: nc.vector.bn_aggr(M[0:nt, 0:2], S_[0:nt, :]), r=[('st', b, 0), ('st', b, 1)], w=[('mv', b)])
                ln_rstd(M[0:nt, 1:2], M[0:nt, 3:4], M[0:nt, 2:3], nt, ('mv', b), f'rs{b}')
                V(lambda: nc.vector.scalar_tensor_tensor(M[0:nt, 2:3], M[0:nt, 0:1], -1.0, M[0:nt, 3:4], ALU.mult, ALU.mult), r=[('mv', b), f'rs{b}'], w=[f'nmr{b}'])

            st_ = {}

            def s1():
                A(lambda: nc.scalar.activation(N[0:nt, :], X[0:nt, :], AF.Identity, bias=M[0:nt, 2:3], scale=M[0:nt, 3:4]), r=[('xt', b), f'rs{b}', f'nmr{b}'], w=[('xn', b)])
                st_['ev'] = transpose_mod(N, ('xn', b), nt, hT, hk, t0, cvi, 8, 0, defer=True)

            def s2():
                st_['ev']()
            return [sL, s0, s1, s2]

        rot = 0
        for (xD, ntok, cvi, hT, hk) in ((xpD, 1024, 0, hTP, 'hTP'), (xsD, 1024, 1, hTS, 'hTS'), (xoD, 258, 1, hTO, 'hTO')):
            for t0 in range(0, ntok, 128):
                units2.append(unit_ln1(xD, t0, min(128, ntok - t0), cvi, hT, hk, rot % NB))
                rot += 1
        pipeline(units2, 1, order=[0, 1, 3, 2])
        dump('hTP', hTP[:, :, 0:32], [128, 8, 32], ('hTP', 7))
        kb.barrier()
    if stop_after <= 2:
        kb.finish()
        return kb


    S_t3 = ExitStack()

    class Rot:
        def __init__(self, name, n, shape, dt, scope):
            self.t = [kb.sb(f'{name}{i}', shape, dt, scope) for i in range(n)]
            self.name = name
            self.i = -1

        def next(self):
            self.i += 1
            j = self.i % len(self.t)
            return self.t[j], (self.name, j)

    kstR = Rot('kst', 2, [128, 512], F32, S_t3); kbfR = Rot('kbf', 2, [128, 512], BF16, S_t3)
    ropeS = kb.sb('ropeS', [128, 8, 2, 64], F32, S_t3); ropeO = kb.sb('ropeO', [128, 2, 2, 64], F32, S_t3)
    hmask = kb.sb('hmask', [128, 2], F32, S_t3)
    usPt = [kb.sb(f'usP{i}', [128, 1032], F32, S_t3) for i in range(2)]; usSt = [kb.sb(f'usS{i}', [128, 1032], F32, S_t3) for i in range(2)]
    usO = kb.sb('usO', [128, 258], F32, S_t3)
    accR = Rot('acc', 2, [128, 1024], F32, S_t3); cvoR = Rot('cvo', 2, [128, 1024], BF16, S_t3)
    kb.dma('sp', ropeS[:], ropeSD, w=['ropeS']); kb.dma('sp', ropeO[:], ropeOD, w=['ropeO']); kb.dma('sp', hmask[:], hmaskD, w=['hmask'])
    V(lambda: nc.vector.memset(VP[:, :, :, 128:130], 1.0), w=['VPones'])
    V(lambda: nc.vector.memset(VS[:, :, :, 128:130], 1.0), w=['VSones'])

    def proj_tm(slot, sk, hT, t0, nt):
        i, k = bank()
        kb.mmg([(lambda kc=kc: MM(pb[i][0:nt, :], lhsT=hT[:, kc, t0:t0 + nt], rhs=slot[:, kc, :], start=(kc == 0), stop=(kc == 7))) for kc in range(8)], r=[sk], w=[k])
        return i, k

    def proj_fm(slot, sk, ct, hT, t0, nt):
        i, k = bank()
        kb.mmg([(lambda kc=kc: MM(pb[i][:, 0:nt], lhsT=slot[:, kc, ct * 128:(ct + 1) * 128], rhs=hT[:, kc, t0:t0 + nt], start=(kc == 0), stop=(kc == 7))) for kc in range(8)], r=[sk], w=[k])
        return i, k

    def to_fm(src_bf, srck, dst, dstk, c0):
        j, kj = tbank()
        kb.mmg([(lambda ct=ct: nc.tensor.transpose(pbb[j][:, ct * 128:(ct + 1) * 128], src_bf[:, ct * 128:(ct + 1) * 128], identb[:, :])) for ct in range(4)], r=[srck, 'identb'], w=[kj])
        V(lambda: nc.vector.tensor_copy(dst[:, 0:4, c0:c0 + 128], pbb[j][:, 0:512].rearrange('p (a b) -> p a b', b=128)), r=[kj], w=[(dstk, c0)])

    def rope_tile(K_, kk, tab, tabk, tt):
        rt, rtk = usPt[1][:, 0:512], ('usP', 1); ru, ruk = usSt[1][:, 0:512], ('usS', 1); B_, bk = kbfR.next()
        x3 = K_[:].rearrange('p (m d) -> p m d', d=64)
        cosb = tab[:, tt, 0, :].unsqueeze(1).to_broadcast([128, 8, 64])
        V(lambda: nc.vector.tensor_tensor(rt.rearrange('p (m d) -> p m d', d=64), x3, cosb, ALU.mult), r=[kk, tabk], w=[rtk])
        x5 = K_[:].rearrange('p (m a r i) -> p m a r i', m=8, a=2, r=2, i=16)
        u5 = ru.rearrange('p (m a r i) -> p m a r i', m=8, a=2, r=2, i=16)
        s5 = tab[:, tt, 1, :].rearrange('p (a r i) -> p a r i', a=2, r=2)
        for r_ in (0, 1):
            P(lambda: nc.gpsimd.tensor_tensor(u5[:, :, :, r_, :], x5[:, :, :, 1 - r_, :], s5[:, :, r_, :].unsqueeze(1).to_broadcast([128, 8, 2, 16]), ALU.mult), r=[kk, tabk], w=[ruk + (r_,)])
        V(lambda: nc.vector.tensor_tensor(B_[:], rt, ru, ALU.add), r=[rtk, ruk + (0,), ruk + (1,)], w=[bk])
        return B_, bk

    def evac_kst(i, k):
        K_, kk = kstR.next()
        A(lambda: nc.scalar.copy(K_[:], pb[i][:, :]), r=[k], w=[kk])
        return K_, kk

    def cast_bf(K_, kk):
        B_, bk = kbfR.next()
        V(lambda: nc.vector.tensor_copy(B_[:], K_[:]), r=[kk], w=[bk])
        return B_, bk

    units3 = []
    cur = {}

    def u_acq(blk):
        def f():
            cur['slot'], cur['sk'] = acquire(blk)
        return f

    def unit_q_p(ct, ch):
        def s0():
            i, k = proj_fm(cur['slot'], cur['sk'], ct, hTP, ch * 512, 512)
            alt(lambda: nc.scalar.copy(QTP[:, ct, ch * 512:(ch + 1) * 512], pb[i][:, :]),
                lambda: nc.vector.tensor_copy(QTP[:, ct, ch * 512:(ch + 1) * 512], pb[i][:, :]), r=[k], w=[('QTP', ct, ch)])
        return [s0]

    def unit_rope(hT, t0, tab, tabk, tt, dst, dstk, c0):
        st_ = {}

        def s0():
            i, k = proj_tm(cur['slot'], cur['sk'], hT, t0, 128)
            K_, kk = evac_kst(i, k)
            st_['b'] = rope_tile(K_, kk, tab, tabk, tt)

        def s1():
            to_fm(st_['b'][0], st_['b'][1], dst, dstk, c0)
        return [s0, s1]

    def unit_k_p(tt):
        st_ = {}

        def s0():
            i, k = proj_tm(cur['slot'], cur['sk'], hTP, tt * 128, 128)
            K_, kk = evac_kst(i, k)
            kb.dma('sp', nkD[tt * 128:(tt + 1) * 128, :], K_[:], r=[kk], w=[('nk', tt)])
            st_['b'] = cast_bf(K_, kk)

        def s1():
            to_fm(st_['b'][0], st_['b'][1], KTP, 'KTP', tt * 128)
        return [s0, s1]

    def unit_ctx_k(kt):
        st_ = {}

        def s0():
            K_, kk = kstR.next()
            kb.dma('sp', K_[:], ckD[kt * 128:(kt + 1) * 128, :], w=[kk])
            st_['b'] = cast_bf(K_, kk)

        def s1():
            to_fm(st_['b'][0], st_['b'][1], KTS, 'KTS', kt * 128)
        return [s0, s1]

    def unit_v_p(tt):
        def s0():
            i, k = proj_tm(cur['slot'], cur['sk'], hTP, tt * 128, 128)
            K_, kk = evac_kst(i, k)
            kb.dma('sp', nvD[tt * 128:(tt + 1) * 128, :], K_[:], r=[kk], w=[('nv', tt)])
            V(lambda: nc.vector.tensor_copy(VP[:, tt, :, 0:128], K_[:].rearrange('p (a b) -> p a b', b=128)), r=[kk], w=[('VP', tt)])
        return [s0]

    def unit_v_s(tt):
        def s0():
            i, k = proj_tm(cur['slot'], cur['sk'], hTS, tt * 128, 128)
            V(lambda: nc.vector.tensor_copy(VS[:, 2 + tt, :, 0:128], pb[i][:, :].rearrange('p (a b) -> p a b', b=128)), r=[k], w=[('VS', 2 + tt)])
        return [s0]

    def unit_ctx_v(kt):
        def s0():
            K_, kk = kstR.next()
            kb.dma('sp', K_[:], cvD[kt * 128:(kt + 1) * 128, :], w=[kk])
            V(lambda: nc.vector.tensor_copy(VS[:, kt, :, 0:128], K_[:].rearrange('p (a b) -> p a b', b=128)), r=[kk], w=[('VS', kt)])
        return [s0]

    units3.append([u_acq(12)])
    for tt in range(2):
        units3.append(unit_rope(hTO, 1 + tt * 128, ropeO, 'ropeO', tt, QTO, 'QTO', tt * 128))
    for kt in range(2):
        units3.append(unit_ctx_k(kt))
    for kt in range(2):
        units3.append(unit_ctx_v(kt))
    for ct in range(4):
        for ch in range(2):
            units3.append(unit_q_p(ct, ch))
    units3.append([u_acq(13)])
    for tt in range(8):
        units3.append(unit_k_p(tt))
        units3.append(unit_rope(hTS, tt * 128, ropeS, 'ropeS', tt, KTS, 'KTS', 256 + tt * 128))
    units3.append([u_acq(14)])
    for tt in range(8):
        units3.append(unit_v_p(tt))
        units3.append(unit_v_s(tt))

    usPv = [t_[:, 0:1032].rearrange('p (s t) -> p s t', t=258) for t_ in usPt]
    rotP = [0]; rotS = [0]

    def conv3(ul, um, ur, usk, accv, acck, outv, ctg, outk):
        A(lambda: nc.scalar.activation(accv, um, AF.Identity, bias=convp[:, ctg, 3:4], scale=convp[:, ctg, 1:2]), r=[usk, 'convp'], w=[acck])
        V(lambda: nc.vector.scalar_tensor_tensor(accv, ul, convp[:, ctg, 0:1], accv, ALU.mult, ALU.add), r=[usk, acck, 'convp'], w=[acck])
        V(lambda: nc.vector.scalar_tensor_tensor(outv, ur, convp[:, ctg, 2:3], accv, ALU.mult, ALU.add), r=[usk, acck, 'convp'], w=[outk])

    def to_tm(C_, ck_, dst, dstk, ct):
        j, kj = tbank()
        kb.mmg([(lambda tt=tt: nc.tensor.transpose(pbb[j][:, tt * 128:(tt + 1) * 128], C_[:, tt * 128:(tt + 1) * 128], identb[:, :])) for tt in range(8)], r=[ck_, 'identb'], w=[kj])
        V(lambda: nc.vector.tensor_copy(dst[:, 0:8, ct * 128:(ct + 1) * 128], pbb[j][:, :].rearrange('p (a b) -> p a b', b=128)), r=[kj], w=[(dstk, ct)])

    def zero_pads():
        for q in range(2):
            V(lambda: nc.vector.memset(usPt[q][:], 0.0), w=[('usP', q)])
            V(lambda: nc.vector.memset(usSt[q][:], 0.0), w=[('usS', q), ('usS', q, 0), ('usS', q, 1)])

    def hy_P(slot, sk, ct, ctg, outv, outk):
        q = rotP[0] % 2; rotP[0] += 1
        usP = usPv[q]
        for ch in range(2):
            i, k = proj_fm(slot, sk, ct, hTP, ch * 512, 512)
            A(lambda: nc.scalar.copy(usP[:, 2 * ch:2 * ch + 2, 1:257], pb[i][:, :].rearrange('p (s t) -> p s t', t=256)), r=[k], w=[('usP', q)])
        ac_, ak = accR.next()
        a3 = ac_[:].rearrange('p (s t) -> p s t', t=256)
        conv3(usP[:, :, 0:256], usP[:, :, 1:257], usP[:, :, 2:258], ('usP', q), a3, ak, outv, ctg, outk)

    def hy_S(slot, sk, ct, ctg, outv, outk):
        q = rotS[0] % 2; rotS[0] += 1
        us = usSt[q]
        for ch in range(2):
            i, k = proj_fm(slot, sk, ct, hTS, ch * 512, 512)
            A(lambda: nc.scalar.copy(us[:, 1 + ch * 512:1 + (ch + 1) * 512], pb[i][:, :]), r=[k], w=[('usS', q)])
        ac_, ak = accR.next()
        conv3(us[:, 0:1024], us[:, 1:1025], us[:, 2:1026], ('usS', q), ac_[:], ak, outv, ctg, outk)

    def unit_hy(which, ct, ctg, dst, dstk):
        st_ = {}

        def s0():
            C_, ck_ = cvoR.next()
            st_['c'] = (C_, ck_)
            if which == 'P':
                hy_P(cur['slot'], cur['sk'], ct, ctg, C_[:].rearrange('p (s t) -> p s t', t=256), ck_)
            else:
                hy_S(cur['slot'], cur['sk'], ct, ctg, C_[:], ck_)

        def s1():
            to_tm(st_['c'][0], st_['c'][1], dst, dstk, ct)
        return [s0, s1]

    def unit_x2(ct):
        ctg = 8 + ct

        def s0():
            hy_P(cur['slot'], cur['sk'], ct, ctg, x2P[:, ct, :].rearrange('p (s t) -> p s t', t=256), ('x2P', ct))

        def s1():
            i, k = proj_fm(cur['slot'], cur['sk'], ct, hTO, 0, 258)
            A(lambda: nc.scalar.copy(usO[:], pb[i][:, 0:258]), r=[k], w=['usO'])
            V(lambda: nc.vector.tensor_scalar(usO[:, 0:1], usO[:, 0:1], hmask[:, 0:1], None, ALU.mult), r=['usO', 'hmask'], w=['usO'])
            V(lambda: nc.vector.tensor_scalar(usO[:, 257:258], usO[:, 257:258], hmask[:, 1:2], None, ALU.mult), r=['usO', 'hmask'], w=['usO'])
            ac_, ak = accR.next()
            conv3(usO[:, 0:256], usO[:, 1:257], usO[:, 2:258], 'usO', ac_[:, 0:256], ak, x2O[:, ct, :], ctg, ('x2O', ct))
        return [lambda: (s0(), s1())]

    units3.append([lambda: (u_acq(15)(), zero_pads())])
    for ct in range(4):
        units3.append(unit_x2(ct))
    for bi, (dP, dPk, dS, dSk) in enumerate(((vP, 'vP', vS, 'vS'), (x1P, 'x1P', x1S, 'x1S'))):
        units3.append([u_acq(16 + bi)])
        for ct in range(4):
            units3.append(unit_hy('P', ct, bi * 4 + ct, dP, dPk))
            units3.append(unit_hy('S', ct, bi * 4 + ct, dS, dSk))
    pipeline(units3, 1)
    kb.barrier()
    S_t3.close(); S_h.close()
    dump('QTO', QTO[:], [128, 4, 256], 'x')
    dump('KTS', KTS[:, :, 0:384], [128, 4, 384], 'x')
    dump('vS', vS[:, 0:2, :], [128, 2, 512], 'x')
    dump('x2O', x2O[:], [128, 4, 256], 'x')
    dump('x1P', x1P[:, 0:2, :], [128, 2, 512], 'x')
    if stop_after <= 3:
        kb.finish()
        return kb

    with ExitStack() as sc:
        EP = [[kb.sb(f'EP{s_}_{m}', [128, 2, 256], BF16, sc) for m in range(8)] for s_ in range(2)]
        EO = [kb.sb(f'EO_{m}', [128, 10, 256], BF16, sc) for m in range(4)]
        on = [kb.sb(f'on{q}', [128, 8, 128], F32, sc) for q in range(2)]
        araw = [kb.sb(f'araw{q}', [128, 4, 128], F32, sc) for q in range(2)]
        an = [kb.sb(f'an{q}', [128, 4, 128], F32, sc) for q in range(2)]; anb = [kb.sb(f'anb{q}', [128, 4, 128], BF16, sc) for q in range(2)]
        sq = [kb.sb(f'sq{q}', [128, 2, 128], BF16, sc) for q in range(2)]
        rz = [kb.sb(f'rz{q}', [128, 8], F32, sc) for q in range(2)]; ss = [kb.sb(f'ss{q}', [128, 8], F32, sc) for q in range(2)]

        def unit_attg(gi, QT, q0, KT, k0, nkt, Vg, vt0, mix, tok0, Eset, eid, maps=tuple(range(8)), final=True):
            def sA():
                for m in maps:
                    h = m // 2
                    pr = slice((m % 2) * 64, (m % 2) * 64 + 64)
                    for kp in range(0, nkt, 2):
                        i, k = bank()
                        for kt in (kp, kp + 1):
                            T(lambda: MM(pb[i][:, (kt - kp) * 256:(kt - kp + 1) * 256], lhsT=KT[pr, h, k0 + kt * 128:k0 + (kt + 1) * 128], rhs=QT[pr, h, q0:q0 + 256], start=True, stop=True), w=[k])
                        A(lambda: nc.scalar.activation(Eset[m - maps[0]][:, kp:kp + 2, :], pb[i][:, :].rearrange('p (a b) -> p a b', b=256), AF.Exp, scale=0.125), r=[k], w=[('E', eid, m - maps[0], kp)])

            def sB():
                for qt in range(2):
                    bks = []
                    for grp in [maps[g0:g0 + 3] for g0 in range(0, len(maps), 3)]:
                        i, k = bank()
                        for li, m in enumerate(grp):
                            h = m // 2
                            kb.mmg([(lambda kt=kt: MM(pb[i][:, li * 129:(li + 1) * 129], lhsT=Eset[m - maps[0]][:, kt, qt * 128:(qt + 1) * 128], rhs=Vg[:, vt0 + kt, h, 0:129], start=(kt == 0), stop=(kt == nkt - 1))) for kt in range(nkt)],
                                   r=[('E', eid, m - maps[0], kp) for kp in range(0, nkt, 2)], w=[k])
                        bks.append((i, k, grp))
                    for (i, k, grp) in bks:
                        n_ = len(grp); m0 = grp[0]
                        pv = pb[i][:, 0:n_ * 129].rearrange('p (a b) -> p a b', b=129)
                        V(lambda: nc.vector.reciprocal(rz[qt][:, m0:m0 + n_], pv[:, :, 128]), r=[k], w=[('rz', qt)])
                        V(lambda: nc.vector.tensor_tensor(on[qt][:, m0:m0 + n_, :], pv[:, :, 0:128], rz[qt][:, m0:m0 + n_].unsqueeze(2).to_broadcast([128, n_, 128]), ALU.mult), r=[k, ('rz', qt)], w=[('on', qt)])
                    if not final:
                        continue
                    onv = on[qt][:].rearrange('p (h two) e -> p h two e', two=2)
                    V(lambda: nc.vector.scalar_tensor_tensor(araw[qt][:], onv[:, :, 1, :], nlam[:, 0:1], onv[:, :, 0, :], ALU.mult, ALU.add), r=[('on', qt)], w=[('araw', qt)])
                    V(lambda: nc.vector.memset(ss[qt][:], 0.0), w=[('ss', qt, hh) for hh in range(4)] + [('ssl', qt)])
                if not final:
                    return
                for qt in range(2):
                    for hh in range(4):
                        A(lambda: nc.scalar.activation(sq[qt][:, hh % 2, :], araw[qt][:, hh, :], AF.Square, accum_out=ss[qt][:, hh:hh + 1]), r=[('araw', qt), ('ss', qt, hh)], w=[('ss', qt, hh), ('sq', qt, hh % 2)])
                    A(lambda: nc.scalar.activation(ss[qt][:, 4:8], ss[qt][:, 0:4], AF.Ln, bias=epst[:, 0:1], scale=1.0 / 128.0), r=[('ss', qt, hh) for hh in range(4)], w=[('ssl', qt)])
                    A(lambda: nc.scalar.activation(ss[qt][:, 4:8], ss[qt][:, 4:8], AF.Exp, scale=-0.5), r=[('ssl', qt)], w=[('ssl', qt)])
                for qt in range(2):
                    V(lambda: nc.vector.tensor_tensor(an[qt][:], araw[qt][:], ss[qt][:, 4:8].unsqueeze(2).to_broadcast([128, 4, 128]), ALU.mult), r=[('araw', qt), ('ssl', qt)], w=[('an', qt)])
                    P(lambda: nc.gpsimd.tensor_tensor(anb[qt][:], an[qt][:], sublnbc[:, :].unsqueeze(1).to_broadcast([128, 4, 128]), ALU.mult), r=[('an', qt)], w=[('anb', qt)])
                for qt in range(2):
                    j, kj = tbank()
                    kb.mmg([(lambda hh=hh: nc.tensor.transpose(pbb[j][:, hh * 128:(hh + 1) * 128], anb[qt][:, hh, :], identb[:, :])) for hh in range(4)], r=[('anb', qt)], w=[kj])
                    V(lambda: nc.vector.tensor_copy(mix[:, 0:4, tok0 + qt * 128:tok0 + (qt + 1) * 128], pbb[j][:, 0:512].rearrange('p (a b) -> p a b', b=128)), r=[kj], w=[('mixatt', tok0, qt)])
            return [sA, sB]

        pipeline([unit_attg(4, QTO, 0, KTS, 0, 10, VS, 0, mixO, 0, EO, 2, maps=(0, 1, 2, 3), final=False)], 1)
        unitsA = [unit_attg(5, QTO, 0, KTS, 0, 10, VS, 0, mixO, 0, EO, 2, maps=(4, 5, 6, 7), final=True)]
        unitsA += [unit_attg(b, QTP, b * 256, KTP, b * 256, 2, VP, b * 2, mixP, b * 256, EP[b % 2], b % 2) for b in range(4)]
        pipeline(unitsA, 1, oldest_first=True)
        dump('attP', mixP[:, 0:4, 0:256], [128, 4, 256], 'x')
        dump('attO', mixO[:, 0:4, 0:256], [128, 4, 256], 'x')
        kb.barrier()
    S_at.close()
    if stop_after <= 4:
        kb.finish()
        return kb
    def hyena_conv(NT, cf, bfm, bft, cfi, bfti, u1_tile, x1_tile, x2v, mixv, PQ, sc, tagp, stages=False, dk=None):
        dk = dk or {}
        kcf = dk.get('cf', []); kbf = dk.get('bf', []); kbft = dk.get('bft', []); kcfi = dk.get('cfi', []); kbfti = dk.get('bfti', [])
        Rb = kb.sb(tagp + 'Rb', [128, NT, 512], BF16, sc); Sb = kb.sb(tagp + 'Sb', [128, NT, 512], BF16, sc)
        zt_ = kb.sb(tagp + 'z', [128, NT, 512], BF16, sc)
        t1 = kb.sb(tagp + tagp + 't1', [128, 512], F32, sc); t2 = kb.sb(tagp + tagp + 't2', [128, 512], F32, sc)
        t3 = kb.sb(tagp + tagp + 't3', [128, 512], F32, sc); t4 = kb.sb(tagp + tagp + 't4', [128, 512], F32, sc)

        def fwd_pw(u_tile, ukeys, o):
            Pq, Qq, pn = PQ[o]
            kpq = dk.get(('PQ', o), [])
            for kt in range(NT):
                ia, ka = bank(); ib, kbk = bank()
                kb.mmg([(lambda st=st: MM(pb[ia][:, :], lhsT=cf[:, st, kt * 128:(kt + 1) * 128], rhs=u_tile(st), start=(st == 0), stop=(st == NT - 1))) for st in range(NT)], r=ukeys + kcf, w=[ka])
                kb.mmg([(lambda st=st: MM(pb[ib][:, :], lhsT=bfm[:, st, kt * 128:(kt + 1) * 128], rhs=u_tile(st), start=(st == 0), stop=(st == NT - 1))) for st in range(NT)], r=ukeys + kbf, w=[kbk])
                V(lambda: nc.vector.tensor_tensor(t1[:], pb[ia][:, :], Pq[:, kt, :], ALU.mult), r=[ka] + kpq, w=[tagp + 't1'])
                V(lambda: nc.vector.tensor_tensor(t2[:], pb[ib][:, :], Qq[:, kt, :], ALU.mult), r=[kbk] + kpq, w=[tagp + 't2'])
                P(lambda: nc.gpsimd.tensor_tensor(Rb[:, kt, :], t1[:], t2[:], ALU.subtract), r=[tagp + 't1', tagp + 't2'], w=[(tagp, 'Rb', kt)])
                V(lambda: nc.vector.tensor_tensor(t3[:], pb[ia][:, :], Qq[:, kt, :], ALU.mult), r=[ka], w=[tagp + 't3'])
                V(lambda: nc.vector.tensor_tensor(t4[:], pb[ib][:, :], Pq[:, kt, :], ALU.mult), r=[kbk], w=[tagp + 't4'])
                if kt % 2:
                    V(lambda: nc.vector.tensor_tensor(Sb[:, kt, :], t3[:], t4[:], ALU.add), r=[tagp + 't3', tagp + 't4'], w=[(tagp, 'Sb', kt)])
                else:
                    P(lambda: nc.gpsimd.tensor_tensor(Sb[:, kt, :], t3[:], t4[:], ALU.add), r=[tagp + 't3', tagp + 't4'], w=[(tagp, 'Sb', kt)])
                if kt == 0:
                    V(lambda: nc.vector.tensor_tensor(Sb[0:1, 0, :], pb[ib][0:1, :], pn[0:1, :], ALU.mult), r=[kbk, (tagp, 'Sb', 0)] + kpq, w=[(tagp, 'Sb', 0)])
        rs_keys = [(tagp, 'Rb', kt) for kt in range(NT)] + [(tagp, 'Sb', kt) for kt in range(NT)]
        def stA():
            fwd_pw(u1_tile, [], 0)

        def stB():
            inv1()

        def stC():
            fwd_pw(lambda st: zt_[:, st, :], [(tagp, 'z', tt) for tt in range(NT)], 1)

        def stD():
            inv2()

        def inv1():
          for tt in range(NT):
            i, k = bank()
            kb.mmg([(lambda kt=kt: MM(pb[i][:, :], lhsT=cf[:, kt, tt * 128:(tt + 1) * 128], rhs=Rb[:, kt, :], start=(kt == 0), stop=False)) for kt in range(NT)]
                   + [(lambda kt=kt: MM(pb[i][:, :], lhsT=bft[:, kt, tt * 128:(tt + 1) * 128], rhs=Sb[:, kt, :], start=False, stop=(kt == NT - 1))) for kt in range(NT)], r=rs_keys + kcf + kbft, w=[k])
            V(lambda: nc.vector.tensor_tensor(zt_[:, tt, :], pb[i][:, :], x1_tile(tt), ALU.mult), r=[k], w=[(tagp, 'z', tt)])

        def inv2():
          for pair in range(2):
            i, k = bank()
            for c2 in range(2):
                ct = pair * 2 + c2
                kb.mmg([(lambda kt=kt: MM(pb[i][:, c2 * 256:(c2 + 1) * 256], lhsT=Rb[:, kt, ct * 128:(ct + 1) * 128], rhs=cfi[:, kt, 0:256], start=(kt == 0), stop=False)) for kt in range(NT)]
                       + [(lambda kt=kt: MM(pb[i][:, c2 * 256:(c2 + 1) * 256], lhsT=Sb[:, kt, ct * 128:(ct + 1) * 128], rhs=bfti[:, kt, 0:256], start=False, stop=(kt == NT - 1))) for kt in range(NT)], r=rs_keys + kcfi + kbfti, w=[k])
            V(lambda: nc.vector.tensor_tensor(mixv[:, pair * 2:pair * 2 + 2, :], pb[i][:, :].rearrange('p (a b) -> p a b', b=256), x2v[:, pair * 2:pair * 2 + 2, :], ALU.mult), r=[k], w=[('hyout', tagp, pair)])

        if stages:
            return [stA, stB, stC, stD]
        stA(); stB(); stC(); stD()

    def load_spectra(n, sc, tagp):
        NT = n // 128
        PQ = []
        for o in (0, 1):
            Pq = kb.sb(f'{tagp}Pq{o}', [128, NT, 512], BF16, sc); Qq = kb.sb(f'{tagp}Qq{o}', [128, NT, 512], BF16, sc)
            pn = kb.sb(f'{tagp}pn{o}', [1, 512], F32, sc)
            kb.dma('sp', Pq[:], spP[(n, o)], w=[(tagp, 'Pq', o)]); kb.dma('sp', Qq[:], spQ[(n, o)], w=[(tagp, 'Qq', o)])
            kb.dma('sp', pn[:], spN[(n, o)], w=[(tagp, 'pn', o)])
            PQ.append((Pq, Qq, pn))
        return PQ

    def load_dft(n, sc, tagp, srcs):
        NT = n // 128
        out = []
        for nm, srcD in srcs:
            t_ = kb.sb(f'{tagp}{nm}', [128, NT, srcD.shape[1]], BF16, sc)
            kb.dma('sp', t_[:], srcD.rearrange('(st p) k -> p st k', p=128), w=[(tagp, nm)])
            out.append(t_)
        return out

    with ExitStack() as sc:
        cf, bfm, bft = load_dft(256, sc, 'd256', (('cf', cfD[256]), ('bf', bfD[256]), ('bft', bftD[256])))
        PQ = load_spectra(256, sc, 'p')
        dkP = {'cf': [('d256', 'cf')], 'bf': [('d256', 'bf')], 'bft': [('d256', 'bft')], 'cfi': [('d256', 'cf')], 'bfti': [('d256', 'bft')],
               ('PQ', 0): [('p', 'Pq', 0), ('p', 'Qq', 0), ('p', 'pn', 0)], ('PQ', 1): [('p', 'Pq', 1), ('p', 'Qq', 1), ('p', 'pn', 1)]}
        pipeline([hyena_conv(2, cf, bfm, bft, cf, bft,
                             lambda st, b=b: vP[:, b * 2 + st, :], lambda tt, b=b: x1P[:, b * 2 + tt, :],
                             x2P[:, :, b * 256:(b + 1) * 256], mixP[:, 4:8, b * 256:(b + 1) * 256], PQ, sc, f'hp{b}', stages=True, dk=dkP) for b in range(4)], 1)
        kb.barrier()
    S_hyP.close()
    with ExitStack() as sc:
        def ld(nm, srcD):
            t_ = kb.sb('d1k' + nm, [128, 8, srcD.shape[1]], BF16, sc)
            kb.dma('sp', t_[:], srcD.rearrange('(st p) k -> p st k', p=128), w=[('d1k', nm)])
            return t_

        def ldsp(o):
            Pq = kb.sb(f'sPq{o}', [128, 8, 512], BF16, sc); Qq = kb.sb(f'sQq{o}', [128, 8, 512], BF16, sc)
            pn = kb.sb(f'spn{o}', [1, 512], F32, sc)
            kb.dma('sp', Pq[:], spP[(1024, o)], w=[('s', 'PQ', o)]); kb.dma('sp', Qq[:], spQ[(1024, o)], w=[('s', 'PQ', o)])
            kb.dma('sp', pn[:], spN[(1024, o)], w=[('s', 'PQ', o)])
            return (Pq, Qq, pn)

        cf = ld('cf', cfD[1024]); bfm = ld('bf', bfD[1024]); PQ0 = ldsp(0)
        bft = ld('bft', bftD[1024]); PQ1 = ldsp(1); cfo = ld('cfo', cfoD); bfto = ld('bfto', bftoD)
        dk = {'cf': [('d1k', 'cf')], 'bf': [('d1k', 'bf')], 'bft': [('d1k', 'bft')], 'cfi': [('d1k', 'cfo')], 'bfti': [('d1k', 'bfto')],
              ('PQ', 0): [('s', 'PQ', 0)], ('PQ', 1): [('s', 'PQ', 1)]}
        hyena_conv(8, cf, bfm, bft, cfo, bfto, lambda st: vS[:, st, :], lambda tt: x1S[:, tt, :], x2O[:, :, :], mixO[:, 4:8, 0:256], [PQ0, PQ1], sc, 'hs', dk=dk)
        kb.barrier()
    S_hy.close()
    dump('hyP', mixP[:, 4:8, 0:256], [128, 4, 256], 'x')
    dump('hyO', mixO[:, 4:8, 0:256], [128, 4, 256], 'x')
    if stop_after <= 5:
        kb.finish()
        return kb

    xmidD = nc.dram_tensor('xmid_scratch', [1280, 1024], F32).ap()
    S7a = ExitStack()
    actT0 = kb.sb('actT0', [128, 22, 512], BF16, S7a)
    ringB = [kb.sb(f'ringB{i}', [128, 8, 512], BF16, S7a) for i in range(3)]
    sg = [kb.sb(f'sg{i}', [128, 512], F32, S7a) for i in range(2)]
    S6 = ExitStack()
    lnbc = {nm: kb.sb('bc_' + nm, [128, 1024], F32, S6) for nm in ('ln1g', 'ln1b')}
    for nm, srcD in (('ln1g', ln1gD), ('ln1b', ln1bD)):
        kb.dma('sp', lnbc[nm][:], srcD.partition_broadcast(128), w=[nm])
    NB6 = 6
    xt6 = [kb.sb(f'xt6_{i}', [128, 1024], F32, S6) for i in range(NB6)]
    y6 = [kb.sb(f'y6_{i}', [128, 1024], F32, S6) for i in range(NB6)]
    xn6 = [kb.sb(f'xn6_{i}', [128, 1024], BF16, S6) for i in range(NB6)]

    def ln_stats(src, srck, S_, M, col0, tag):
        V(lambda: nc.vector.bn_stats(S_[:, 0:6], src[:, 0:512]), r=[srck], w=[tag + 'st0'])
        V(lambda: nc.vector.bn_stats(S_[:, 6:12], src[:, 512:1024]), r=[srck], w=[tag + 'st1'])
        V(lambda: nc.vector.bn_aggr(M[:, col0:col0 + 2], S_[:, :]), r=[tag + 'st0', tag + 'st1'], w=[tag + 'mv'])
        ln_rstd(M[:, col0 + 1:col0 + 2], M[:, col0 + 3:col0 + 4], M[:, col0 + 2:col0 + 3], 128, tag + 'mv', tag + 'rs')
        V(lambda: nc.vector.scalar_tensor_tensor(M[:, col0 + 2:col0 + 3], M[:, col0:col0 + 1], -1.0, M[:, col0 + 3:col0 + 4], ALU.mult, ALU.mult), r=[tag + 'mv', tag + 'rs'], w=[tag + 'nmr'])
        return M[:, col0 + 3:col0 + 4], M[:, col0 + 2:col0 + 3], [tag + 'rs', tag + 'nmr']

    tiles = [(tt, 0, xpD[tt * 128:(tt + 1) * 128, :], mixP, tt * 128) for tt in range(8)] + [(8 + tt, 1, xoD[1 + tt * 128:1 + (tt + 1) * 128, :], mixO, tt * 128) for tt in range(2)]
    acquire(18)
    wo = [ring[18 % NR], ring[19 % NR]]; wok = [('ring', 18 % NR), ('ring', 19 % NR)]
    st6 = [kb.sb(f'st6b_{i}', [128, 24], F32, S6) for i in range(NB6)]
    mv6 = [kb.sb(f'mv6b_{i}', [128, 8], F32, S6) for i in range(NB6)]

    def unit_p6(tile, cvi, xsrc, mix, m0):
        b = tile % NB6
        Y = y6[b]
        S_, M = st6[b], mv6[b]
        st_ = {}

        def sL():
            kb.dma('pool', xt6[b][:], xsrc, w=[('xt6', b)])

        def s0():
            for half in range(2):
                i, k = bank()
                kb.mmg([(lambda kc=kc: MM(pb[i][:, :], lhsT=mix[:, kc, m0:m0 + 128], rhs=wo[half][:, kc, :], start=(kc == 0), stop=(kc == 7))) for kc in range(8)], r=[wok[half]], w=[k, ('mixrd', tile)])
                V(lambda: nc.vector.tensor_tensor(Y[:, half * 512:(half + 1) * 512], pb[i][:, :], gbc[(cvi, 0)][:, half * 512:(half + 1) * 512], ALU.mult), r=[k], w=[('y6', b)])
            V(lambda: nc.vector.scalar_tensor_tensor(Y[:], xt6[b][:], ALPHA, Y[:], ALU.mult, ALU.add), r=[('xt6', b), ('y6', b)], w=[('y6', b)])
            st_['a'] = ln_stats(Y, ('y6', b), S_[:, 0:12], M, 0, f'p6a{b}')

        def s1a():
            sc_, bi_, ks = st_['a']
            A(lambda: nc.scalar.activation(Y[:], Y[:], AF.Identity, bias=bi_, scale=sc_), r=[('y6', b)] + ks, w=[('y6', b)])
            P(lambda: nc.gpsimd.tensor_tensor(Y[:], Y[:], lnbc['ln1g'][:], ALU.mult), r=[('y6', b), 'ln1g'], w=[('y6', b)])
            P(lambda: nc.gpsimd.tensor_tensor(Y[:], Y[:], lnbc['ln1b'][:], ALU.add), r=[('y6', b), 'ln1b'], w=[('y6', b)])
            kb.dma('sp', xmidD[tile * 128:(tile + 1) * 128, :], Y[:], r=[('y6', b)], w=[('xmidD', tile)])

        def s1b():
            st_['b'] = ln_stats(Y, ('y6', b), S_[:, 12:24], M, 4, f'p6b{b}')

        def s2a():
            sc_, bi_, ks = st_['b']
            A(lambda: nc.scalar.activation(xn6[b][:], Y[:], AF.Identity, bias=bi_, scale=sc_), r=[('y6', b)] + ks, w=[('xn6', b)])
            st_['ev'] = transpose_mod(xn6[b], ('xn6', b), 128, mix, 'h2T%d' % tile, m0, cvi, 32, 24, defer=True)

        def s2b():
            st_['ev']()
        return [sL, s0, s1a, s1b, s2a, s2b]

    wupv = wupD.rearrange('(kc p) c -> p kc c', p=128)
    plan2 = []
    for g in range(6):
        ncol = 512 if g < 5 else 256
        plan2.append((wupv[:, :, g * 512:g * 512 + ncol], ncol))
        plan2.append((wupv[:, :, DFF + g * 512:DFF + g * 512 + ncol], ncol))

    class WRing:
        def __init__(self, slots, nblocks=None):
            self.slots = slots; self.issued = 0; self.nblocks = len(plan2) if nblocks is None else nblocks

        def issue_to(self, k):
            n_ = len(self.slots)
            while self.issued < min(self.nblocks, k):
                j = self.issued
                src, ncol = plan2[j]
                t_, key = self.slots[j % n_]
                kb.dma('pool', t_[:, :, 0:ncol], src, w=[key])
                self.issued += 1

        def acquire(self, i):
            self.issue_to(i + len(self.slots))

        def get(self, j):
            return self.slots[j % len(self.slots)]

    wslot = {'r0': (ring[0], ('wslot', 'r0')), 'r1': (ring[1], ('wslot', 'r1')), 'r2': (ring[2], ('wslot', 'r2')),
             'b0': (ringB[0], ('wslot', 'b0')), 'b1': (ringB[1], ('wslot', 'b1')), 'b2': (ringB[2], ('wslot', 'b2'))}

    def ffn_unit(wr, g, cti, chunk, ci, dst, rkeys, resident=False):
        def f():
            if cti == 0 and not resident:
                wr.acquire(2 * g)
            wg, wgk = wr.get(2 * g); wu, wuk = wr.get(2 * g + 1)
            hsrc, h0, nt = chunk
            j = g * 4 + cti
            ig, kg = bank(); iu, ku = bank()
            kb.mmg([(lambda kc=kc: MM(pb[ig][:, 0:nt], lhsT=wg[:, kc, cti * 128:(cti + 1) * 128], rhs=hsrc[:, kc, h0:h0 + nt], start=(kc == 0), stop=(kc == 7))) for kc in range(8)], r=[wgk] + rkeys, w=[kg])
            kb.mmg([(lambda kc=kc: MM(pb[iu][:, 0:nt], lhsT=wu[:, kc, cti * 128:(cti + 1) * 128], rhs=hsrc[:, kc, h0:h0 + nt], start=(kc == 0), stop=(kc == 7))) for kc in range(8)], r=[wuk] + rkeys, w=[ku])
            sb_ = (j * 3 + ci) % 2
            A(lambda: nc.scalar.activation(sg[sb_][:, 0:nt], pb[ig][:, 0:nt], AF.Silu), r=[kg], w=[('sg', sb_)])
            V(lambda: nc.vector.tensor_tensor(dst(j), pb[iu][:, 0:nt], sg[sb_][:, 0:nt], ALU.mult), r=[ku, ('sg', sb_)], w=[('actT', j, ci)])
        return f

    pipeline([unit_p6(*t) for t in tiles[0:4]], 1, order=[0, 5, 4, 2, 3, 1])
    ringX = WRing([wslot[n_] for n_ in ('r2', 'b0', 'b1', 'b2')])
    h2k0 = [('h2T%d' % t_, kc) for t_ in range(4) for kc in range(8)]
    ffn0 = [ffn_unit(ringX, g, cti, (mixP, 0, 512), 0, (lambda j: actT0[:, j, 0:512]), h2k0) for g in range(6) for cti in range(plan2[2 * g][1] // 128)]
    it0 = iter(ffn0)
    for _ in pipeline_gen([unit_p6(*t) for t in tiles[4:]], 1, order=[0, 5, 4, 2, 3, 1]):
        for _q in range(2):
            f_ = next(it0, None)
            if f_ is not None:
                f_()
    for f_ in it0:
        f_()
    kb.barrier()
    S6.close()
    if stop_after <= 6:
        kb.finish()
        return kb

    S7 = ExitStack()
    actT12 = kb.sb('actT12', [128, 22, 768], BF16, S7)
    wdn = kb.sb('wdn', [128, 22, 1024], BF16, S7)
    wdv = wdnD.rearrange('(j p) c -> p j c', p=128)
    st7 = [kb.sb(f'st7_{i}', [128, 12], F32, S7) for i in range(4)]
    mv7 = [kb.sb(f'mv7_{i}', [128, 8], F32, S7) for i in range(4)]
    ringY = WRing([wslot[n_] for n_ in ('r0', 'r1', 'r2', 'b0', 'b1', 'b2')], nblocks=8)
    ringY.issue_to(2)

    def pass2(wr, g, resident):
        for cti in range(plan2[2 * g][1] // 128):
            ffn_unit(wr, g, cti, (mixP, 512, 512), 1, (lambda j: actT12[:, j, 0:512]), [], resident=resident)()
            ffn_unit(wr, g, cti, (mixO, 0, 256), 2, (lambda j: actT12[:, j, 512:768]), [], resident=resident)()

    pass2(ringX, 4, True)
    pass2(ringX, 5, True)
    for g in range(4):
        pass2(ringY, g, False)
        if g == 1:
            for q in range(4):
                j0, j1 = (0, 6, 12, 18)[q], (6, 12, 18, 22)[q]
                kb.dma('pool', wdn[:, j0:j1, :], wdv[:, j0:j1, :], w=[('wdn', q)])
    kb.barrier()
    def f32v(t_, idx):
        return t_[:].rearrange('p a b -> p (a b)').bitcast(F32)[:, idx * 1024:(idx + 1) * 1024]
    lnbc = {'ln2g': f32v(ringB[0], 0), 'ln2b': f32v(ringB[0], 1)}
    for nm, srcD in (('ln2g', ln2gD), ('ln2b', ln2bD)):
        kb.dma('sp', lnbc[nm], srcD.partition_broadcast(128), w=[nm])
    xt7 = [f32v(ringB[1], 0), f32v(ringB[1], 1)]
    y7 = [gbc[(0, 0)][:], gbc[(1, 0)][:], f32v(ringB[2], 0)]

    def act_tile(j, tile):
        return actT0[:, j, tile * 128:(tile + 1) * 128] if tile < 4 else actT12[:, j, (tile - 4) * 128:(tile - 3) * 128]

    def unit_p7(tile, cvi, xsrc, mix, m0):
        b = tile % 3
        bx = tile % 2
        Y = y7[b]
        st_ = {}

        def sL():
            kb.dma('pool', xt7[bx], xmidD[tile * 128:(tile + 1) * 128, :], w=[('xt7', bx)])

        def s0():
            for half in range(2):
                i, k = bank()
                kb.mmg([(lambda j=j: MM(pb[i][:, :], lhsT=act_tile(j, tile), rhs=wdn[:, j, half * 512:(half + 1) * 512], start=(j == 0), stop=(j == 21))) for j in range(22)], r=[('wdn', q) for q in range(4)], w=[k])
                V(lambda: nc.vector.tensor_tensor(Y[:, half * 512:(half + 1) * 512], pb[i][:, :], gbc[(cvi, 1)][:, half * 512:(half + 1) * 512], ALU.mult), r=[k], w=[('y7', b)])
            V(lambda: nc.vector.scalar_tensor_tensor(Y, xt7[bx], ALPHA, Y, ALU.mult, ALU.add), r=[('y7', b), ('xt7', bx)], w=[('y7', b)])
            st_['a'] = ln_stats(Y, ('y7', b), st7[b], mv7[b], 0, f'p7{b}')

        def s1a():
            sc_, bi_, ks = st_['a']
            A(lambda: nc.scalar.activation(Y, Y, AF.Identity, bias=bi_, scale=sc_), r=[('y7', b)] + ks, w=[('y7', b)])
            P(lambda: nc.gpsimd.tensor_tensor(Y, Y, lnbc['ln2g'], ALU.mult), r=[('y7', b), 'ln2g'], w=[('y7', b)])

        def s1():
            V(lambda: nc.vector.tensor_tensor(Y, Y, lnbc['ln2b'], ALU.add), r=[('y7', b), 'ln2b'], w=[('y7', b)])
            if tile < 8:
                kb.dma('sp', ypD[tile * 128:(tile + 1) * 128, :], Y, r=[('y7', b)], w=[('yp', tile)])
            else:
                kb.dma('sp', yoD[(tile - 8) * 128:(tile - 7) * 128, :], Y, r=[('y7', b)], w=[('yo', tile)])
        return [sL, s0, s1a, s1]

    pipeline([unit_p7(*t) for t in tiles], 1, order=[0, 3, 2, 1])
    kb.finish()
    return kb


_NC_CACHE = {}


def _in_maps(inp):
    c = _consts()
    f = lambda a: np.ascontiguousarray(np.asarray(a, dtype=np.float32))
    shared = {
        'mod_w': f(inp['mod_w'][0]), 'mod_b': f(inp['mod_b'][0]), 'w_in': f(inp['w_in'][0]),
        'lq1': f(inp['da_lq1'][0]), 'lk1': f(inp['da_lk1'][0]), 'lq2': f(inp['da_lq2'][0]), 'lk2': f(inp['da_lk2'][0]),
        'subln': f(inp['da_subln'][0]), 'conv_w': f(inp['hy_conv_w'][0]), 'conv_b': f(inp['hy_conv_b'][0]),
        'hw1': f(inp['hy_w1'][0]), 'hb1': f(inp['hy_b1'][0]), 'hw2': f(inp['hy_w2'][0]), 'hb2': f(inp['hy_b2'][0]),
        'hfreq': f(inp['hy_freq'][0]), 'hw3': f(inp['hy_w3'][0]), 'hdecay': f(np.asarray(inp['hy_decay'][0]).reshape(-1)),
        'hskip': f(np.asarray(inp['hy_skip'][0]).reshape(-1)), 'w_out': f(inp['w_out'][0]),
        'ln1_g': f(inp['ln1_g'][0]), 'ln1_b': f(inp['ln1_b'][0]), 'w_up': f(inp['w_up'][0]), 'w_down': f(inp['w_down'][0]),
        'ln2_g': f(inp['ln2_g'][0]), 'ln2_b': f(inp['ln2_b'][0]),
        'identb': c['identb'], 'identf': c['identf'], 'sel2': c['sel2'],
    }
    for n in (256, 1024):
        for nm in ('cf', 'bf', 'bft', 'wk', 'zt', 'nt'):
            shared[f'{nm}{n}'] = c[f'{nm}{n}']
    rope = c['rope']
    shared['ropeS'] = np.ascontiguousarray(rope.reshape(8, 128, 2, 64).transpose(1, 0, 2, 3))
    xp = f(inp['x_prompt']); xs = f(inp['x_sample']); ck = f(inp['cache_k']); cv = f(inp['cache_v'])
    cc = f(inp['c']); cctx = f(inp['c_ctx'])
    maps = []
    for core in range(NCORE):
        b, j = core // 4, core % 4
        m = dict(shared)
        m['xp'] = np.ascontiguousarray(xp[4 * core:4 * core + 4].reshape(1024, 1024))
        m['xs'] = np.ascontiguousarray(xs[b])
        xo = np.zeros((258, 1024), np.float32)
        lo, hi = 256 * j - 1, 256 * j + 257
        slo, shi = max(lo, 0), min(hi, 1024)
        xo[slo - lo:shi - lo] = xs[b, slo:shi]
        m['xo'] = xo
        hm = np.ones((128, 2), np.float32)
        if lo < 0:
            hm[:, 0] = 0.0
        if hi > 1024:
            hm[:, 1] = 0.0
        m['hmask'] = hm
        m['ck'] = np.ascontiguousarray(ck[b, 0].reshape(256, 512))
        m['cv'] = np.ascontiguousarray(cv[b, 0].reshape(256, 512))
        m['cvec'] = np.ascontiguousarray(np.stack([cctx, cc[b]], axis=0))
        m['cfo'] = np.ascontiguousarray(c['cf1024'][:, 256 * j:256 * j + 256])
        m['bfto'] = np.ascontiguousarray(c['bft1024'][:, 256 * j:256 * j + 256])
        m['ropeO'] = np.ascontiguousarray(rope[256 * j:256 * j + 256].reshape(2, 128, 2, 64).transpose(1, 0, 2, 3))
        maps.append(m)
    return maps


def kernel(**inp):
    if 'nc' not in _NC_CACHE:
        _NC_CACHE['nc'] = build().nc
    nc = _NC_CACHE['nc']
    maps = _in_maps(inp)
    res = run_bass_kernel_spmd(nc, maps, core_ids=list(range(NCORE)))
    R = res.results
    y_prompt = np.concatenate([R[c]['yp'].reshape(4, 256, 1024) for c in range(NCORE)], axis=0).astype(np.float32)
    y_sample = np.stack([np.concatenate([R[4 * b + j]['yo'] for j in range(4)], axis=0) for b in range(2)], axis=0).astype(np.float32)
    nk = np.concatenate([R[c]['nk'].reshape(4, 1, 256, 8, 64) for c in range(NCORE)], axis=0).astype(np.float32)
    nv = np.concatenate([R[c]['nv'].reshape(4, 1, 256, 4, 128) for c in range(NCORE)], axis=0).astype(np.float32)
    return (y_prompt, y_sample, nk, nv)
```

```python
import math
from contextlib import ExitStack
import numpy as np
import ml_dtypes
import concourse.bass as bass
import concourse.mybir as mybir
from concourse.bass_utils import run_bass_kernel_spmd

F32 = mybir.dt.float32
BF16 = mybir.dt.bfloat16
AF = mybir.ActivationFunctionType
ALU = mybir.AluOpType
AX = mybir.AxisListType

D = 1024
NCORE = 8
NDS = 12
LAM_INIT = 0.8 - 0.6 * math.exp(-0.3 * 0)
ALPHA = 2.0 ** 0.25
EPS = 1e-5
DFF = 2816
TWO_PI = 2.0 * math.pi


class KB:
    def __init__(self):
        self.nc = bass.Bass("TRN2", target_bir_lowering=False)
        nc = self.nc
        self.es = ExitStack()
        self.E = {'pe': nc.tensor, 'act': nc.scalar, 'dve': nc.vector, 'pool': nc.gpsimd, 'sp': nc.sync}
        self.sem = {}
        self.cnt = {}
        for e in self.E:
            nm = 'c_' + e
            self.sem[nm] = self.es.enter_context(nc.semaphore(nm))
            self.cnt[nm] = 0
        self.dq = {'sp': [], 'pool': []}
        for q in self.dq:
            for i in range(NDS):
                nm = f'd_{q}{i}'
                self.sem[nm] = self.es.enter_context(nc.semaphore(nm))
                self.cnt[nm] = 0
                self.dq[q].append(nm)
        self.dqi = {'sp': 0, 'pool': 0}
        self.waited = {e: {} for e in self.E}
        self.lw = {}
        self.rd = {}
        self.nps = 0
        self.dumps = []

    def sb(self, name, shape, dt, scope=None):
        return (scope or self.es).enter_context(self.nc.sbuf_tensor("s_" + name, list(shape), dt))

    def ps(self, name, shape, dt):
        return self.es.enter_context(self.nc.psum_tensor("p_" + name, list(shape), dt))

    def _wait(self, e, evs):
        need = {}
        for ev in evs:
            if ev is None:
                continue
            s, v = ev
            if need.get(s, 0) < v:
                need[s] = v
        for s, v in need.items():
            if self.waited[e].get(s, 0) < v:
                self.E[e].wait_ge(self.sem[s], v)
                self.waited[e][s] = v

    def _deps(self, e, r, w):
        own = 'c_' + e
        evs = []
        for k in r:
            evs.append(self.lw.get(k))
        for k in w:
            for ev in [self.lw.get(k)] + list(self.rd.get(k, {}).items()):
                if ev is not None and not (e == 'pe' and ev[0] == own):
                    evs.append(ev)
        return evs

    def _commit(self, ev, r, w):
        for k in r:
            d = self.rd.setdefault(k, {})
            if d.get(ev[0], 0) < ev[1]:
                d[ev[0]] = ev[1]
        for k in w:
            self.lw[k] = ev
            self.rd[k] = {}

    def op(self, e, fn, r=(), w=()):
        evs = self._deps(e, r, w)
        if e != 'pe':
            claim = [k for k in r if isinstance(k, tuple) and len(k) == 2 and k[0] in ('pb', 'pbT') and k not in w]
            own = 'c_' + e
            for k in claim:
                for ev in [self.lw.get(k)] + list(self.rd.get(k, {}).items()):
                    if ev is not None and ev[0] != own:
                        evs.append(ev)
            w = list(w) + claim
        self._wait(e, evs)
        ins = fn()
        s = 'c_' + e
        self.cnt[s] += 1
        ins.then_inc(self.sem[s], 1)
        self._commit((s, self.cnt[s]), r, w)

    def mmg(self, fns, r=(), w=()):
        self._wait('pe', self._deps('pe', r, w))
        ins = None
        for fn in fns:
            ins = fn()
        s = 'c_pe'
        self.cnt[s] += 1
        ins.then_inc(self.sem[s], 1)
        self._commit((s, self.cnt[s]), r, w)

    def dma(self, q, out, in_, r=(), w=()):
        nm = self.dq[q][self.dqi[q]]
        self.dqi[q] = (self.dqi[q] + 1) % NDS
        evs = self._deps(q, r, w)
        if self.cnt[nm] > 0:
            evs.append((nm, self.cnt[nm]))
        self._wait(q, evs)
        ins = self.E[q].dma_start(out=out, in_=in_)
        self.cnt[nm] += 16
        ins.then_inc(self.sem[nm], 16)
        self._commit((nm, self.cnt[nm]), r, w)

    def barrier(self):
        for e in self.E:
            self._wait(e, [(s, v) for s, v in self.cnt.items() if v > 0])
        self.lw = {}
        self.rd = {}

    def finish(self):
        self._wait('sp', [(s, v) for s, v in self.cnt.items() if v > 0])

    def bank(self):
        i = self.nps % 8
        self.nps += 1
        return i


def _dft_tables(n):
    s = np.arange(n, dtype=np.float64)
    th = np.pi / n
    ang = th * np.outer(s, s)
    cf = np.cos(ang)
    bf = np.sin(ang)
    bf[:, 0] = (-1.0) ** s
    bft = bf.T.copy()
    wk = np.full((n,), 1.0 / n)
    wk[0] = 1.0 / (2 * n)
    return cf, bf, bft, wk


def _z_table(n):
    pos = np.arange(n, dtype=np.float32)
    t = (pos / np.float32(n - 1))[:, None]
    bands = np.linspace(1e-4, 16 - 1, 16, dtype=np.float32)
    ang = (np.float32(2.0 * math.pi / n) * pos[:, None] * bands).astype(np.float32)
    z = np.concatenate([t, np.cos(ang), -np.sin(ang)], axis=-1).astype(np.float32)
    return z, t[:, 0].astype(np.float32)


def _rope_tables_unused(n):
    pos = np.arange(n)
    row = (pos // 64).astype(np.float32)
    col = (pos % 64).astype(np.float32)
    inv = (10000.0 ** (-np.arange(0, 32, 2, dtype=np.float32) / 32)).astype(np.float32)
    p = np.arange(128)
    d = p % 64
    a = d // 32
    r = (d % 32) // 16
    i = d % 16
    posa = np.where(a[:, None] == 0, row[None, :], col[None, :])
    ang = (posa * inv[i][:, None]).astype(np.float32)
    cos = np.cos(ang).astype(np.float32)
    sin = np.sin(ang).astype(np.float32) * np.where(r == 0, -1.0, 1.0)[:, None].astype(np.float32)
    return cos, sin


_CONST_CACHE = {}


def _consts():
    if _CONST_CACHE:
        return _CONST_CACHE
    bf = ml_dtypes.bfloat16
    c = {}
    for n in (256, 1024):
        cf, bfm, bft, wk = _dft_tables(n)
        c[f'cf{n}'] = cf.astype(np.float32).astype(bf)
        c[f'bf{n}'] = bfm.astype(np.float32).astype(bf)
        c[f'bft{n}'] = bft.astype(np.float32).astype(bf)
        nt = n // 128
        c[f'wk{n}'] = np.ascontiguousarray(wk.reshape(nt, 128).T).astype(np.float32)
        z, t = _z_table(n)
        c[f'zt{n}'] = np.ascontiguousarray(z.T).astype(np.float32)
        c[f'nt{n}'] = np.ascontiguousarray((-t).reshape(nt, 128).T).astype(np.float32)
    pos = np.arange(1024)
    row = (pos // 64).astype(np.float32); col = (pos % 64).astype(np.float32)
    inv = (10000.0 ** (-np.arange(0, 32, 2, dtype=np.float32) / 32)).astype(np.float32)
    d = np.arange(64); a = d // 32; r = (d % 32) // 16; ii = d % 16
    posa = np.where(a[None, :] == 0, row[:, None], col[:, None]).astype(np.float32)
    ang = (posa * inv[ii][None, :]).astype(np.float32)
    tab = np.stack([np.cos(ang), np.sin(ang) * np.where(r == 0, -1.0, 1.0)[None, :]], axis=1).astype(np.float32)
    c['rope'] = tab
    c['identb'] = np.eye(128, dtype=np.float32).astype(bf)
    c['identf'] = np.eye(128, dtype=np.float32)
    sel = np.zeros((2, 2, 128), np.float32)
    sel[0, 0, :] = 1.0
    sel[1, 1, :] = 1.0
    c['sel2'] = sel
    _CONST_CACHE.update(c)
    return c


def build(stop_after=99, debug=()):
    kb = KB()
    nc = kb.nc
    G = kb.es

    def din(name, shape, dt=F32):
        return nc.dram_tensor(name, list(shape), dt, kind="ExternalInput").ap()

    def dout(name, shape):
        return nc.dram_tensor(name, list(shape), F32, kind="ExternalOutput").ap()

    xpD = din('xp', [1024, 1024]); xsD = din('xs', [1024, 1024]); xoD = din('xo', [258, 1024])
    ckD = din('ck', [256, 512]); cvD = din('cv', [256, 512]); cvecD = din('cvec', [2, 1024])
    modwD = din('mod_w', [1024, 6144]); modbD = din('mod_b', [6144]); winD = din('w_in', [1024, 3072])
    lqD = [din(n, [64]) for n in ('lq1', 'lk1', 'lq2', 'lk2')]
    sublnD = din('subln', [128])
    cwD = din('conv_w', [3, 1536]); cbD = din('conv_b', [1536])
    hw1D = din('hw1', [33, 64]); hb1D = din('hb1', [64]); hw2D = din('hw2', [64, 64]); hb2D = din('hb2', [64])
    hfD = din('hfreq', [64]); hw3D = din('hw3', [64, 2048]); hdecD = din('hdecay', [2048]); hskD = din('hskip', [1024])
    woutD = din('w_out', [1024, 1024]); ln1gD = din('ln1_g', [1024]); ln1bD = din('ln1_b', [1024])
    wupD = din('w_up', [1024, 5632]); wdnD = din('w_down', [2816, 1024]); ln2gD = din('ln2_g', [1024]); ln2bD = din('ln2_b', [1024])
    cfD = {n: din(f'cf{n}', [n, n], BF16) for n in (256, 1024)}
    bfD = {n: din(f'bf{n}', [n, n], BF16) for n in (256, 1024)}
    bftD = {n: din(f'bft{n}', [n, n], BF16) for n in (256, 1024)}
    wkD = {n: din(f'wk{n}', [128, n // 128]) for n in (256, 1024)}
    ztD = {n: din(f'zt{n}', [33, n]) for n in (256, 1024)}
    ntD = {n: din(f'nt{n}', [128, n // 128]) for n in (256, 1024)}
    cfoD = din('cfo', [1024, 256], BF16); bftoD = din('bfto', [1024, 256], BF16)
    ropeSD = din('ropeS', [128, 8, 2, 64]); ropeOD = din('ropeO', [128, 2, 2, 64])
    hmaskD = din('hmask', [128, 2])
    identbD = din('identb', [128, 128], BF16); identfD = din('identf', [128, 128]); sel2D = din('sel2', [2, 2, 128])
    ypD = dout('yp', [1024, 1024]); yoD = dout('yo', [256, 1024]); nkD = dout('nk', [1024, 512]); nvD = dout('nv', [1024, 512])
    spP = {(n, o): nc.dram_tensor(f'spP{n}_{o}', [128, n // 128, 512], BF16).ap() for n in (256, 1024) for o in (0, 1)}
    spQ = {(n, o): nc.dram_tensor(f'spQ{n}_{o}', [128, n // 128, 512], BF16).ap() for n in (256, 1024) for o in (0, 1)}
    spN = {(n, o): nc.dram_tensor(f'spN{n}_{o}', [1, 512], F32).ap() for n in (256, 1024) for o in (0, 1)}
    dbg = {}

    def dump(name, ap, shape, rkey):
        if name in debug:
            kb.barrier()
            d = nc.dram_tensor('dbg_' + name, list(shape), F32, kind="ExternalOutput").ap()
            if ap.dtype == F32:
                kb.dma('sp', d, ap, r=[rkey])
            else:
                with ExitStack() as ds:
                    tmp = kb.sb('dt_' + name, list(shape), F32, ds)
                    kb.op('dve', lambda: nc.vector.tensor_copy(tmp[:], ap), r=[rkey], w=['dtmp'])
                    kb.dma('sp', d, tmp[:], r=['dtmp'])
                    kb.barrier()

    V = lambda fn, r=(), w=(): kb.op('dve', fn, r, w)
    A = lambda fn, r=(), w=(): kb.op('act', fn, r, w)
    P = lambda fn, r=(), w=(): kb.op('pool', fn, r, w)
    T = lambda fn, r=(), w=(): kb.op('pe', fn, r, w)
    MM = nc.tensor.matmul
    cnt = [0]

    def alt(fa, fv, r, w):
        cnt[0] += 1
        if fa is None:
            V(fv, r, w)
        elif cnt[0] % 2:
            A(fa, r, w)
        else:
            V(fv, r, w)

    def pipeline_gen(units, depth=1, oldest_first=False, order=None):
        n = len(units)
        S = max(len(u) for u in units)
        for step in range(n + (S - 1) * depth):
            for st_ in (order if order is not None else (reversed(range(S)) if oldest_first else range(S))):
                u = step - st_ * depth
                if 0 <= u < n and st_ < len(units[u]):
                    units[u][st_]()
            yield step

    def pipeline(units, depth=1, oldest_first=False, order=None):
        for _ in pipeline_gen(units, depth, oldest_first, order):
            pass

    NPB = 6
    pb = [kb.ps(f'pb{i}', [128, 512], F32) for i in range(NPB)]
    pbb = [kb.ps(f'pbT{i}', [128, 1024], BF16) for i in range(2)]
    nbk = [0, 0]

    def bank():
        i = nbk[0] % NPB
        nbk[0] += 1
        return i, ('pb', i)

    def tbank():
        i = nbk[1] % 2
        nbk[1] += 1
        return i, ('pbT', i)

    identb = kb.sb('identb', [128, 128], BF16); identf = kb.sb('identf', [128, 128], F32)
    sel2 = kb.sb('sel2', [2, 2, 128], F32)
    epst = kb.sb('epst', [128, 1], F32); negpi = kb.sb('negpi', [128, 1], F32)
    modT = kb.sb('modT', [128, 48, 2], F32)
    gbc = {(c, g): kb.sb(f'gbc{c}{g}', [128, 1024], F32) for c in (0, 1) for g in (0, 1)}
    nlam = kb.sb('nlam', [128, 1], F32)
    sublnbc = kb.sb('sublnbc', [128, 128], F32)
    convp = kb.sb('convp', [128, 12, 4], F32)
    NR = 3
    ring = [kb.sb(f'ring{i}', [128, 8, 512], BF16) for i in range(NR)]
    kb.dma('sp', identb[:], identbD, w=['identb'])
    kb.dma('sp', identf[:], identfD, w=['identf'])
    kb.dma('sp', sel2[:], sel2D, w=['sel2'])
    V(lambda: nc.vector.memset(epst[:], EPS), w=['epst'])
    V(lambda: nc.vector.memset(negpi[:], -math.pi), w=['negpi'])

    plan = []
    for b in range(12):
        plan.append(modwD.rearrange('(kc p) c -> p kc c', p=128)[:, :, b * 512:(b + 1) * 512])
    for b in (0, 1, 2, 5, 3, 4):
        plan.append(winD.rearrange('(kc p) c -> p kc c', p=128)[:, :, b * 512:(b + 1) * 512])
    for b in range(2):
        plan.append(woutD.rearrange('(kc p) c -> p kc c', p=128)[:, :, b * 512:(b + 1) * 512])
    issued = [0]

    def acquire(i):
        while issued[0] < min(len(plan), i + NR):
            j = issued[0]
            kb.dma('pool', ring[j % NR][:], plan[j], w=[('ring', j % NR)])
            issued[0] += 1
        return ring[i % NR], ('ring', i % NR)

    with ExitStack() as sc:
        crow = kb.sb('crow', [2, 1024], F32, sc); srow = kb.sb('srow', [2, 1024], F32, sc)
        sT = kb.sb('sT', [128, 16], BF16, sc)
        mbb = [kb.sb(f'mbb{i}', [2, 512], F32, sc) for i in range(2)]; mrow = [kb.sb(f'mrow{i}', [2, 512], F32, sc) for i in range(2)]
        lq = kb.sb('lq', [128, 4, 64], F32, sc); lpr = kb.sb('lpr', [128, 2, 64], F32, sc); ls = kb.sb('ls', [128, 2], F32, sc)
        cwrow = kb.sb('cwrow', [4, 1536], F32, sc)

        def p1_prologue():
            kb.dma('sp', crow[:], cvecD, w=['crow'])
            A(lambda: nc.scalar.activation(srow[:], crow[:], AF.Silu), r=['crow'], w=['srow'])
            i, k = bank()
            for kc in range(8):
                T(lambda: nc.tensor.transpose(pb[i][:, kc * 2:kc * 2 + 2], srow[0:2, kc * 128:(kc + 1) * 128], identf[0:2, 0:2]), r=['srow', 'identf'], w=[k])
            V(lambda: nc.vector.tensor_copy(sT[:], pb[i][:, 0:16]), r=[k], w=['sT'])

        def p1_block(blk):
            def f():
                slot, sk = acquire(blk)
                mb_, mr_ = mbb[blk % 2], mrow[blk % 2]
                i, k = bank()
                kb.mmg([(lambda kc=kc: MM(pb[i][0:2, :], lhsT=sT[:, kc * 2:kc * 2 + 2], rhs=slot[:, kc, :], start=(kc == 0), stop=(kc == 7))) for kc in range(8)], r=['sT', sk], w=[k])
                kb.dma('sp', mb_[:], modbD[blk * 512:(blk + 1) * 512].partition_broadcast(2), w=[('mbb', blk % 2)])
                V(lambda: nc.vector.tensor_tensor(mr_[:], pb[i][0:2, :], mb_[:], ALU.add), r=[k, ('mbb', blk % 2)], w=[('mrow', blk % 2)])
                j, kj = bank()
                for q in range(4):
                    T(lambda: nc.tensor.transpose(pb[j][:, q * 2:q * 2 + 2], mr_[0:2, q * 128:(q + 1) * 128], identf[0:2, 0:2]), r=[('mrow', blk % 2), 'identf'], w=[kj])
                V(lambda: nc.vector.tensor_copy(modT[:, blk * 4:(blk + 1) * 4, :], pb[j][:, 0:8].rearrange('p (a b) -> p a b', b=2)), r=[kj], w=['modT'])
                if blk in (4, 5, 10, 11):
                    g = 0 if blk < 6 else 1
                    half = blk % 2
                    for cvi in (0, 1):
                        i2, k2 = bank()
                        T(lambda: MM(pb[i2][:, :], lhsT=sel2[0:2, cvi, :], rhs=mr_[0:2, :], start=True, stop=True), r=['sel2', ('mrow', blk % 2)], w=[k2])
                        A(lambda: nc.scalar.copy(gbc[(cvi, g)][:, half * 512:(half + 1) * 512], pb[i2][:, :]), r=[k2], w=[('gbc', cvi, g)])
            return f

        def p1_epilogue():
            V(lambda: nc.vector.tensor_scalar(modT[:, 8:16, :], modT[:, 8:16, :], 1.0, None, ALU.add), r=['modT'], w=['modT'])
            V(lambda: nc.vector.tensor_scalar(modT[:, 32:40, :], modT[:, 32:40, :], 1.0, None, ALU.add), r=['modT'], w=['modT'])
            for q in range(4):
                kb.dma('sp', lq[:, q, :], lqD[q].partition_broadcast(128), w=['lq'])
            V(lambda: nc.vector.tensor_tensor(lpr[:, 0, :], lq[:, 0, :], lq[:, 1, :], ALU.mult), r=['lq'], w=['lpr'])
            V(lambda: nc.vector.tensor_tensor(lpr[:, 1, :], lq[:, 2, :], lq[:, 3, :], ALU.mult), r=['lq'], w=['lpr'])
            V(lambda: nc.vector.reduce_sum(ls[:], lpr[:], axis=AX.X), r=['lpr'], w=['ls'])
            A(lambda: nc.scalar.activation(ls[:], ls[:], AF.Exp), r=['ls'], w=['ls'])
            V(lambda: nc.vector.scalar_tensor_tensor(nlam[:], ls[:, 1:2], -LAM_INIT, ls[:, 0:1], ALU.add, ALU.subtract), r=['ls'], w=['nlam'])
            kb.dma('sp', sublnbc[:], sublnD.partition_broadcast(128), w=['sublnbc'])
            V(lambda: nc.vector.tensor_scalar(sublnbc[:], sublnbc[:], 1.0 - LAM_INIT, None, ALU.mult), r=['sublnbc'], w=['sublnbc'])
            kb.dma('sp', cwrow[0:3, :], cwD, w=['cwrow'])
            kb.dma('sp', cwrow[3:4, :], cbD.unsqueeze(0), w=['cwrow'])
            i, k = bank()
            for ct in range(12):
                T(lambda: nc.tensor.transpose(pb[i][:, ct * 4:ct * 4 + 4], cwrow[0:4, ct * 128:(ct + 1) * 128], identf[0:4, 0:4]), r=['cwrow', 'identf'], w=[k])
            V(lambda: nc.vector.tensor_copy(convp[:], pb[i][:, 0:48].rearrange('p (a b) -> p a b', b=4)), r=[k], w=['convp'])

        p1_units = [p1_prologue] + [p1_block(b_) for b_ in range(12)] + [p1_epilogue]
        p1_next = [0]

        def p1_tick(n_=1):
            for _ in range(n_):
                if p1_next[0] < len(p1_units):
                    p1_units[p1_next[0]]()
                    p1_next[0] += 1

        w1s = kb.sb('w1s', [33, 64], F32, sc); w2s = kb.sb('w2s', [64, 64], F32, sc)
        w3f = kb.sb('w3f', [64, 2048], F32, sc); w3b = kb.sb('w3b', [64, 2048], BF16, sc)
        frow = kb.sb('frow', [3, 64], F32, sc); fmv = kb.sb('fmv', [64, 4], F32, sc)
        decbc = kb.sb('decbc', [128, 2048], F32, sc); skrow = kb.sb('skrow', [1, 1024], F32, sc)
        kb.dma('sp', w1s[:], hw1D, w=['w1s']); kb.dma('sp', w2s[:], hw2D, w=['w2s'])
        kb.dma('sp', w3f[:], hw3D, w=['w3f'])
        kb.dma('sp', frow[0:1, :], hfD.unsqueeze(0), w=['frow'])
        kb.dma('sp', frow[1:2, :], hb1D.unsqueeze(0), w=['frow'])
        kb.dma('sp', frow[2:3, :], hb2D.unsqueeze(0), w=['frow'])
        kb.dma('sp', decbc[:], hdecD.partition_broadcast(128), w=['decbc'])
        kb.dma('sp', skrow[:], hskD.unsqueeze(0), w=['skrow'])
        p1_tick(2)
        V(lambda: nc.vector.tensor_copy(w3b[:], w3f[:]), r=['w3f'], w=['w3b'])
        A(lambda: nc.scalar.activation(decbc[:], decbc[:], AF.Abs), r=['decbc'], w=['decbc'])
        i, k = bank()
        T(lambda: nc.tensor.transpose(pb[i][0:64, 0:3], frow[0:3, :], identf[0:3, 0:3]), r=['frow', 'identf'], w=[k])
        V(lambda: nc.vector.tensor_copy(fmv[:, 0:3], pb[i][0:64, 0:3]), r=[k], w=['fmv'])
        V(lambda: nc.vector.tensor_scalar(fmv[:, 1:3], fmv[:, 1:3], fmv[:, 0:1], None, ALU.mult), r=['fmv'], w=['fmv'])
        for n in (1024, 256):
            NT = n // 128
            with ExitStack() as s2:
                cf = kb.sb(f'f_cf{n}', [128, NT, n], BF16, s2); bfm = kb.sb(f'f_bf{n}', [128, NT, n], BF16, s2)
                kb.dma('sp', cf[:], cfD[n].rearrange('(st p) k -> p st k', p=128), w=['cf'])
                kb.dma('sp', bfm[:], bfD[n].rearrange('(st p) k -> p st k', p=128), w=['bf'])
                zt = kb.sb(f'zt{n}', [33, n], F32, s2); ntt = kb.sb(f'ntt{n}', [128, NT], F32, s2)
                wkt = kb.sb(f'wkt{n}', [128, NT], F32, s2)
                kb.dma('sp', zt[:], ztD[n], w=['zt']); kb.dma('sp', ntt[:], ntD[n], w=['ntt'])
                kb.dma('sp', wkt[:], wkD[n], w=['wkt'])
                arg = kb.sb(f'arg{n}', [64, n], F32, s2); h1 = kb.sb(f'h1{n}', [64, n], F32, s2)
                h2b = kb.sb(f'h2b{n}', [64, n], BF16, s2)
                rr = kb.sb(f'rr{n}', [64, 512], F32, s2); rr2 = kb.sb(f'rr2{n}', [64, 512], F32, s2)
                for (wl, wlk, src, srck, dst, dstk, bcol) in ((w1s, 'w1s', zt, 'zt', h1, 'h1', 1), (w2s, 'w2s', h1, 'h1', h2b, 'h2b', 2)):
                    for c0 in range(0, n, 512):
                        cw = min(512, n - c0)
                        i, k = bank()
                        T(lambda: MM(pb[i][0:64, 0:cw], lhsT=wl[:], rhs=src[:, c0:c0 + cw], start=True, stop=True), r=[wlk, srck], w=[k])
                        V(lambda: nc.vector.tensor_scalar(arg[:, c0:c0 + cw], pb[i][0:64, 0:cw], fmv[:, 0:1], fmv[:, bcol:bcol + 1], ALU.mult, ALU.add), r=[k, 'fmv'], w=[('arg', c0)])
                        V(lambda: nc.vector.tensor_scalar(rr[:, 0:cw], arg[:, c0:c0 + cw], math.pi, -TWO_PI, ALU.is_gt, ALU.mult), r=[('arg', c0)], w=['rr'])
                        V(lambda: nc.vector.tensor_scalar(rr2[:, 0:cw], arg[:, c0:c0 + cw], -math.pi, TWO_PI, ALU.is_lt, ALU.mult), r=[('arg', c0)], w=['rr2'])
                        V(lambda: nc.vector.tensor_tensor(rr[:, 0:cw], rr[:, 0:cw], rr2[:, 0:cw], ALU.add), r=['rr', 'rr2'], w=['rr'])
                        V(lambda: nc.vector.tensor_tensor(arg[:, c0:c0 + cw], arg[:, c0:c0 + cw], rr[:, 0:cw], ALU.add), r=[('arg', c0), 'rr'], w=[('arg', c0)])
                        A(lambda: nc.scalar.activation(dst[:, c0:c0 + cw], arg[:, c0:c0 + cw], AF.Sin), r=[('arg', c0)], w=[dstk])
                    p1_tick()
                hsd = kb.sb(f'hsd{n}', [128, NT, 512], BF16, s2); hdd = kb.sb(f'hdd{n}', [128, NT, 512], BF16, s2)
                dEs = [kb.sb(f'dE{n}_{q}', [128, 1024], F32, s2) for q in range(2)]
                f0s = [kb.sb(f'f0{n}_{q}', [128, 512], F32, s2) for q in range(2)]; f1s = [kb.sb(f'f1{n}_{q}', [128, 512], F32, s2) for q in range(2)]
                Pq = kb.sb(f'Pq{n}', [128, NT, 512], BF16, s2); Qq = kb.sb(f'Qq{n}', [128, NT, 512], BF16, s2)
                pnr = kb.sb(f'pnr{n}', [1, 512], F32, s2)
                for o in (0, 1):
                    for st in range(NT):
                        q = st % 2
                        dE, f0, f1 = dEs[q], f0s[q], f1s[q]
                        A(lambda: nc.scalar.activation(dE[:], decbc[:, o * 1024:(o + 1) * 1024], AF.Exp, scale=ntt[:, st:st + 1]), r=['decbc', 'ntt'], w=[('dE', q)])
                        ia, ka = bank(); ib, kbk = bank()
                        T(lambda: MM(pb[ia][:, :], lhsT=h2b[:, st * 128:(st + 1) * 128], rhs=w3b[:, o * 1024:o * 1024 + 512], start=True, stop=True), r=['h2b', 'w3b'], w=[ka])
                        T(lambda: MM(pb[ib][:, :], lhsT=h2b[:, st * 128:(st + 1) * 128], rhs=w3b[:, o * 1024 + 512:o * 1024 + 1024], start=True, stop=True), r=['h2b', 'w3b'], w=[kbk])
                        V(lambda: nc.vector.tensor_tensor(f0[:], pb[ia][:, :], dE[:, 0:512], ALU.mult), r=[ka, ('dE', q)], w=[('f0', q)])
                        V(lambda: nc.vector.tensor_tensor(f1[:], pb[ib][:, :], dE[:, 512:1024], ALU.mult), r=[kbk, ('dE', q)], w=[('f1', q)])
                        if st == 0:
                            V(lambda: nc.vector.memset(f1[0:1, :], 0.0), r=[('f1', q)], w=[('f1', q)])
                            V(lambda: nc.vector.tensor_tensor(f0[0:1, :], f0[0:1, :], skrow[0:1, o * 512:(o + 1) * 512], ALU.add), r=[('f0', q), 'skrow'], w=[('f0', q)])
                        P(lambda: nc.gpsimd.tensor_tensor(hsd[:, st, :], f0[:], f1[:], ALU.add), r=[('f0', q), ('f1', q)], w=[('hsd', st)])
                        V(lambda: nc.vector.tensor_tensor(hdd[:, st, :], f0[:], f1[:], ALU.subtract), r=[('f0', q), ('f1', q)], w=[('hdd', st)])
                        if st % 2 == 1:
                            p1_tick()
                    hs_keys = [('hsd', st) for st in range(NT)]; hd_keys = [('hdd', st) for st in range(NT)]
                    for kt in range(NT):
                        ia, ka = bank(); ib, kbk = bank()
                        kb.mmg([(lambda st=st: MM(pb[ia][:, :], lhsT=cf[:, st, kt * 128:(kt + 1) * 128], rhs=hsd[:, st, :], start=(st == 0), stop=(st == NT - 1))) for st in range(NT)], r=['cf'] + hs_keys, w=[ka])
                        kb.mmg([(lambda st=st: MM(pb[ib][:, :], lhsT=bfm[:, st, kt * 128:(kt + 1) * 128], rhs=hdd[:, st, :], start=(st == 0), stop=(st == NT - 1))) for st in range(NT)], r=['bf'] + hd_keys, w=[kbk])
                        A(lambda: nc.scalar.activation(Pq[:, kt, :], pb[ia][:, :], AF.Identity, scale=wkt[:, kt:kt + 1]), r=[ka, 'wkt'], w=[('Pq', kt)])
                        V(lambda: nc.vector.tensor_scalar(Qq[:, kt, :], pb[ib][:, :], wkt[:, kt:kt + 1], None, ALU.mult), r=[kbk, 'wkt'], w=[('Qq', kt)])
                        if kt == 0:
                            V(lambda: nc.vector.memset(Qq[0:1, 0, :], 0.0), r=[('Qq', 0)], w=[('Qq', 0)])
                        if kt % 2 == 1:
                            p1_tick()
                    i, k = bank()
                    kb.mmg([(lambda st=st: MM(pb[i][0:1, :], lhsT=bfm[:, st, 0:1], rhs=hsd[:, st, :], start=(st == 0), stop=(st == NT - 1))) for st in range(NT)], r=['bf'] + hs_keys, w=[k])
                    V(lambda: nc.vector.tensor_scalar(pnr[:], pb[i][0:1, :], 1.0 / (2 * n), None, ALU.mult), r=[k], w=['pnr'])
                    kb.dma('sp', spP[(n, o)], Pq[:], r=[('Pq', kt) for kt in range(NT)], w=[('spP', n, o)])
                    kb.dma('sp', spQ[(n, o)], Qq[:], r=[('Qq', kt) for kt in range(NT)], w=[('spQ', n, o)])
                    kb.dma('sp', spN[(n, o)], pnr[:], r=['pnr'], w=[('spN', n, o)])
                if n == 256:
                    p1_tick(100)
                kb.barrier()
        dump('modT', modT[:], [128, 48, 2], 'modT')
        kb.barrier()
    if stop_after <= 1:
        kb.finish()
        return kb

    S_mix = ExitStack(); S_hy = ExitStack(); S_hyP = ExitStack(); S_at = ExitStack(); S_h = ExitStack()
    hTP = kb.sb('hTP', [128, 8, 1024], BF16, S_mix); hTO = kb.sb('hTO', [128, 8, 264], BF16, S_mix)
    mixP = hTP; mixO = hTO
    vS = kb.sb('vS', [128, 8, 512], BF16, S_hy); x1S = kb.sb('x1S', [128, 8, 512], BF16, S_hy); x2O = kb.sb('x2O', [128, 4, 256], BF16, S_hy)
    vP = kb.sb('vP', [128, 8, 512], BF16, S_hyP); x1P = kb.sb('x1P', [128, 8, 512], BF16, S_hyP); x2P = kb.sb('x2P', [128, 4, 1024], BF16, S_hyP)
    QTP = kb.sb('QTP', [128, 4, 1024], BF16, S_at); KTP = kb.sb('KTP', [128, 4, 1024], BF16, S_at)
    VP = kb.sb('VP', [128, 8, 4, 130], BF16, S_at)
    QTO = kb.sb('QTO', [128, 4, 256], BF16, S_at); KTS = kb.sb('KTS', [128, 4, 1280], BF16, S_at)
    VS = kb.sb('VS', [128, 10, 4, 130], BF16, S_at)
    hTS = kb.sb('hTS', [128, 8, 1024], BF16, S_h)

    def ln_rstd(mv_ap, rstd_ap, lnv_ap, npart, rk, wk_):
        A(lambda: nc.scalar.activation(lnv_ap, mv_ap, AF.Ln, bias=epst[0:npart, 0:1], scale=1.0), r=[rk, 'epst'], w=[wk_ + 'l'])
        A(lambda: nc.scalar.activation(rstd_ap, lnv_ap, AF.Exp, scale=-0.5), r=[wk_ + 'l'], w=[wk_])

    def transpose_mod(xn, xnk, nt, dst, dstk, t0, cvi, sc_c0, sh_c0, defer=False):
        for hb in range(2):
            kb.mmg([(lambda kc=kc: nc.tensor.transpose(pbb[hb][:, (kc - 4 * hb) * 128:(kc - 4 * hb) * 128 + nt], xn[0:nt, kc * 128:(kc + 1) * 128], identb[0:nt, 0:nt])) for kc in range(4 * hb, 4 * hb + 4)],
                   r=[xnk, 'identb'], w=[('pbT', hb)])

        def evac():
            for kc in range(4):
                A(lambda: nc.scalar.activation(dst[:, kc, t0:t0 + nt], pbb[0][:, kc * 128:kc * 128 + nt], AF.Identity, bias=modT[:, sh_c0 + kc, cvi:cvi + 1], scale=modT[:, sc_c0 + kc, cvi:cvi + 1]), r=[('pbT', 0), 'modT'], w=[(dstk, kc)])
            for kc in range(4, 8):
                V(lambda: nc.vector.tensor_scalar(dst[:, kc, t0:t0 + nt], pbb[1][:, (kc - 4) * 128:(kc - 4) * 128 + nt], modT[:, sc_c0 + kc, cvi:cvi + 1], modT[:, sh_c0 + kc, cvi:cvi + 1], ALU.mult, ALU.add), r=[('pbT', 1), 'modT'], w=[(dstk, kc)])
        if defer:
            return evac
        evac()

    with ExitStack() as sc:
        NB = 4
        xt = [kb.sb(f'xt{i}', [128, 1024], F32, sc) for i in range(NB)]
        xn = [kb.sb(f'xn{i}', [128, 1024], BF16, sc) for i in range(NB)]
        st = [kb.sb(f'st{i}', [128, 12], F32, sc) for i in range(NB)]
        mv = [kb.sb(f'mv{i}', [128, 4], F32, sc) for i in range(NB)]
        units2 = []

        def unit_ln1(xD, t0, nt, cvi, hT, hk, b):
            X, N, S_, M = xt[b], xn[b], st[b], mv[b]

            def sL():
                kb.dma('sp', X[0:nt, :], xD[t0:t0 + nt, :], w=[('xt', b)])

            def s0():
                V(lambda: nc.vector.bn_stats(S_[0:nt, 0:6], X[0:nt, 0:512]), r=[('xt', b)], w=[('st', b, 0)])
                V(lambda: nc.vector.bn_stats(S_[0:nt, 6:12], X[0:nt, 512:1024]), r=[('xt', b)], w=[('st', b, 1)])
                V(lambda: nc.vector.bn_aggr(M[0:nt, 0:2], S_[0:nt, :]), r=[('st', b, 0), ('st', b, 1)], w=[('mv', b)])
                ln_rstd(M[0:nt, 1:2], M[0:nt, 3:4], M[0:nt, 2:3], nt, ('mv', b), f'rs{b}')
                V(lambda: nc.vector.scalar_tensor_tensor(M[0:nt, 2:3], M[0:nt, 0:1], -1.0, M[0:nt, 3:4], ALU.mult, ALU.mult), r=[('mv', b), f'rs{b}'], w=[f'nmr{b}'])

            st_ = {}

            def s1():
                A(lambda: nc.scalar.activation(N[0:nt, :], X[0:nt, :], AF.Identity, bias=M[0:nt, 2:3], scale=M[0:nt, 3:4]), r=[('xt', b), f'rs{b}', f'nmr{b}'], w=[('xn', b)])
                st_['ev'] = transpose_mod(N, ('xn', b), nt, hT, hk, t0, cvi, 8, 0, defer=True)

            def s2():
                st_['ev']()
            return [sL, s0, s1, s2]

        rot = 0
        for (xD, ntok, cvi, hT, hk) in ((xpD, 1024, 0, hTP, 'hTP'), (xsD, 1024, 1, hTS, 'hTS'), (xoD, 258, 1, hTO, 'hTO')):
            for t0 in range(0, ntok, 128):
                units2.append(unit_ln1(xD, t0, min(128, ntok - t0), cvi, hT, hk, rot % NB))
                rot += 1
        pipeline(units2, 1, order=[0, 1, 3, 2])
        dump('hTP', hTP[:, :, 0:32], [128, 8, 32], ('hTP', 7))
        kb.barrier()
    if stop_after <= 2:
        kb.finish()
        return kb


    S_t3 = ExitStack()

    class Rot:
        def __init__(self, name, n, shape, dt, scope):
            self.t = [kb.sb(f'{name}{i}', shape, dt, scope) for i in range(n)]
            self.name = name
            self.i = -1

        def next(self):
            self.i += 1
            j = self.i % len(self.t)
            return self.t[j], (self.name, j)

    kstR = Rot('kst', 2, [128, 512], F32, S_t3); kbfR = Rot('kbf', 2, [128, 512], BF16, S_t3)
    ropeS = kb.sb('ropeS', [128, 8, 2, 64], F32, S_t3); ropeO = kb.sb('ropeO', [128, 2, 2, 64], F32, S_t3)
    hmask = kb.sb('hmask', [128, 2], F32, S_t3)
    usPt = [kb.sb(f'usP{i}', [128, 1032], F32, S_t3) for i in range(2)]; usSt = [kb.sb(f'usS{i}', [128, 1032], F32, S_t3) for i in range(2)]
    usO = kb.sb('usO', [128, 258], F32, S_t3)
    accR = Rot('acc', 2, [128, 1024], F32, S_t3); cvoR = Rot('cvo', 2, [128, 1024], BF16, S_t3)
    kb.dma('sp', ropeS[:], ropeSD, w=['ropeS']); kb.dma('sp', ropeO[:], ropeOD, w=['ropeO']); kb.dma('sp', hmask[:], hmaskD, w=['hmask'])
    V(lambda: nc.vector.memset(VP[:, :, :, 128:130], 1.0), w=['VPones'])
    V(lambda: nc.vector.memset(VS[:, :, :, 128:130], 1.0), w=['VSones'])

    def proj_tm(slot, sk, hT, t0, nt):
        i, k = bank()
        kb.mmg([(lambda kc=kc: MM(pb[i][0:nt, :], lhsT=hT[:, kc, t0:t0 + nt], rhs=slot[:, kc, :], start=(kc == 0), stop=(kc == 7))) for kc in range(8)], r=[sk], w=[k])
        return i, k

    def proj_fm(slot, sk, ct, hT, t0, nt):
        i, k = bank()
        kb.mmg([(lambda kc=kc: MM(pb[i][:, 0:nt], lhsT=slot[:, kc, ct * 128:(ct + 1) * 128], rhs=hT[:, kc, t0:t0 + nt], start=(kc == 0), stop=(kc == 7))) for kc in range(8)], r=[sk], w=[k])
        return i, k

    def to_fm(src_bf, srck, dst, dstk, c0):
        j, kj = tbank()
        kb.mmg([(lambda ct=ct: nc.tensor.transpose(pbb[j][:, ct * 128:(ct + 1) * 128], src_bf[:, ct * 128:(ct + 1) * 128], identb[:, :])) for ct in range(4)], r=[srck, 'identb'], w=[kj])
        V(lambda: nc.vector.tensor_copy(dst[:, 0:4, c0:c0 + 128], pbb[j][:, 0:512].rearrange('p (a b) -> p a b', b=128)), r=[kj], w=[(dstk, c0)])

    def rope_tile(K_, kk, tab, tabk, tt):
        rt, rtk = usPt[1][:, 0:512], ('usP', 1); ru, ruk = usSt[1][:, 0:512], ('usS', 1); B_, bk = kbfR.next()
        x3 = K_[:].rearrange('p (m d) -> p m d', d=64)
        cosb = tab[:, tt, 0, :].unsqueeze(1).to_broadcast([128, 8, 64])
        V(lambda: nc.vector.tensor_tensor(rt.rearrange('p (m d) -> p m d', d=64), x3, cosb, ALU.mult), r=[kk, tabk], w=[rtk])
        x5 = K_[:].rearrange('p (m a r i) -> p m a r i', m=8, a=2, r=2, i=16)
        u5 = ru.rearrange('p (m a r i) -> p m a r i', m=8, a=2, r=2, i=16)
        s5 = tab[:, tt, 1, :].rearrange('p (a r i) -> p a r i', a=2, r=2)
        for r_ in (0, 1):
            P(lambda: nc.gpsimd.tensor_tensor(u5[:, :, :, r_, :], x5[:, :, :, 1 - r_, :], s5[:, :, r_, :].unsqueeze(1).to_broadcast([128, 8, 2, 16]), ALU.mult), r=[kk, tabk], w=[ruk + (r_,)])
        V(lambda: nc.vector.tensor_tensor(B_[:], rt, ru, ALU.add), r=[rtk, ruk + (0,), ruk + (1,)], w=[bk])
        return B_, bk

    def evac_kst(i, k):
        K_, kk = kstR.next()
        A(lambda: nc.scalar.copy(K_[:], pb[i][:, :]), r=[k], w=[kk])
        return K_, kk

    def cast_bf(K_, kk):
        B_, bk = kbfR.next()
        V(lambda: nc.vector.tensor_copy(B_[:], K_[:]), r=[kk], w=[bk])
        return B_, bk

    units3 = []
    cur = {}

    def u_acq(blk):
        def f():
            cur['slot'], cur['sk'] = acquire(blk)
        return f

    def unit_q_p(ct, ch):
        def s0():
            i, k = proj_fm(cur['slot'], cur['sk'], ct, hTP, ch * 512, 512)
            alt(lambda: nc.scalar.copy(QTP[:, ct, ch * 512:(ch + 1) * 512], pb[i][:, :]),
                lambda: nc.vector.tensor_copy(QTP[:, ct, ch * 512:(ch + 1) * 512], pb[i][:, :]), r=[k], w=[('QTP', ct, ch)])
        return [s0]

    def unit_rope(hT, t0, tab, tabk, tt, dst, dstk, c0):
        st_ = {}

        def s0():
            i, k = proj_tm(cur['slot'], cur['sk'], hT, t0, 128)
            K_, kk = evac_kst(i, k)
            st_['b'] = rope_tile(K_, kk, tab, tabk, tt)

        def s1():
            to_fm(st_['b'][0], st_['b'][1], dst, dstk, c0)
        return [s0, s1]

    def unit_k_p(tt):
        st_ = {}

        def s0():
            i, k = proj_tm(cur['slot'], cur['sk'], hTP, tt * 128, 128)
            K_, kk = evac_kst(i, k)
            kb.dma('sp', nkD[tt * 128:(tt + 1) * 128, :], K_[:], r=[kk], w=[('nk', tt)])
            st_['b'] = cast_bf(K_, kk)

        def s1():
            to_fm(st_['b'][0], st_['b'][1], KTP, 'KTP', tt * 128)
        return [s0, s1]

    def unit_ctx_k(kt):
        st_ = {}

        def s0():
            K_, kk = kstR.next()
            kb.dma('sp', K_[:], ckD[kt * 128:(kt + 1) * 128, :], w=[kk])
            st_['b'] = cast_bf(K_, kk)

        def s1():
            to_fm(st_['b'][0], st_['b'][1], KTS, 'KTS', kt * 128)
        return [s0, s1]

    def unit_v_p(tt):
        def s0():
            i, k = proj_tm(cur['slot'], cur['sk'], hTP, tt * 128, 128)
            K_, kk = evac_kst(i, k)
            kb.dma('sp', nvD[tt * 128:(tt + 1) * 128, :], K_[:], r=[kk], w=[('nv', tt)])
            V(lambda: nc.vector.tensor_copy(VP[:, tt, :, 0:128], K_[:].rearrange('p (a b) -> p a b', b=128)), r=[kk], w=[('VP', tt)])
        return [s0]

    def unit_v_s(tt):
        def s0():
            i, k = proj_tm(cur['slot'], cur['sk'], hTS, tt * 128, 128)
            V(lambda: nc.vector.tensor_copy(VS[:, 2 + tt, :, 0:128], pb[i][:, :].rearrange('p (a b) -> p a b', b=128)), r=[k], w=[('VS', 2 + tt)])
        return [s0]

    def unit_ctx_v(kt):
        def s0():
            K_, kk = kstR.next()
            kb.dma('sp', K_[:], cvD[kt * 128:(kt + 1) * 128, :], w=[kk])
            V(lambda: nc.vector.tensor_copy(VS[:, kt, :, 0:128], K_[:].rearrange('p (a b) -> p a b', b=128)), r=[kk], w=[('VS', kt)])
        return [s0]

    units3.append([u_acq(12)])
    for tt in range(2):
        units3.append(unit_rope(hTO, 1 + tt * 128, ropeO, 'ropeO', tt, QTO, 'QTO', tt * 128))
    for kt in range(2):
        units3.append(unit_ctx_k(kt))
    for kt in range(2):
        units3.append(unit_ctx_v(kt))
    for ct in range(4):
        for ch in range(2):
            units3.append(unit_q_p(ct, ch))
    units3.append([u_acq(13)])
    for tt in range(8):
        units3.append(unit_k_p(tt))
        units3.append(unit_rope(hTS, tt * 128, ropeS, 'ropeS', tt, KTS, 'KTS', 256 + tt * 128))
    units3.append([u_acq(14)])
    for tt in range(8):
        units3.append(unit_v_p(tt))
        units3.append(unit_v_s(tt))

    usPv = [t_[:, 0:1032].rearrange('p (s t) -> p s t', t=258) for t_ in usPt]
    rotP = [0]; rotS = [0]

    def conv3(ul, um, ur, usk, accv, acck, outv, ctg, outk):
        A(lambda: nc.scalar.activation(accv, um, AF.Identity, bias=convp[:, ctg, 3:4], scale=convp[:, ctg, 1:2]), r=[usk, 'convp'], w=[acck])
        V(lambda: nc.vector.scalar_tensor_tensor(accv, ul, convp[:, ctg, 0:1], accv, ALU.mult, ALU.add), r=[usk, acck, 'convp'], w=[acck])
        V(lambda: nc.vector.scalar_tensor_tensor(outv, ur, convp[:, ctg, 2:3], accv, ALU.mult, ALU.add), r=[usk, acck, 'convp'], w=[outk])

    def to_tm(C_, ck_, dst, dstk, ct):
        j, kj = tbank()
        kb.mmg([(lambda tt=tt: nc.tensor.transpose(pbb[j][:, tt * 128:(tt + 1) * 128], C_[:, tt * 128:(tt + 1) * 128], identb[:, :])) for tt in range(8)], r=[ck_, 'identb'], w=[kj])
        V(lambda: nc.vector.tensor_copy(dst[:, 0:8, ct * 128:(ct + 1) * 128], pbb[j][:, :].rearrange('p (a b) -> p a b', b=128)), r=[kj], w=[(dstk, ct)])

    def zero_pads():
        for q in range(2):
            V(lambda: nc.vector.memset(usPt[q][:], 0.0), w=[('usP', q)])
            V(lambda: nc.vector.memset(usSt[q][:], 0.0), w=[('usS', q), ('usS', q, 0), ('usS', q, 1)])

    def hy_P(slot, sk, ct, ctg, outv, outk):
        q = rotP[0] % 2; rotP[0] += 1
        usP = usPv[q]
        for ch in range(2):
            i, k = proj_fm(slot, sk, ct, hTP, ch * 512, 512)
            A(lambda: nc.scalar.copy(usP[:, 2 * ch:2 * ch + 2, 1:257], pb[i][:, :].rearrange('p (s t) -> p s t', t=256)), r=[k], w=[('usP', q)])
        ac_, ak = accR.next()
        a3 = ac_[:].rearrange('p (s t) -> p s t', t=256)
        conv3(usP[:, :, 0:256], usP[:, :, 1:257], usP[:, :, 2:258], ('usP', q), a3, ak, outv, ctg, outk)

    def hy_S(slot, sk, ct, ctg, outv, outk):
        q = rotS[0] % 2; rotS[0] += 1
        us = usSt[q]
        for ch in range(2):
            i, k = proj_fm(slot, sk, ct, hTS, ch * 512, 512)
            A(lambda: nc.scalar.copy(us[:, 1 + ch * 512:1 + (ch + 1) * 512], pb[i][:, :]), r=[k], w=[('usS', q)])
        ac_, ak = accR.next()
        conv3(us[:, 0:1024], us[:, 1:1025], us[:, 2:1026], ('usS', q), ac_[:], ak, outv, ctg, outk)

    def unit_hy(which, ct, ctg, dst, dstk):
        st_ = {}

        def s0():
            C_, ck_ = cvoR.next()
            st_['c'] = (C_, ck_)
            if which == 'P':
                hy_P(cur['slot'], cur['sk'], ct, ctg, C_[:].rearrange('p (s t) -> p s t', t=256), ck_)
            else:
                hy_S(cur['slot'], cur['sk'], ct, ctg, C_[:], ck_)

        def s1():
            to_tm(st_['c'][0], st_['c'][1], dst, dstk, ct)
        return [s0, s1]

    def unit_x2(ct):
        ctg = 8 + ct

        def s0():
            hy_P(cur['slot'], cur['sk'], ct, ctg, x2P[:, ct, :].rearrange('p (s t) -> p s t', t=256), ('x2P', ct))

        def s1():
            i, k = proj_fm(cur['slot'], cur['sk'], ct, hTO, 0, 258)
            A(lambda: nc.scalar.copy(usO[:], pb[i][:, 0:258]), r=[k], w=['usO'])
            V(lambda: nc.vector.tensor_scalar(usO[:, 0:1], usO[:, 0:1], hmask[:, 0:1], None, ALU.mult), r=['usO', 'hmask'], w=['usO'])
            V(lambda: nc.vector.tensor_scalar(usO[:, 257:258], usO[:, 257:258], hmask[:, 1:2], None, ALU.mult), r=['usO', 'hmask'], w=['usO'])
            ac_, ak = accR.next()
            conv3(usO[:, 0:256], usO[:, 1:257], usO[:, 2:258], 'usO', ac_[:, 0:256], ak, x2O[:, ct, :], ctg, ('x2O', ct))
        return [lambda: (s0(), s1())]

    units3.append([lambda: (u_acq(15)(), zero_pads())])
    for ct in range(4):
        units3.append(unit_x2(ct))
    for bi, (dP, dPk, dS, dSk) in enumerate(((vP, 'vP', vS, 'vS'), (x1P, 'x1P', x1S, 'x1S'))):
        units3.append([u_acq(16 + bi)])
        for ct in range(4):
            units3.append(unit_hy('P', ct, bi * 4 + ct, dP, dPk))
            units3.append(unit_hy('S', ct, bi * 4 + ct, dS, dSk))
    pipeline(units3, 1)
    kb.barrier()
    S_t3.close(); S_h.close()
    dump('QTO', QTO[:], [128, 4, 256], 'x')
    dump('KTS', KTS[:, :, 0:384], [128, 4, 384], 'x')
    dump('vS', vS[:, 0:2, :], [128, 2, 512], 'x')
    dump('x2O', x2O[:], [128, 4, 256], 'x')
    dump('x1P', x1P[:, 0:2, :], [128, 2, 512], 'x')
    if stop_after <= 3:
        kb.finish()
        return kb

    with ExitStack() as sc:
        EP = [[kb.sb(f'EP{s_}_{m}', [128, 2, 256], BF16, sc) for m in range(8)] for s_ in range(2)]
        EO = [kb.sb(f'EO_{m}', [128, 10, 256], BF16, sc) for m in range(4)]
        on = [kb.sb(f'on{q}', [128, 8, 128], F32, sc) for q in range(2)]
        araw = [kb.sb(f'araw{q}', [128, 4, 128], F32, sc) for q in range(2)]
        an = [kb.sb(f'an{q}', [128, 4, 128], F32, sc) for q in range(2)]; anb = [kb.sb(f'anb{q}', [128, 4, 128], BF16, sc) for q in range(2)]
        sq = [kb.sb(f'sq{q}', [128, 2, 128], BF16, sc) for q in range(2)]
        rz = [kb.sb(f'rz{q}', [128, 8], F32, sc) for q in range(2)]; ss = [kb.sb(f'ss{q}', [128, 8], F32, sc) for q in range(2)]

        def unit_attg(gi, QT, q0, KT, k0, nkt, Vg, vt0, mix, tok0, Eset, eid, maps=tuple(range(8)), final=True):
            def sA():
                for m in maps:
                    h = m // 2
                    pr = slice((m % 2) * 64, (m % 2) * 64 + 64)
                    for kp in range(0, nkt, 2):
                        i, k = bank()
                        for kt in (kp, kp + 1):
                            T(lambda: MM(pb[i][:, (kt - kp) * 256:(kt - kp + 1) * 256], lhsT=KT[pr, h, k0 + kt * 128:k0 + (kt + 1) * 128], rhs=QT[pr, h, q0:q0 + 256], start=True, stop=True), w=[k])
                        A(lambda: nc.scalar.activation(Eset[m - maps[0]][:, kp:kp + 2, :], pb[i][:, :].rearrange('p (a b) -> p a b', b=256), AF.Exp, scale=0.125), r=[k], w=[('E', eid, m - maps[0], kp)])

            def sB():
                for qt in range(2):
                    bks = []
                    for grp in [maps[g0:g0 + 3] for g0 in range(0, len(maps), 3)]:
                        i, k = bank()
                        for li, m in enumerate(grp):
                            h = m // 2
                            kb.mmg([(lambda kt=kt: MM(pb[i][:, li * 129:(li + 1) * 129], lhsT=Eset[m - maps[0]][:, kt, qt * 128:(qt + 1) * 128], rhs=Vg[:, vt0 + kt, h, 0:129], start=(kt == 0), stop=(kt == nkt - 1))) for kt in range(nkt)],
                                   r=[('E', eid, m - maps[0], kp) for kp in range(0, nkt, 2)], w=[k])
                        bks.append((i, k, grp))
                    for (i, k, grp) in bks:
                        n_ = len(grp); m0 = grp[0]
                        pv = pb[i][:, 0:n_ * 129].rearrange('p (a b) -> p a b', b=129)
                        V(lambda: nc.vector.reciprocal(rz[qt][:, m0:m0 + n_], pv[:, :, 128]), r=[k], w=[('rz', qt)])
                        V(lambda: nc.vector.tensor_tensor(on[qt][:, m0:m0 + n_, :], pv[:, :, 0:128], rz[qt][:, m0:m0 + n_].unsqueeze(2).to_broadcast([128, n_, 128]), ALU.mult), r=[k, ('rz', qt)], w=[('on', qt)])
                    if not final:
                        continue
                    onv = on[qt][:].rearrange('p (h two) e -> p h two e', two=2)
                    V(lambda: nc.vector.scalar_tensor_tensor(araw[qt][:], onv[:, :, 1, :], nlam[:, 0:1], onv[:, :, 0, :], ALU.mult, ALU.add), r=[('on', qt)], w=[('araw', qt)])
                    V(lambda: nc.vector.memset(ss[qt][:], 0.0), w=[('ss', qt, hh) for hh in range(4)] + [('ssl', qt)])
                if not final:
                    return
                for qt in range(2):
                    for hh in range(4):
                        A(lambda: nc.scalar.activation(sq[qt][:, hh % 2, :], araw[qt][:, hh, :], AF.Square, accum_out=ss[qt][:, hh:hh + 1]), r=[('araw', qt), ('ss', qt, hh)], w=[('ss', qt, hh), ('sq', qt, hh % 2)])
                    A(lambda: nc.scalar.activation(ss[qt][:, 4:8], ss[qt][:, 0:4], AF.Ln, bias=epst[:, 0:1], scale=1.0 / 128.0), r=[('ss', qt, hh) for hh in range(4)], w=[('ssl', qt)])
                    A(lambda: nc.scalar.activation(ss[qt][:, 4:8], ss[qt][:, 4:8], AF.Exp, scale=-0.5), r=[('ssl', qt)], w=[('ssl', qt)])
                for qt in range(2):
                    V(lambda: nc.vector.tensor_tensor(an[qt][:], araw[qt][:], ss[qt][:, 4:8].unsqueeze(2).to_broadcast([128, 4, 128]), ALU.mult), r=[('araw', qt), ('ssl', qt)], w=[('an', qt)])
                    P(lambda: nc.gpsimd.tensor_tensor(anb[qt][:], an[qt][:], sublnbc[:, :].unsqueeze(1).to_broadcast([128, 4, 128]), ALU.mult), r=[('an', qt)], w=[('anb', qt)])
                for qt in range(2):
                    j, kj = tbank()
                    kb.mmg([(lambda hh=hh: nc.tensor.transpose(pbb[j][:, hh * 128:(hh + 1) * 128], anb[qt][:, hh, :], identb[:, :])) for hh in range(4)], r=[('anb', qt)], w=[kj])
                    V(lambda: nc.vector.tensor_copy(mix[:, 0:4, tok0 + qt * 128:tok0 + (qt + 1) * 128], pbb[j][:, 0:512].rearrange('p (a b) -> p a b', b=128)), r=[kj], w=[('mixatt', tok0, qt)])
            return [sA, sB]

        unitsA = [unit_attg(b, QTP, b * 256, KTP, b * 256, 2, VP, b * 2, mixP, b * 256, EP[b % 2], b % 2) for b in range(4)]
        unitsA.append(unit_attg(4, QTO, 0, KTS, 0, 10, VS, 0, mixO, 0, EO, 2, maps=(0, 1, 2, 3), final=False))
        pipeline(unitsA, 1, oldest_first=True)
        pipeline([unit_attg(5, QTO, 0, KTS, 0, 10, VS, 0, mixO, 0, EO, 2, maps=(4, 5, 6, 7), final=True)], 1)
        dump('attP', mixP[:, 0:4, 0:256], [128, 4, 256], 'x')
        dump('attO', mixO[:, 0:4, 0:256], [128, 4, 256], 'x')
        kb.barrier()
    S_at.close()
    if stop_after <= 4:
        kb.finish()
        return kb
    def hyena_conv(NT, cf, bfm, bft, cfi, bfti, u1_tile, x1_tile, x2v, mixv, PQ, sc, tagp, stages=False, dk=None):
        dk = dk or {}
        kcf = dk.get('cf', []); kbf = dk.get('bf', []); kbft = dk.get('bft', []); kcfi = dk.get('cfi', []); kbfti = dk.get('bfti', [])
        Rb = kb.sb(tagp + 'Rb', [128, NT, 512], BF16, sc); Sb = kb.sb(tagp + 'Sb', [128, NT, 512], BF16, sc)
        zt_ = kb.sb(tagp + 'z', [128, NT, 512], BF16, sc)
        t1 = kb.sb(tagp + tagp + 't1', [128, 512], F32, sc); t2 = kb.sb(tagp + tagp + 't2', [128, 512], F32, sc)
        t3 = kb.sb(tagp + tagp + 't3', [128, 512], F32, sc); t4 = kb.sb(tagp + tagp + 't4', [128, 512], F32, sc)

        def fwd_pw(u_tile, ukeys, o):
            Pq, Qq, pn = PQ[o]
            kpq = dk.get(('PQ', o), [])
            for kt in range(NT):
                ia, ka = bank(); ib, kbk = bank()
                kb.mmg([(lambda st=st: MM(pb[ia][:, :], lhsT=cf[:, st, kt * 128:(kt + 1) * 128], rhs=u_tile(st), start=(st == 0), stop=(st == NT - 1))) for st in range(NT)], r=ukeys + kcf, w=[ka])
                kb.mmg([(lambda st=st: MM(pb[ib][:, :], lhsT=bfm[:, st, kt * 128:(kt + 1) * 128], rhs=u_tile(st), start=(st == 0), stop=(st == NT - 1))) for st in range(NT)], r=ukeys + kbf, w=[kbk])
                V(lambda: nc.vector.tensor_tensor(t1[:], pb[ia][:, :], Pq[:, kt, :], ALU.mult), r=[ka] + kpq, w=[tagp + 't1'])
                V(lambda: nc.vector.tensor_tensor(t2[:], pb[ib][:, :], Qq[:, kt, :], ALU.mult), r=[kbk] + kpq, w=[tagp + 't2'])
                P(lambda: nc.gpsimd.tensor_tensor(Rb[:, kt, :], t1[:], t2[:], ALU.subtract), r=[tagp + 't1', tagp + 't2'], w=[(tagp, 'Rb', kt)])
                V(lambda: nc.vector.tensor_tensor(t3[:], pb[ia][:, :], Qq[:, kt, :], ALU.mult), r=[ka], w=[tagp + 't3'])
                V(lambda: nc.vector.tensor_tensor(t4[:], pb[ib][:, :], Pq[:, kt, :], ALU.mult), r=[kbk], w=[tagp + 't4'])
                P(lambda: nc.gpsimd.tensor_tensor(Sb[:, kt, :], t3[:], t4[:], ALU.add), r=[tagp + 't3', tagp + 't4'], w=[(tagp, 'Sb', kt)])
                if kt == 0:
                    V(lambda: nc.vector.tensor_tensor(Sb[0:1, 0, :], pb[ib][0:1, :], pn[0:1, :], ALU.mult), r=[kbk, (tagp, 'Sb', 0)] + kpq, w=[(tagp, 'Sb', 0)])
        rs_keys = [(tagp, 'Rb', kt) for kt in range(NT)] + [(tagp, 'Sb', kt) for kt in range(NT)]
        def stA():
            fwd_pw(u1_tile, [], 0)

        def stB():
            inv1()

        def stC():
            fwd_pw(lambda st: zt_[:, st, :], [(tagp, 'z', tt) for tt in range(NT)], 1)

        def stD():
            inv2()

        def inv1():
          for tt in range(NT):
            i, k = bank()
            kb.mmg([(lambda kt=kt: MM(pb[i][:, :], lhsT=cf[:, kt, tt * 128:(tt + 1) * 128], rhs=Rb[:, kt, :], start=(kt == 0), stop=False)) for kt in range(NT)]
                   + [(lambda kt=kt: MM(pb[i][:, :], lhsT=bft[:, kt, tt * 128:(tt + 1) * 128], rhs=Sb[:, kt, :], start=False, stop=(kt == NT - 1))) for kt in range(NT)], r=rs_keys + kcf + kbft, w=[k])
            V(lambda: nc.vector.tensor_tensor(zt_[:, tt, :], pb[i][:, :], x1_tile(tt), ALU.mult), r=[k], w=[(tagp, 'z', tt)])

        def inv2():
          for pair in range(2):
            i, k = bank()
            for c2 in range(2):
                ct = pair * 2 + c2
                kb.mmg([(lambda kt=kt: MM(pb[i][:, c2 * 256:(c2 + 1) * 256], lhsT=Rb[:, kt, ct * 128:(ct + 1) * 128], rhs=cfi[:, kt, 0:256], start=(kt == 0), stop=False)) for kt in range(NT)]
                       + [(lambda kt=kt: MM(pb[i][:, c2 * 256:(c2 + 1) * 256], lhsT=Sb[:, kt, ct * 128:(ct + 1) * 128], rhs=bfti[:, kt, 0:256], start=False, stop=(kt == NT - 1))) for kt in range(NT)], r=rs_keys + kcfi + kbfti, w=[k])
            V(lambda: nc.vector.tensor_tensor(mixv[:, pair * 2:pair * 2 + 2, :], pb[i][:, :].rearrange('p (a b) -> p a b', b=256), x2v[:, pair * 2:pair * 2 + 2, :], ALU.mult), r=[k], w=[('hyout', tagp, pair)])

        if stages:
            return [stA, stB, stC, stD]
        stA(); stB(); stC(); stD()

    def load_spectra(n, sc, tagp):
        NT = n // 128
        PQ = []
        for o in (0, 1):
            Pq = kb.sb(f'{tagp}Pq{o}', [128, NT, 512], BF16, sc); Qq = kb.sb(f'{tagp}Qq{o}', [128, NT, 512], BF16, sc)
            pn = kb.sb(f'{tagp}pn{o}', [1, 512], F32, sc)
            kb.dma('sp', Pq[:], spP[(n, o)], w=[(tagp, 'Pq', o)]); kb.dma('sp', Qq[:], spQ[(n, o)], w=[(tagp, 'Qq', o)])
            kb.dma('sp', pn[:], spN[(n, o)], w=[(tagp, 'pn', o)])
            PQ.append((Pq, Qq, pn))
        return PQ

    def load_dft(n, sc, tagp, srcs):
        NT = n // 128
        out = []
        for nm, srcD in srcs:
            t_ = kb.sb(f'{tagp}{nm}', [128, NT, srcD.shape[1]], BF16, sc)
            kb.dma('sp', t_[:], srcD.rearrange('(st p) k -> p st k', p=128), w=[(tagp, nm)])
            out.append(t_)
        return out

    with ExitStack() as sc:
        cf, bfm, bft = load_dft(256, sc, 'd256', (('cf', cfD[256]), ('bf', bfD[256]), ('bft', bftD[256])))
        PQ = load_spectra(256, sc, 'p')
        dkP = {'cf': [('d256', 'cf')], 'bf': [('d256', 'bf')], 'bft': [('d256', 'bft')], 'cfi': [('d256', 'cf')], 'bfti': [('d256', 'bft')],
               ('PQ', 0): [('p', 'Pq', 0), ('p', 'Qq', 0), ('p', 'pn', 0)], ('PQ', 1): [('p', 'Pq', 1), ('p', 'Qq', 1), ('p', 'pn', 1)]}
        pipeline([hyena_conv(2, cf, bfm, bft, cf, bft,
                             lambda st, b=b: vP[:, b * 2 + st, :], lambda tt, b=b: x1P[:, b * 2 + tt, :],
                             x2P[:, :, b * 256:(b + 1) * 256], mixP[:, 4:8, b * 256:(b + 1) * 256], PQ, sc, f'hp{b}', stages=True, dk=dkP) for b in range(4)], 1, oldest_first=True)
        kb.barrier()
    S_hyP.close()
    with ExitStack() as sc:
        def ld(nm, srcD):
            t_ = kb.sb('d1k' + nm, [128, 8, srcD.shape[1]], BF16, sc)
            kb.dma('sp', t_[:], srcD.rearrange('(st p) k -> p st k', p=128), w=[('d1k', nm)])
            return t_

        def ldsp(o):
            Pq = kb.sb(f'sPq{o}', [128, 8, 512], BF16, sc); Qq = kb.sb(f'sQq{o}', [128, 8, 512], BF16, sc)
            pn = kb.sb(f'spn{o}', [1, 512], F32, sc)
            kb.dma('sp', Pq[:], spP[(1024, o)], w=[('s', 'PQ', o)]); kb.dma('sp', Qq[:], spQ[(1024, o)], w=[('s', 'PQ', o)])
            kb.dma('sp', pn[:], spN[(1024, o)], w=[('s', 'PQ', o)])
            return (Pq, Qq, pn)

        cf = ld('cf', cfD[1024]); bfm = ld('bf', bfD[1024]); PQ0 = ldsp(0)
        bft = ld('bft', bftD[1024]); PQ1 = ldsp(1); cfo = ld('cfo', cfoD); bfto = ld('bfto', bftoD)
        dk = {'cf': [('d1k', 'cf')], 'bf': [('d1k', 'bf')], 'bft': [('d1k', 'bft')], 'cfi': [('d1k', 'cfo')], 'bfti': [('d1k', 'bfto')],
              ('PQ', 0): [('s', 'PQ', 0)], ('PQ', 1): [('s', 'PQ', 1)]}
        hyena_conv(8, cf, bfm, bft, cfo, bfto, lambda st: vS[:, st, :], lambda tt: x1S[:, tt, :], x2O[:, :, :], mixO[:, 4:8, 0:256], [PQ0, PQ1], sc, 'hs', dk=dk)
        kb.barrier()
    S_hy.close()
    dump('hyP', mixP[:, 4:8, 0:256], [128, 4, 256], 'x')
    dump('hyO', mixO[:, 4:8, 0:256], [128, 4, 256], 'x')
    if stop_after <= 5:
        kb.finish()
        return kb

    xmidD = nc.dram_tensor('xmid_scratch', [1280, 1024], F32).ap()
    S7a = ExitStack()
    actT0 = kb.sb('actT0', [128, 22, 512], BF16, S7a)
    ringB = [kb.sb(f'ringB{i}', [128, 8, 512], BF16, S7a) for i in range(3)]
    sg = [kb.sb(f'sg{i}', [128, 512], F32, S7a) for i in range(2)]
    S6 = ExitStack()
    lnbc = {nm: kb.sb('bc_' + nm, [128, 1024], F32, S6) for nm in ('ln1g', 'ln1b')}
    for nm, srcD in (('ln1g', ln1gD), ('ln1b', ln1bD)):
        kb.dma('sp', lnbc[nm][:], srcD.partition_broadcast(128), w=[nm])
    NB6 = 6
    xt6 = [kb.sb(f'xt6_{i}', [128, 1024], F32, S6) for i in range(NB6)]
    y6 = [kb.sb(f'y6_{i}', [128, 1024], F32, S6) for i in range(NB6)]
    xn6 = [kb.sb(f'xn6_{i}', [128, 1024], BF16, S6) for i in range(NB6)]

    def ln_stats(src, srck, S_, M, col0, tag):
        V(lambda: nc.vector.bn_stats(S_[:, 0:6], src[:, 0:512]), r=[srck], w=[tag + 'st0'])
        V(lambda: nc.vector.bn_stats(S_[:, 6:12], src[:, 512:1024]), r=[srck], w=[tag + 'st1'])
        V(lambda: nc.vector.bn_aggr(M[:, col0:col0 + 2], S_[:, :]), r=[tag + 'st0', tag + 'st1'], w=[tag + 'mv'])
        ln_rstd(M[:, col0 + 1:col0 + 2], M[:, col0 + 3:col0 + 4], M[:, col0 + 2:col0 + 3], 128, tag + 'mv', tag + 'rs')
        V(lambda: nc.vector.scalar_tensor_tensor(M[:, col0 + 2:col0 + 3], M[:, col0:col0 + 1], -1.0, M[:, col0 + 3:col0 + 4], ALU.mult, ALU.mult), r=[tag + 'mv', tag + 'rs'], w=[tag + 'nmr'])
        return M[:, col0 + 3:col0 + 4], M[:, col0 + 2:col0 + 3], [tag + 'rs', tag + 'nmr']

    tiles = [(tt, 0, xpD[tt * 128:(tt + 1) * 128, :], mixP, tt * 128) for tt in range(8)] + [(8 + tt, 1, xoD[1 + tt * 128:1 + (tt + 1) * 128, :], mixO, tt * 128) for tt in range(2)]
    acquire(18)
    wo = [ring[18 % NR], ring[19 % NR]]; wok = [('ring', 18 % NR), ('ring', 19 % NR)]
    st6 = [kb.sb(f'st6b_{i}', [128, 24], F32, S6) for i in range(NB6)]
    mv6 = [kb.sb(f'mv6b_{i}', [128, 8], F32, S6) for i in range(NB6)]

    def unit_p6(tile, cvi, xsrc, mix, m0):
        b = tile % NB6
        Y = y6[b]
        S_, M = st6[b], mv6[b]
        st_ = {}

        def sL():
            kb.dma('pool', xt6[b][:], xsrc, w=[('xt6', b)])

        def s0():
            for half in range(2):
                i, k = bank()
                kb.mmg([(lambda kc=kc: MM(pb[i][:, :], lhsT=mix[:, kc, m0:m0 + 128], rhs=wo[half][:, kc, :], start=(kc == 0), stop=(kc == 7))) for kc in range(8)], r=[wok[half]], w=[k, ('mixrd', tile)])
                V(lambda: nc.vector.tensor_tensor(Y[:, half * 512:(half + 1) * 512], pb[i][:, :], gbc[(cvi, 0)][:, half * 512:(half + 1) * 512], ALU.mult), r=[k], w=[('y6', b)])
            V(lambda: nc.vector.scalar_tensor_tensor(Y[:], xt6[b][:], ALPHA, Y[:], ALU.mult, ALU.add), r=[('xt6', b), ('y6', b)], w=[('y6', b)])
            st_['a'] = ln_stats(Y, ('y6', b), S_[:, 0:12], M, 0, f'p6a{b}')

        def s1a():
            sc_, bi_, ks = st_['a']
            A(lambda: nc.scalar.activation(Y[:], Y[:], AF.Identity, bias=bi_, scale=sc_), r=[('y6', b)] + ks, w=[('y6', b)])
            P(lambda: nc.gpsimd.tensor_tensor(Y[:], Y[:], lnbc['ln1g'][:], ALU.mult), r=[('y6', b), 'ln1g'], w=[('y6', b)])
            P(lambda: nc.gpsimd.tensor_tensor(Y[:], Y[:], lnbc['ln1b'][:], ALU.add), r=[('y6', b), 'ln1b'], w=[('y6', b)])
            kb.dma('sp', xmidD[tile * 128:(tile + 1) * 128, :], Y[:], r=[('y6', b)], w=[('xmidD', tile)])

        def s1b():
            st_['b'] = ln_stats(Y, ('y6', b), S_[:, 12:24], M, 4, f'p6b{b}')

        def s2a():
            sc_, bi_, ks = st_['b']
            A(lambda: nc.scalar.activation(xn6[b][:], Y[:], AF.Identity, bias=bi_, scale=sc_), r=[('y6', b)] + ks, w=[('xn6', b)])
            st_['ev'] = transpose_mod(xn6[b], ('xn6', b), 128, mix, 'h2T%d' % tile, m0, cvi, 32, 24, defer=True)

        def s2b():
            st_['ev']()
        return [sL, s0, s1a, s1b, s2a, s2b]

    wupv = wupD.rearrange('(kc p) c -> p kc c', p=128)
    plan2 = []
    for g in range(6):
        ncol = 512 if g < 5 else 256
        plan2.append((wupv[:, :, g * 512:g * 512 + ncol], ncol))
        plan2.append((wupv[:, :, DFF + g * 512:DFF + g * 512 + ncol], ncol))

    class WRing:
        def __init__(self, slots, nblocks=None):
            self.slots = slots; self.issued = 0; self.nblocks = len(plan2) if nblocks is None else nblocks

        def issue_to(self, k):
            n_ = len(self.slots)
            while self.issued < min(self.nblocks, k):
                j = self.issued
                src, ncol = plan2[j]
                t_, key = self.slots[j % n_]
                kb.dma('pool', t_[:, :, 0:ncol], src, w=[key])
                self.issued += 1

        def acquire(self, i):
            self.issue_to(i + len(self.slots))

        def get(self, j):
            return self.slots[j % len(self.slots)]

    wslot = {'r0': (ring[0], ('wslot', 'r0')), 'r1': (ring[1], ('wslot', 'r1')), 'r2': (ring[2], ('wslot', 'r2')),
             'b0': (ringB[0], ('wslot', 'b0')), 'b1': (ringB[1], ('wslot', 'b1')), 'b2': (ringB[2], ('wslot', 'b2'))}

    def ffn_unit(wr, g, cti, chunk, ci, dst, rkeys, resident=False):
        def f():
            if cti == 0 and not resident:
                wr.acquire(2 * g)
            wg, wgk = wr.get(2 * g); wu, wuk = wr.get(2 * g + 1)
            hsrc, h0, nt = chunk
            j = g * 4 + cti
            ig, kg = bank(); iu, ku = bank()
            kb.mmg([(lambda kc=kc: MM(pb[ig][:, 0:nt], lhsT=wg[:, kc, cti * 128:(cti + 1) * 128], rhs=hsrc[:, kc, h0:h0 + nt], start=(kc == 0), stop=(kc == 7))) for kc in range(8)], r=[wgk] + rkeys, w=[kg])
            kb.mmg([(lambda kc=kc: MM(pb[iu][:, 0:nt], lhsT=wu[:, kc, cti * 128:(cti + 1) * 128], rhs=hsrc[:, kc, h0:h0 + nt], start=(kc == 0), stop=(kc == 7))) for kc in range(8)], r=[wuk] + rkeys, w=[ku])
            sb_ = (j * 3 + ci) % 2
            A(lambda: nc.scalar.activation(sg[sb_][:, 0:nt], pb[ig][:, 0:nt], AF.Silu), r=[kg], w=[('sg', sb_)])
            V(lambda: nc.vector.tensor_tensor(dst(j), pb[iu][:, 0:nt], sg[sb_][:, 0:nt], ALU.mult), r=[ku, ('sg', sb_)], w=[('actT', j, ci)])
        return f

    pipeline([unit_p6(*t) for t in tiles[0:4]], 1, order=[0, 5, 4, 2, 3, 1])
    ringX = WRing([wslot[n_] for n_ in ('r2', 'b0', 'b1', 'b2')])
    h2k0 = [('h2T%d' % t_, kc) for t_ in range(4) for kc in range(8)]
    ffn0 = [ffn_unit(ringX, g, cti, (mixP, 0, 512), 0, (lambda j: actT0[:, j, 0:512]), h2k0) for g in range(6) for cti in range(plan2[2 * g][1] // 128)]
    it0 = iter(ffn0)
    for _ in pipeline_gen([unit_p6(*t) for t in tiles[4:]], 1, order=[0, 5, 4, 2, 3, 1]):
        for _q in range(2):
            f_ = next(it0, None)
            if f_ is not None:
                f_()
    for f_ in it0:
        f_()
    kb.barrier()
    S6.close()
    if stop_after <= 6:
        kb.finish()
        return kb

    S7 = ExitStack()
    actT12 = kb.sb('actT12', [128, 22, 768], BF16, S7)
    wdn = kb.sb('wdn', [128, 22, 1024], BF16, S7)
    wdv = wdnD.rearrange('(j p) c -> p j c', p=128)
    st7 = [kb.sb(f'st7_{i}', [128, 12], F32, S7) for i in range(4)]
    mv7 = [kb.sb(f'mv7_{i}', [128, 8], F32, S7) for i in range(4)]
    ringY = WRing([wslot[n_] for n_ in ('r0', 'r1', 'r2', 'b0', 'b1', 'b2')], nblocks=8)
    ringY.issue_to(2)

    def pass2(wr, g, resident):
        for cti in range(plan2[2 * g][1] // 128):
            ffn_unit(wr, g, cti, (mixP, 512, 512), 1, (lambda j: actT12[:, j, 0:512]), [], resident=resident)()
            ffn_unit(wr, g, cti, (mixO, 0, 256), 2, (lambda j: actT12[:, j, 512:768]), [], resident=resident)()

    pass2(ringX, 4, True)
    pass2(ringX, 5, True)
    for g in range(4):
        pass2(ringY, g, False)
        if g == 1:
            for q in range(4):
                j0, j1 = (0, 6, 12, 18)[q], (6, 12, 18, 22)[q]
                kb.dma('pool', wdn[:, j0:j1, :], wdv[:, j0:j1, :], w=[('wdn', q)])
    kb.barrier()
    def f32v(t_, idx):
        return t_[:].rearrange('p a b -> p (a b)').bitcast(F32)[:, idx * 1024:(idx + 1) * 1024]
    lnbc = {'ln2g': f32v(ringB[0], 0), 'ln2b': f32v(ringB[0], 1)}
    for nm, srcD in (('ln2g', ln2gD), ('ln2b', ln2bD)):
        kb.dma('sp', lnbc[nm], srcD.partition_broadcast(128), w=[nm])
    xt7 = [f32v(ringB[1], 0), f32v(ringB[1], 1)]
    y7 = [gbc[(0, 0)][:], gbc[(1, 0)][:], f32v(ringB[2], 0)]

    def act_tile(j, tile):
        return actT0[:, j, tile * 128:(tile + 1) * 128] if tile < 4 else actT12[:, j, (tile - 4) * 128:(tile - 3) * 128]

    def unit_p7(tile, cvi, xsrc, mix, m0):
        b = tile % 3
        bx = tile % 2
        Y = y7[b]
        st_ = {}

        def sL():
            kb.dma('pool', xt7[bx], xmidD[tile * 128:(tile + 1) * 128, :], w=[('xt7', bx)])

        def s0():
            for half in range(2):
                i, k = bank()
                kb.mmg([(lambda j=j: MM(pb[i][:, :], lhsT=act_tile(j, tile), rhs=wdn[:, j, half * 512:(half + 1) * 512], start=(j == 0), stop=(j == 21))) for j in range(22)], r=[('wdn', q) for q in range(4)], w=[k])
                V(lambda: nc.vector.tensor_tensor(Y[:, half * 512:(half + 1) * 512], pb[i][:, :], gbc[(cvi, 1)][:, half * 512:(half + 1) * 512], ALU.mult), r=[k], w=[('y7', b)])
            V(lambda: nc.vector.scalar_tensor_tensor(Y, xt7[bx], ALPHA, Y, ALU.mult, ALU.add), r=[('y7', b), ('xt7', bx)], w=[('y7', b)])
            st_['a'] = ln_stats(Y, ('y7', b), st7[b], mv7[b], 0, f'p7{b}')

        def s1a():
            sc_, bi_, ks = st_['a']
            A(lambda: nc.scalar.activation(Y, Y, AF.Identity, bias=bi_, scale=sc_), r=[('y7', b)] + ks, w=[('y7', b)])
            P(lambda: nc.gpsimd.tensor_tensor(Y, Y, lnbc['ln2g'], ALU.mult), r=[('y7', b), 'ln2g'], w=[('y7', b)])

        def s1():
            V(lambda: nc.vector.tensor_tensor(Y, Y, lnbc['ln2b'], ALU.add), r=[('y7', b), 'ln2b'], w=[('y7', b)])
            if tile < 8:
                kb.dma('sp', ypD[tile * 128:(tile + 1) * 128, :], Y, r=[('y7', b)], w=[('yp', tile)])
            else:
                kb.dma('sp', yoD[(tile - 8) * 128:(tile - 7) * 128, :], Y, r=[('y7', b)], w=[('yo', tile)])
        return [sL, s0, s1a, s1]

    pipeline([unit_p7(*t) for t in tiles], 1, order=[0, 3, 2, 1])
    kb.finish()
    return kb


_NC_CACHE = {}


def _in_maps(inp):
    c = _consts()
    f = lambda a: np.ascontiguousarray(np.asarray(a, dtype=np.float32))
    shared = {
        'mod_w': f(inp['mod_w'][0]), 'mod_b': f(inp['mod_b'][0]), 'w_in': f(inp['w_in'][0]),
        'lq1': f(inp['da_lq1'][0]), 'lk1': f(inp['da_lk1'][0]), 'lq2': f(inp['da_lq2'][0]), 'lk2': f(inp['da_lk2'][0]),
        'subln': f(inp['da_subln'][0]), 'conv_w': f(inp['hy_conv_w'][0]), 'conv_b': f(inp['hy_conv_b'][0]),
        'hw1': f(inp['hy_w1'][0]), 'hb1': f(inp['hy_b1'][0]), 'hw2': f(inp['hy_w2'][0]), 'hb2': f(inp['hy_b2'][0]),
        'hfreq': f(inp['hy_freq'][0]), 'hw3': f(inp['hy_w3'][0]), 'hdecay': f(np.asarray(inp['hy_decay'][0]).reshape(-1)),
        'hskip': f(np.asarray(inp['hy_skip'][0]).reshape(-1)), 'w_out': f(inp['w_out'][0]),
        'ln1_g': f(inp['ln1_g'][0]), 'ln1_b': f(inp['ln1_b'][0]), 'w_up': f(inp['w_up'][0]), 'w_down': f(inp['w_down'][0]),
        'ln2_g': f(inp['ln2_g'][0]), 'ln2_b': f(inp['ln2_b'][0]),
        'identb': c['identb'], 'identf': c['identf'], 'sel2': c['sel2'],
    }
    for n in (256, 1024):
        for nm in ('cf', 'bf', 'bft', 'wk', 'zt', 'nt'):
            shared[f'{nm}{n}'] = c[f'{nm}{n}']
    rope = c['rope']
    shared['ropeS'] = np.ascontiguousarray(rope.reshape(8, 128, 2, 64).transpose(1, 0, 2, 3))
    xp = f(inp['x_prompt']); xs = f(inp['x_sample']); ck = f(inp['cache_k']); cv = f(inp['cache_v'])
    cc = f(inp['c']); cctx = f(inp['c_ctx'])
    maps = []
    for core in range(NCORE):
        b, j = core // 4, core % 4
        m = dict(shared)
        m['xp'] = np.ascontiguousarray(xp[4 * core:4 * core + 4].reshape(1024, 1024))
        m['xs'] = np.ascontiguousarray(xs[b])
        xo = np.zeros((258, 1024), np.float32)
        lo, hi = 256 * j - 1, 256 * j + 257
        slo, shi = max(lo, 0), min(hi, 1024)
        xo[slo - lo:shi - lo] = xs[b, slo:shi]
        m['xo'] = xo
        hm = np.ones((128, 2), np.float32)
        if lo < 0:
            hm[:, 0] = 0.0
        if hi > 1024:
            hm[:, 1] = 0.0
        m['hmask'] = hm
        m['ck'] = np.ascontiguousarray(ck[b, 0].reshape(256, 512))
        m['cv'] = np.ascontiguousarray(cv[b, 0].reshape(256, 512))
        m['cvec'] = np.ascontiguousarray(np.stack([cctx, cc[b]], axis=0))
        m['cfo'] = np.ascontiguousarray(c['cf1024'][:, 256 * j:256 * j + 256])
        m['bfto'] = np.ascontiguousarray(c['bft1024'][:, 256 * j:256 * j + 256])
        m['ropeO'] = np.ascontiguousarray(rope[256 * j:256 * j + 256].reshape(2, 128, 2, 64).transpose(1, 0, 2, 3))
        maps.append(m)
    return maps


def kernel(**inp):
    if 'nc' not in _NC_CACHE:
        _NC_CACHE['nc'] = build().nc
    nc = _NC_CACHE['nc']
    maps = _in_maps(inp)
    res = run_bass_kernel_spmd(nc, maps, core_ids=list(range(NCORE)))
    R = res.results
    y_prompt = np.concatenate([R[c]['yp'].reshape(4, 256, 1024) for c in range(NCORE)], axis=0).astype(np.float32)
    y_sample = np.stack([np.concatenate([R[4 * b + j]['yo'] for j in range(4)], axis=0) for b in range(2)], axis=0).astype(np.float32)
    nk = np.concatenate([R[c]['nk'].reshape(4, 1, 256, 8, 64) for c in range(NCORE)], axis=0).astype(np.float32)
    nv = np.concatenate([R[c]['nv'].reshape(4, 1, 256, 4, 128) for c in range(NCORE)], axis=0).astype(np.float32)
    return (y_prompt, y_sample, nk, nv)
```

```python
import math
from contextlib import ExitStack
import numpy as np
import ml_dtypes
import concourse.bass as bass
import concourse.mybir as mybir
from concourse.bass_utils import run_bass_kernel_spmd

F32 = mybir.dt.float32
BF16 = mybir.dt.bfloat16
AF = mybir.ActivationFunctionType
ALU = mybir.AluOpType
AX = mybir.AxisListType

D = 1024
NCORE = 8
NDS = 12
LAM_INIT = 0.8 - 0.6 * math.exp(-0.3 * 0)
ALPHA = 2.0 ** 0.25
EPS = 1e-5
DFF = 2816
TWO_PI = 2.0 * math.pi


class KB:
    def __init__(self):
        self.nc = bass.Bass("TRN2", target_bir_lowering=False)
        nc = self.nc
        self.es = ExitStack()
        self.E = {'pe': nc.tensor, 'act': nc.scalar, 'dve': nc.vector, 'pool': nc.gpsimd, 'sp': nc.sync}
        self.sem = {}
        self.cnt = {}
        for e in self.E:
            nm = 'c_' + e
            self.sem[nm] = self.es.enter_context(nc.semaphore(nm))
            self.cnt[nm] = 0
        self.dq = {'sp': [], 'pool': []}
        for q in self.dq:
            for i in range(NDS):
                nm = f'd_{q}{i}'
                self.sem[nm] = self.es.enter_context(nc.semaphore(nm))
                self.cnt[nm] = 0
                self.dq[q].append(nm)
        self.dqi = {'sp': 0, 'pool': 0}
        self.waited = {e: {} for e in self.E}
        self.lw = {}
        self.rd = {}
        self.nps = 0
        self.dumps = []

    def sb(self, name, shape, dt, scope=None):
        return (scope or self.es).enter_context(self.nc.sbuf_tensor("s_" + name, list(shape), dt))

    def ps(self, name, shape, dt):
        return self.es.enter_context(self.nc.psum_tensor("p_" + name, list(shape), dt))

    def _wait(self, e, evs):
        need = {}
        for ev in evs:
            if ev is None:
                continue
            s, v = ev
            if need.get(s, 0) < v:
                need[s] = v
        for s, v in need.items():
            if self.waited[e].get(s, 0) < v:
                self.E[e].wait_ge(self.sem[s], v)
                self.waited[e][s] = v

    def _deps(self, e, r, w):
        own = 'c_' + e
        evs = []
        for k in r:
            evs.append(self.lw.get(k))
        for k in w:
            for ev in [self.lw.get(k)] + list(self.rd.get(k, {}).items()):
                if ev is not None and not (e == 'pe' and ev[0] == own):
                    evs.append(ev)
        return evs

    def _commit(self, ev, r, w):
        for k in r:
            d = self.rd.setdefault(k, {})
            if d.get(ev[0], 0) < ev[1]:
                d[ev[0]] = ev[1]
        for k in w:
            self.lw[k] = ev
            self.rd[k] = {}

    def op(self, e, fn, r=(), w=()):
        evs = self._deps(e, r, w)
        if e != 'pe':
            claim = [k for k in r if isinstance(k, tuple) and len(k) == 2 and k[0] in ('pb', 'pbT') and k not in w]
            own = 'c_' + e
            for k in claim:
                for ev in [self.lw.get(k)] + list(self.rd.get(k, {}).items()):
                    if ev is not None and ev[0] != own:
                        evs.append(ev)
            w = list(w) + claim
        self._wait(e, evs)
        ins = fn()
        s = 'c_' + e
        self.cnt[s] += 1
        ins.then_inc(self.sem[s], 1)
        self._commit((s, self.cnt[s]), r, w)

    def mmg(self, fns, r=(), w=()):
        self._wait('pe', self._deps('pe', r, w))
        ins = None
        for fn in fns:
            ins = fn()
        s = 'c_pe'
        self.cnt[s] += 1
        ins.then_inc(self.sem[s], 1)
        self._commit((s, self.cnt[s]), r, w)

    def dma(self, q, out, in_, r=(), w=()):
        nm = self.dq[q][self.dqi[q]]
        self.dqi[q] = (self.dqi[q] + 1) % NDS
        evs = self._deps(q, r, w)
        if self.cnt[nm] > 0:
            evs.append((nm, self.cnt[nm]))
        self._wait(q, evs)
        ins = self.E[q].dma_start(out=out, in_=in_)
        self.cnt[nm] += 16
        ins.then_inc(self.sem[nm], 16)
        self._commit((nm, self.cnt[nm]), r, w)

    def barrier(self):
        for e in self.E:
            self._wait(e, [(s, v) for s, v in self.cnt.items() if v > 0])
        self.lw = {}
        self.rd = {}

    def finish(self):
        self._wait('sp', [(s, v) for s, v in self.cnt.items() if v > 0])

    def bank(self):
        i = self.nps % 8
        self.nps += 1
        return i


def _dft_tables(n):
    s = np.arange(n, dtype=np.float64)
    th = np.pi / n
    ang = th * np.outer(s, s)
    cf = np.cos(ang)
    bf = np.sin(ang)
    bf[:, 0] = (-1.0) ** s
    bft = bf.T.copy()
    wk = np.full((n,), 1.0 / n)
    wk[0] = 1.0 / (2 * n)
    return cf, bf, bft, wk


def _z_table(n):
    pos = np.arange(n, dtype=np.float32)
    t = (pos / np.float32(n - 1))[:, None]
    bands = np.linspace(1e-4, 16 - 1, 16, dtype=np.float32)
    ang = (np.float32(2.0 * math.pi / n) * pos[:, None] * bands).astype(np.float32)
    z = np.concatenate([t, np.cos(ang), -np.sin(ang)], axis=-1).astype(np.float32)
    return z, t[:, 0].astype(np.float32)


def _rope_tables_unused(n):
    pos = np.arange(n)
    row = (pos // 64).astype(np.float32)
    col = (pos % 64).astype(np.float32)
    inv = (10000.0 ** (-np.arange(0, 32, 2, dtype=np.float32) / 32)).astype(np.float32)
    p = np.arange(128)
    d = p % 64
    a = d // 32
    r = (d % 32) // 16
    i = d % 16
    posa = np.where(a[:, None] == 0, row[None, :], col[None, :])
    ang = (posa * inv[i][:, None]).astype(np.float32)
    cos = np.cos(ang).astype(np.float32)
    sin = np.sin(ang).astype(np.float32) * np.where(r == 0, -1.0, 1.0)[:, None].astype(np.float32)
    return cos, sin


_CONST_CACHE = {}


def _consts():
    if _CONST_CACHE:
        return _CONST_CACHE
    bf = ml_dtypes.bfloat16
    c = {}
    for n in (256, 1024):
        cf, bfm, bft, wk = _dft_tables(n)
        c[f'cf{n}'] = cf.astype(np.float32).astype(bf)
        c[f'bf{n}'] = bfm.astype(np.float32).astype(bf)
        c[f'bft{n}'] = bft.astype(np.float32).astype(bf)
        nt = n // 128
        c[f'wk{n}'] = np.ascontiguousarray(wk.reshape(nt, 128).T).astype(np.float32)
        z, t = _z_table(n)
        c[f'zt{n}'] = np.ascontiguousarray(z.T).astype(np.float32)
        c[f'nt{n}'] = np.ascontiguousarray((-t).reshape(nt, 128).T).astype(np.float32)
    pos = np.arange(1024)
    row = (pos // 64).astype(np.float32); col = (pos % 64).astype(np.float32)
    inv = (10000.0 ** (-np.arange(0, 32, 2, dtype=np.float32) / 32)).astype(np.float32)
    d = np.arange(64); a = d // 32; r = (d % 32) // 16; ii = d % 16
    posa = np.where(a[None, :] == 0, row[:, None], col[:, None]).astype(np.float32)
    ang = (posa * inv[ii][None, :]).astype(np.float32)
    tab = np.stack([np.cos(ang), np.sin(ang) * np.where(r == 0, -1.0, 1.0)[None, :]], axis=1).astype(np.float32)
    c['rope'] = tab
    c['identb'] = np.eye(128, dtype=np.float32).astype(bf)
    c['identf'] = np.eye(128, dtype=np.float32)
    sel = np.zeros((2, 2, 128), np.float32)
    sel[0, 0, :] = 1.0
    sel[1, 1, :] = 1.0
    c['sel2'] = sel
    _CONST_CACHE.update(c)
    return c


def build(stop_after=99, debug=()):
    kb = KB()
    nc = kb.nc
    G = kb.es

    def din(name, shape, dt=F32):
        return nc.dram_tensor(name, list(shape), dt, kind="ExternalInput").ap()

    def dout(name, shape):
        return nc.dram_tensor(name, list(shape), F32, kind="ExternalOutput").ap()

    xpD = din('xp', [1024, 1024]); xsD = din('xs', [1024, 1024]); xoD = din('xo', [258, 1024])
    ckD = din('ck', [256, 512]); cvD = din('cv', [256, 512]); cvecD = din('cvec', [2, 1024])
    modwD = din('mod_w', [1024, 6144]); modbD = din('mod_b', [6144]); winD = din('w_in', [1024, 3072])
    lqD = [din(n, [64]) for n in ('lq1', 'lk1', 'lq2', 'lk2')]
    sublnD = din('subln', [128])
    cwD = din('conv_w', [3, 1536]); cbD = din('conv_b', [1536])
    hw1D = din('hw1', [33, 64]); hb1D = din('hb1', [64]); hw2D = din('hw2', [64, 64]); hb2D = din('hb2', [64])
    hfD = din('hfreq', [64]); hw3D = din('hw3', [64, 2048]); hdecD = din('hdecay', [2048]); hskD = din('hskip', [1024])
    woutD = din('w_out', [1024, 1024]); ln1gD = din('ln1_g', [1024]); ln1bD = din('ln1_b', [1024])
    wupD = din('w_up', [1024, 5632]); wdnD = din('w_down', [2816, 1024]); ln2gD = din('ln2_g', [1024]); ln2bD = din('ln2_b', [1024])
    cfD = {n: din(f'cf{n}', [n, n], BF16) for n in (256, 1024)}
    bfD = {n: din(f'bf{n}', [n, n], BF16) for n in (256, 1024)}
    bftD = {n: din(f'bft{n}', [n, n], BF16) for n in (256, 1024)}
    wkD = {n: din(f'wk{n}', [128, n // 128]) for n in (256, 1024)}
    ztD = {n: din(f'zt{n}', [33, n]) for n in (256, 1024)}
    ntD = {n: din(f'nt{n}', [128, n // 128]) for n in (256, 1024)}
    cfoD = din('cfo', [1024, 256], BF16); bftoD = din('bfto', [1024, 256], BF16)
    ropeSD = din('ropeS', [128, 8, 2, 64]); ropeOD = din('ropeO', [128, 2, 2, 64])
    hmaskD = din('hmask', [128, 2])
    identbD = din('identb', [128, 128], BF16); identfD = din('identf', [128, 128]); sel2D = din('sel2', [2, 2, 128])
    ypD = dout('yp', [1024, 1024]); yoD = dout('yo', [256, 1024]); nkD = dout('nk', [1024, 512]); nvD = dout('nv', [1024, 512])
    spP = {(n, o): nc.dram_tensor(f'spP{n}_{o}', [128, n // 128, 512], BF16).ap() for n in (256, 1024) for o in (0, 1)}
    spQ = {(n, o): nc.dram_tensor(f'spQ{n}_{o}', [128, n // 128, 512], BF16).ap() for n in (256, 1024) for o in (0, 1)}
    spN = {(n, o): nc.dram_tensor(f'spN{n}_{o}', [1, 512], F32).ap() for n in (256, 1024) for o in (0, 1)}
    dbg = {}

    def dump(name, ap, shape, rkey):
        if name in debug:
            kb.barrier()
            d = nc.dram_tensor('dbg_' + name, list(shape), F32, kind="ExternalOutput").ap()
            if ap.dtype == F32:
                kb.dma('sp', d, ap, r=[rkey])
            else:
                with ExitStack() as ds:
                    tmp = kb.sb('dt_' + name, list(shape), F32, ds)
                    kb.op('dve', lambda: nc.vector.tensor_copy(tmp[:], ap), r=[rkey], w=['dtmp'])
                    kb.dma('sp', d, tmp[:], r=['dtmp'])
                    kb.barrier()

    V = lambda fn, r=(), w=(): kb.op('dve', fn, r, w)
    A = lambda fn, r=(), w=(): kb.op('act', fn, r, w)
    P = lambda fn, r=(), w=(): kb.op('pool', fn, r, w)
    T = lambda fn, r=(), w=(): kb.op('pe', fn, r, w)
    MM = nc.tensor.matmul
    cnt = [0]

    def alt(fa, fv, r, w):
        cnt[0] += 1
        if fa is None:
            V(fv, r, w)
        elif cnt[0] % 2:
            A(fa, r, w)
        else:
            V(fv, r, w)

    def pipeline_gen(units, depth=1, oldest_first=False, order=None):
        n = len(units)
        S = max(len(u) for u in units)
        for step in range(n + (S - 1) * depth):
            for st_ in (order if order is not None else (reversed(range(S)) if oldest_first else range(S))):
                u = step - st_ * depth
                if 0 <= u < n and st_ < len(units[u]):
                    units[u][st_]()
            yield step

    def pipeline(units, depth=1, oldest_first=False, order=None):
        for _ in pipeline_gen(units, depth, oldest_first, order):
            pass

    NPB = 6
    pb = [kb.ps(f'pb{i}', [128, 512], F32) for i in range(NPB)]
    pbb = [kb.ps(f'pbT{i}', [128, 1024], BF16) for i in range(2)]
    nbk = [0, 0]

    def bank():
        i = nbk[0] % NPB
        nbk[0] += 1
        return i, ('pb', i)

    def tbank():
        i = nbk[1] % 2
        nbk[1] += 1
        return i, ('pbT', i)

    identb = kb.sb('identb', [128, 128], BF16); identf = kb.sb('identf', [128, 128], F32)
    sel2 = kb.sb('sel2', [2, 2, 128], F32)
    epst = kb.sb('epst', [128, 1], F32); negpi = kb.sb('negpi', [128, 1], F32)
    modT = kb.sb('modT', [128, 48, 2], F32)
    gbc = {(c, g): kb.sb(f'gbc{c}{g}', [128, 1024], F32) for c in (0, 1) for g in (0, 1)}
    nlam = kb.sb('nlam', [128, 1], F32)
    sublnbc = kb.sb('sublnbc', [128, 128], F32)
    convp = kb.sb('convp', [128, 12, 4], F32)
    NR = 3
    ring = [kb.sb(f'ring{i}', [128, 8, 512], BF16) for i in range(NR)]
    kb.dma('sp', identb[:], identbD, w=['identb'])
    kb.dma('sp', identf[:], identfD, w=['identf'])
    kb.dma('sp', sel2[:], sel2D, w=['sel2'])
    V(lambda: nc.vector.memset(epst[:], EPS), w=['epst'])
    V(lambda: nc.vector.memset(negpi[:], -math.pi), w=['negpi'])

    plan = []
    for b in range(12):
        plan.append(modwD.rearrange('(kc p) c -> p kc c', p=128)[:, :, b * 512:(b + 1) * 512])
    for b in (0, 1, 2, 5, 3, 4):
        plan.append(winD.rearrange('(kc p) c -> p kc c', p=128)[:, :, b * 512:(b + 1) * 512])
    for b in range(2):
        plan.append(woutD.rearrange('(kc p) c -> p kc c', p=128)[:, :, b * 512:(b + 1) * 512])
    issued = [0]

    def acquire(i):
        while issued[0] < min(len(plan), i + NR):
            j = issued[0]
            kb.dma('pool', ring[j % NR][:], plan[j], w=[('ring', j % NR)])
            issued[0] += 1
        return ring[i % NR], ('ring', i % NR)

    with ExitStack() as sc:
        crow = kb.sb('crow', [2, 1024], F32, sc); srow = kb.sb('srow', [2, 1024], F32, sc)
        sT = kb.sb('sT', [128, 16], BF16, sc)
        mbb = [kb.sb(f'mbb{i}', [2, 512], F32, sc) for i in range(2)]; mrow = [kb.sb(f'mrow{i}', [2, 512], F32, sc) for i in range(2)]
        lq = kb.sb('lq', [128, 4, 64], F32, sc); lpr = kb.sb('lpr', [128, 2, 64], F32, sc); ls = kb.sb('ls', [128, 2], F32, sc)
        cwrow = kb.sb('cwrow', [4, 1536], F32, sc)

        def p1_prologue():
            kb.dma('sp', crow[:], cvecD, w=['crow'])
            A(lambda: nc.scalar.activation(srow[:], crow[:], AF.Silu), r=['crow'], w=['srow'])
            i, k = bank()
            for kc in range(8):
                T(lambda: nc.tensor.transpose(pb[i][:, kc * 2:kc * 2 + 2], srow[0:2, kc * 128:(kc + 1) * 128], identf[0:2, 0:2]), r=['srow', 'identf'], w=[k])
            V(lambda: nc.vector.tensor_copy(sT[:], pb[i][:, 0:16]), r=[k], w=['sT'])

        def p1_block(blk):
            def f():
                slot, sk = acquire(blk)
                mb_, mr_ = mbb[blk % 2], mrow[blk % 2]
                i, k = bank()
                kb.mmg([(lambda kc=kc: MM(pb[i][0:2, :], lhsT=sT[:, kc * 2:kc * 2 + 2], rhs=slot[:, kc, :], start=(kc == 0), stop=(kc == 7))) for kc in range(8)], r=['sT', sk], w=[k])
                kb.dma('sp', mb_[:], modbD[blk * 512:(blk + 1) * 512].partition_broadcast(2), w=[('mbb', blk % 2)])
                V(lambda: nc.vector.tensor_tensor(mr_[:], pb[i][0:2, :], mb_[:], ALU.add), r=[k, ('mbb', blk % 2)], w=[('mrow', blk % 2)])
                j, kj = bank()
                for q in range(4):
                    T(lambda: nc.tensor.transpose(pb[j][:, q * 2:q * 2 + 2], mr_[0:2, q * 128:(q + 1) * 128], identf[0:2, 0:2]), r=[('mrow', blk % 2), 'identf'], w=[kj])
                V(lambda: nc.vector.tensor_copy(modT[:, blk * 4:(blk + 1) * 4, :], pb[j][:, 0:8].rearrange('p (a b) -> p a b', b=2)), r=[kj], w=['modT'])
                if blk in (4, 5, 10, 11):
                    g = 0 if blk < 6 else 1
                    half = blk % 2
                    for cvi in (0, 1):
                        i2, k2 = bank()
                        T(lambda: MM(pb[i2][:, :], lhsT=sel2[0:2, cvi, :], rhs=mr_[0:2, :], start=True, stop=True), r=['sel2', ('mrow', blk % 2)], w=[k2])
                        A(lambda: nc.scalar.copy(gbc[(cvi, g)][:, half * 512:(half + 1) * 512], pb[i2][:, :]), r=[k2], w=[('gbc', cvi, g)])
            return f

        def p1_epilogue():
            V(lambda: nc.vector.tensor_scalar(modT[:, 8:16, :], modT[:, 8:16, :], 1.0, None, ALU.add), r=['modT'], w=['modT'])
            V(lambda: nc.vector.tensor_scalar(modT[:, 32:40, :], modT[:, 32:40, :], 1.0, None, ALU.add), r=['modT'], w=['modT'])
            for q in range(4):
                kb.dma('sp', lq[:, q, :], lqD[q].partition_broadcast(128), w=['lq'])
            V(lambda: nc.vector.tensor_tensor(lpr[:, 0, :], lq[:, 0, :], lq[:, 1, :], ALU.mult), r=['lq'], w=['lpr'])
            V(lambda: nc.vector.tensor_tensor(lpr[:, 1, :], lq[:, 2, :], lq[:, 3, :], ALU.mult), r=['lq'], w=['lpr'])
            V(lambda: nc.vector.reduce_sum(ls[:], lpr[:], axis=AX.X), r=['lpr'], w=['ls'])
            A(lambda: nc.scalar.activation(ls[:], ls[:], AF.Exp), r=['ls'], w=['ls'])
            V(lambda: nc.vector.scalar_tensor_tensor(nlam[:], ls[:, 1:2], -LAM_INIT, ls[:, 0:1], ALU.add, ALU.subtract), r=['ls'], w=['nlam'])
            kb.dma('sp', sublnbc[:], sublnD.partition_broadcast(128), w=['sublnbc'])
            V(lambda: nc.vector.tensor_scalar(sublnbc[:], sublnbc[:], 1.0 - LAM_INIT, None, ALU.mult), r=['sublnbc'], w=['sublnbc'])
            kb.dma('sp', cwrow[0:3, :], cwD, w=['cwrow'])
            kb.dma('sp', cwrow[3:4, :], cbD.unsqueeze(0), w=['cwrow'])
            i, k = bank()
            for ct in range(12):
                T(lambda: nc.tensor.transpose(pb[i][:, ct * 4:ct * 4 + 4], cwrow[0:4, ct * 128:(ct + 1) * 128], identf[0:4, 0:4]), r=['cwrow', 'identf'], w=[k])
            V(lambda: nc.vector.tensor_copy(convp[:], pb[i][:, 0:48].rearrange('p (a b) -> p a b', b=4)), r=[k], w=['convp'])

        p1_units = [p1_prologue] + [p1_block(b_) for b_ in range(12)] + [p1_epilogue]
        p1_next = [0]

        def p1_tick(n_=1):
            for _ in range(n_):
                if p1_next[0] < len(p1_units):
                    p1_units[p1_next[0]]()
                    p1_next[0] += 1

        w1s = kb.sb('w1s', [33, 64], F32, sc); w2s = kb.sb('w2s', [64, 64], F32, sc)
        w3f = kb.sb('w3f', [64, 2048], F32, sc); w3b = kb.sb('w3b', [64, 2048], BF16, sc)
        frow = kb.sb('frow', [3, 64], F32, sc); fmv = kb.sb('fmv', [64, 4], F32, sc)
        decbc = kb.sb('decbc', [128, 2048], F32, sc); skrow = kb.sb('skrow', [1, 1024], F32, sc)
        kb.dma('sp', w1s[:], hw1D, w=['w1s']); kb.dma('sp', w2s[:], hw2D, w=['w2s'])
        kb.dma('sp', w3f[:], hw3D, w=['w3f'])
        kb.dma('sp', frow[0:1, :], hfD.unsqueeze(0), w=['frow'])
        kb.dma('sp', frow[1:2, :], hb1D.unsqueeze(0), w=['frow'])
        kb.dma('sp', frow[2:3, :], hb2D.unsqueeze(0), w=['frow'])
        kb.dma('sp', decbc[:], hdecD.partition_broadcast(128), w=['decbc'])
        kb.dma('sp', skrow[:], hskD.unsqueeze(0), w=['skrow'])
        p1_tick(2)
        V(lambda: nc.vector.tensor_copy(w3b[:], w3f[:]), r=['w3f'], w=['w3b'])
        A(lambda: nc.scalar.activation(decbc[:], decbc[:], AF.Abs), r=['decbc'], w=['decbc'])
        i, k = bank()
        T(lambda: nc.tensor.transpose(pb[i][0:64, 0:3], frow[0:3, :], identf[0:3, 0:3]), r=['frow', 'identf'], w=[k])
        V(lambda: nc.vector.tensor_copy(fmv[:, 0:3], pb[i][0:64, 0:3]), r=[k], w=['fmv'])
        V(lambda: nc.vector.tensor_scalar(fmv[:, 1:3], fmv[:, 1:3], fmv[:, 0:1], None, ALU.mult), r=['fmv'], w=['fmv'])
        for n in (1024, 256):
            NT = n // 128
            with ExitStack() as s2:
                cf = kb.sb(f'f_cf{n}', [128, NT, n], BF16, s2); bfm = kb.sb(f'f_bf{n}', [128, NT, n], BF16, s2)
                kb.dma('sp', cf[:], cfD[n].rearrange('(st p) k -> p st k', p=128), w=['cf'])
                kb.dma('sp', bfm[:], bfD[n].rearrange('(st p) k -> p st k', p=128), w=['bf'])
                zt = kb.sb(f'zt{n}', [33, n], F32, s2); ntt = kb.sb(f'ntt{n}', [128, NT], F32, s2)
                wkt = kb.sb(f'wkt{n}', [128, NT], F32, s2)
                kb.dma('sp', zt[:], ztD[n], w=['zt']); kb.dma('sp', ntt[:], ntD[n], w=['ntt'])
                kb.dma('sp', wkt[:], wkD[n], w=['wkt'])
                arg = kb.sb(f'arg{n}', [64, n], F32, s2); h1 = kb.sb(f'h1{n}', [64, n], F32, s2)
                h2b = kb.sb(f'h2b{n}', [64, n], BF16, s2)
                rr = kb.sb(f'rr{n}', [64, 512], F32, s2); rr2 = kb.sb(f'rr2{n}', [64, 512], F32, s2)
                for (wl, wlk, src, srck, dst, dstk, bcol) in ((w1s, 'w1s', zt, 'zt', h1, 'h1', 1), (w2s, 'w2s', h1, 'h1', h2b, 'h2b', 2)):
                    for c0 in range(0, n, 512):
                        cw = min(512, n - c0)
                        i, k = bank()
                        T(lambda: MM(pb[i][0:64, 0:cw], lhsT=wl[:], rhs=src[:, c0:c0 + cw], start=True, stop=True), r=[wlk, srck], w=[k])
                        V(lambda: nc.vector.tensor_scalar(arg[:, c0:c0 + cw], pb[i][0:64, 0:cw], fmv[:, 0:1], fmv[:, bcol:bcol + 1], ALU.mult, ALU.add), r=[k, 'fmv'], w=[('arg', c0)])
                        V(lambda: nc.vector.tensor_scalar(rr[:, 0:cw], arg[:, c0:c0 + cw], math.pi, -TWO_PI, ALU.is_gt, ALU.mult), r=[('arg', c0)], w=['rr'])
                        V(lambda: nc.vector.tensor_scalar(rr2[:, 0:cw], arg[:, c0:c0 + cw], -math.pi, TWO_PI, ALU.is_lt, ALU.mult), r=[('arg', c0)], w=['rr2'])
                        V(lambda: nc.vector.tensor_tensor(rr[:, 0:cw], rr[:, 0:cw], rr2[:, 0:cw], ALU.add), r=['rr', 'rr2'], w=['rr'])
                        V(lambda: nc.vector.tensor_tensor(arg[:, c0:c0 + cw], arg[:, c0:c0 + cw], rr[:, 0:cw], ALU.add), r=[('arg', c0), 'rr'], w=[('arg', c0)])
                        A(lambda: nc.scalar.activation(dst[:, c0:c0 + cw], arg[:, c0:c0 + cw], AF.Sin), r=[('arg', c0)], w=[dstk])
                    p1_tick()
                hsd = kb.sb(f'hsd{n}', [128, NT, 512], BF16, s2); hdd = kb.sb(f'hdd{n}', [128, NT, 512], BF16, s2)
                dEs = [kb.sb(f'dE{n}_{q}', [128, 1024], F32, s2) for q in range(2)]
                f0s = [kb.sb(f'f0{n}_{q}', [128, 512], F32, s2) for q in range(2)]; f1s = [kb.sb(f'f1{n}_{q}', [128, 512], F32, s2) for q in range(2)]
                Pq = kb.sb(f'Pq{n}', [128, NT, 512], BF16, s2); Qq = kb.sb(f'Qq{n}', [128, NT, 512], BF16, s2)
                pnr = kb.sb(f'pnr{n}', [1, 512], F32, s2)
                for o in (0, 1):
                    for st in range(NT):
                        q = st % 2
                        dE, f0, f1 = dEs[q], f0s[q], f1s[q]
                        A(lambda: nc.scalar.activation(dE[:], decbc[:, o * 1024:(o + 1) * 1024], AF.Exp, scale=ntt[:, st:st + 1]), r=['decbc', 'ntt'], w=[('dE', q)])
                        ia, ka = bank(); ib, kbk = bank()
                        T(lambda: MM(pb[ia][:, :], lhsT=h2b[:, st * 128:(st + 1) * 128], rhs=w3b[:, o * 1024:o * 1024 + 512], start=True, stop=True), r=['h2b', 'w3b'], w=[ka])
                        T(lambda: MM(pb[ib][:, :], lhsT=h2b[:, st * 128:(st + 1) * 128], rhs=w3b[:, o * 1024 + 512:o * 1024 + 1024], start=True, stop=True), r=['h2b', 'w3b'], w=[kbk])
                        V(lambda: nc.vector.tensor_tensor(f0[:], pb[ia][:, :], dE[:, 0:512], ALU.mult), r=[ka, ('dE', q)], w=[('f0', q)])
                        V(lambda: nc.vector.tensor_tensor(f1[:], pb[ib][:, :], dE[:, 512:1024], ALU.mult), r=[kbk, ('dE', q)], w=[('f1', q)])
                        if st == 0:
                            V(lambda: nc.vector.memset(f1[0:1, :], 0.0), r=[('f1', q)], w=[('f1', q)])
                            V(lambda: nc.vector.tensor_tensor(f0[0:1, :], f0[0:1, :], skrow[0:1, o * 512:(o + 1) * 512], ALU.add), r=[('f0', q), 'skrow'], w=[('f0', q)])
                        P(lambda: nc.gpsimd.tensor_tensor(hsd[:, st, :], f0[:], f1[:], ALU.add), r=[('f0', q), ('f1', q)], w=[('hsd', st)])
                        V(lambda: nc.vector.tensor_tensor(hdd[:, st, :], f0[:], f1[:], ALU.subtract), r=[('f0', q), ('f1', q)], w=[('hdd', st)])
                        if st % 2 == 1:
                            p1_tick()
                    hs_keys = [('hsd', st) for st in range(NT)]; hd_keys = [('hdd', st) for st in range(NT)]
                    for kt in range(NT):
                        ia, ka = bank(); ib, kbk = bank()
                        kb.mmg([(lambda st=st: MM(pb[ia][:, :], lhsT=cf[:, st, kt * 128:(kt + 1) * 128], rhs=hsd[:, st, :], start=(st == 0), stop=(st == NT - 1))) for st in range(NT)], r=['cf'] + hs_keys, w=[ka])
                        kb.mmg([(lambda st=st: MM(pb[ib][:, :], lhsT=bfm[:, st, kt * 128:(kt + 1) * 128], rhs=hdd[:, st, :], start=(st == 0), stop=(st == NT - 1))) for st in range(NT)], r=['bf'] + hd_keys, w=[kbk])
                        A(lambda: nc.scalar.activation(Pq[:, kt, :], pb[ia][:, :], AF.Identity, scale=wkt[:, kt:kt + 1]), r=[ka, 'wkt'], w=[('Pq', kt)])
                        V(lambda: nc.vector.tensor_scalar(Qq[:, kt, :], pb[ib][:, :], wkt[:, kt:kt + 1], None, ALU.mult), r=[kbk, 'wkt'], w=[('Qq', kt)])
                        if kt == 0:
                            V(lambda: nc.vector.memset(Qq[0:1, 0, :], 0.0), r=[('Qq', 0)], w=[('Qq', 0)])
                        if kt % 2 == 1:
                            p1_tick()
                    i, k = bank()
                    kb.mmg([(lambda st=st: MM(pb[i][0:1, :], lhsT=bfm[:, st, 0:1], rhs=hsd[:, st, :], start=(st == 0), stop=(st == NT - 1))) for st in range(NT)], r=['bf'] + hs_keys, w=[k])
                    V(lambda: nc.vector.tensor_scalar(pnr[:], pb[i][0:1, :], 1.0 / (2 * n), None, ALU.mult), r=[k], w=['pnr'])
                    kb.dma('sp', spP[(n, o)], Pq[:], r=[('Pq', kt) for kt in range(NT)], w=[('spP', n, o)])
                    kb.dma('sp', spQ[(n, o)], Qq[:], r=[('Qq', kt) for kt in range(NT)], w=[('spQ', n, o)])
                    kb.dma('sp', spN[(n, o)], pnr[:], r=['pnr'], w=[('spN', n, o)])
                if n == 256:
                    p1_tick(100)
                kb.barrier()
        dump('modT', modT[:], [128, 48, 2], 'modT')
        kb.barrier()
    if stop_after <= 1:
        kb.finish()
        return kb

    S_mix = ExitStack(); S_hy = ExitStack(); S_hyP = ExitStack(); S_at = ExitStack(); S_h = ExitStack()
    hTP = kb.sb('hTP', [128, 8, 1024], BF16, S_mix); hTO = kb.sb('hTO', [128, 8, 264], BF16, S_mix)
    mixP = hTP; mixO = hTO
    vS = kb.sb('vS', [128, 8, 512], BF16, S_hy); x1S = kb.sb('x1S', [128, 8, 512], BF16, S_hy); x2O = kb.sb('x2O', [128, 4, 256], BF16, S_hy)
    vP = kb.sb('vP', [128, 8, 512], BF16, S_hyP); x1P = kb.sb('x1P', [128, 8, 512], BF16, S_hyP); x2P = kb.sb('x2P', [128, 4, 1024], BF16, S_hyP)
    QTP = kb.sb('QTP', [128, 4, 1024], BF16, S_at); KTP = kb.sb('KTP', [128, 4, 1024], BF16, S_at)
    VP = kb.sb('VP', [128, 8, 4, 130], BF16, S_at)
    QTO = kb.sb('QTO', [128, 4, 256], BF16, S_at); KTS = kb.sb('KTS', [128, 4, 1280], BF16, S_at)
    VS = kb.sb('VS', [128, 10, 4, 130], BF16, S_at)
    hTS = kb.sb('hTS', [128, 8, 1024], BF16, S_h)

    def ln_rstd(mv_ap, rstd_ap, lnv_ap, npart, rk, wk_):
        A(lambda: nc.scalar.activation(lnv_ap, mv_ap, AF.Ln, bias=epst[0:npart, 0:1], scale=1.0), r=[rk, 'epst'], w=[wk_ + 'l'])
        A(lambda: nc.scalar.activation(rstd_ap, lnv_ap, AF.Exp, scale=-0.5), r=[wk_ + 'l'], w=[wk_])

    def transpose_mod(xn, xnk, nt, dst, dstk, t0, cvi, sc_c0, sh_c0, defer=False):
        for hb in range(2):
            kb.mmg([(lambda kc=kc: nc.tensor.transpose(pbb[hb][:, (kc - 4 * hb) * 128:(kc - 4 * hb) * 128 + nt], xn[0:nt, kc * 128:(kc + 1) * 128], identb[0:nt, 0:nt])) for kc in range(4 * hb, 4 * hb + 4)],
                   r=[xnk, 'identb'], w=[('pbT', hb)])

        def evac():
            for kc in range(4):
                A(lambda: nc.scalar.activation(dst[:, kc, t0:t0 + nt], pbb[0][:, kc * 128:kc * 128 + nt], AF.Identity, bias=modT[:, sh_c0 + kc, cvi:cvi + 1], scale=modT[:, sc_c0 + kc, cvi:cvi + 1]), r=[('pbT', 0), 'modT'], w=[(dstk, kc)])
            for kc in range(4, 8):
                V(lambda: nc.vector.tensor_scalar(dst[:, kc, t0:t0 + nt], pbb[1][:, (kc - 4) * 128:(kc - 4) * 128 + nt], modT[:, sc_c0 + kc, cvi:cvi + 1], modT[:, sh_c0 + kc, cvi:cvi + 1], ALU.mult, ALU.add), r=[('pbT', 1), 'modT'], w=[(dstk, kc)])
        if defer:
            return evac
        evac()

    with ExitStack() as sc:
        NB = 4
        xt = [kb.sb(f'xt{i}', [128, 1024], F32, sc) for i in range(NB)]
        xn = [kb.sb(f'xn{i}', [128, 1024], BF16, sc) for i in range(NB)]
        st = [kb.sb(f'st{i}', [128, 12], F32, sc) for i in range(NB)]
        mv = [kb.sb(f'mv{i}', [128, 4], F32, sc) for i in range(NB)]
        units2 = []

        def unit_ln1(xD, t0, nt, cvi, hT, hk, b):
            X, N, S_, M = xt[b], xn[b], st[b], mv[b]

            def sL():
                kb.dma('sp', X[0:nt, :], xD[t0:t0 + nt, :], w=[('xt', b)])

            def s0():
                V(lambda: nc.vector.bn_stats(S_[0:nt, 0:6], X[0:nt, 0:512]), r=[('xt', b)], w=[('st', b, 0)])
                V(lambda: nc.vector.bn_stats(S_[0:nt, 6:12], X[0:nt, 512:1024]), r=[('xt', b)], w=[('st', b, 1)])
                V(lambda: nc.vector.bn_aggr(M[0:nt, 0:2], S_[0:nt, :]), r=[('st', b, 0), ('st', b, 1)], w=[('mv', b)])
                ln_rstd(M[0:nt, 1:2], M[0:nt, 3:4], M[0:nt, 2:3], nt, ('mv', b), f'rs{b}')
                V(lambda: nc.vector.scalar_tensor_tensor(M[0:nt, 2:3], M[0:nt, 0:1], -1.0, M[0:nt, 3:4], ALU.mult, ALU.mult), r=[('mv', b), f'rs{b}'], w=[f'nmr{b}'])

            st_ = {}

            def s1():
                A(lambda: nc.scalar.activation(N[0:nt, :], X[0:nt, :], AF.Identity, bias=M[0:nt, 2:3], scale=M[0:nt, 3:4]), r=[('xt', b), f'rs{b}', f'nmr{b}'], w=[('xn', b)])
                st_['ev'] = transpose_mod(N, ('xn', b), nt, hT, hk, t0, cvi, 8, 0, defer=True)

            def s2():
                st_['ev']()
            return [sL, s0, s1, s2]

        rot = 0
        for (xD, ntok, cvi, hT, hk) in ((xpD, 1024, 0, hTP, 'hTP'), (xsD, 1024, 1, hTS, 'hTS'), (xoD, 258, 1, hTO, 'hTO')):
            for t0 in range(0, ntok, 128):
                units2.append(unit_ln1(xD, t0, min(128, ntok - t0), cvi, hT, hk, rot % NB))
                rot += 1
        pipeline(units2, 1, order=[0, 1, 3, 2])
        dump('hTP', hTP[:, :, 0:32], [128, 8, 32], ('hTP', 7))
        kb.barrier()
    if stop_after <= 2:
        kb.finish()
        return kb


    S_t3 = ExitStack()

    class Rot:
        def __init__(self, name, n, shape, dt, scope):
            self.t = [kb.sb(f'{name}{i}', shape, dt, scope) for i in range(n)]
            self.name = name
            self.i = -1

        def next(self):
            self.i += 1
            j = self.i % len(self.t)
            return self.t[j], (self.name, j)

    kstR = Rot('kst', 2, [128, 512], F32, S_t3); kbfR = Rot('kbf', 2, [128, 512], BF16, S_t3)
    ropeS = kb.sb('ropeS', [128, 8, 2, 64], F32, S_t3); ropeO = kb.sb('ropeO', [128, 2, 2, 64], F32, S_t3)
    hmask = kb.sb('hmask', [128, 2], F32, S_t3)
    usPt = [kb.sb(f'usP{i}', [128, 1032], F32, S_t3) for i in range(2)]; usSt = [kb.sb(f'usS{i}', [128, 1032], F32, S_t3) for i in range(2)]
    usO = kb.sb('usO', [128, 258], F32, S_t3)
    accR = Rot('acc', 2, [128, 1024], F32, S_t3); cvoR = Rot('cvo', 2, [128, 1024], BF16, S_t3)
    kb.dma('sp', ropeS[:], ropeSD, w=['ropeS']); kb.dma('sp', ropeO[:], ropeOD, w=['ropeO']); kb.dma('sp', hmask[:], hmaskD, w=['hmask'])
    V(lambda: nc.vector.memset(VP[:, :, :, 128:130], 1.0), w=['VPones'])
    V(lambda: nc.vector.memset(VS[:, :, :, 128:130], 1.0), w=['VSones'])

    def proj_tm(slot, sk, hT, t0, nt):
        i, k = bank()
        kb.mmg([(lambda kc=kc: MM(pb[i][0:nt, :], lhsT=hT[:, kc, t0:t0 + nt], rhs=slot[:, kc, :], start=(kc == 0), stop=(kc == 7))) for kc in range(8)], r=[sk], w=[k])
        return i, k

    def proj_fm(slot, sk, ct, hT, t0, nt):
        i, k = bank()
        kb.mmg([(lambda kc=kc: MM(pb[i][:, 0:nt], lhsT=slot[:, kc, ct * 128:(ct + 1) * 128], rhs=hT[:, kc, t0:t0 + nt], start=(kc == 0), stop=(kc == 7))) for kc in range(8)], r=[sk], w=[k])
        return i, k

    def to_fm(src_bf, srck, dst, dstk, c0):
        j, kj = tbank()
        kb.mmg([(lambda ct=ct: nc.tensor.transpose(pbb[j][:, ct * 128:(ct + 1) * 128], src_bf[:, ct * 128:(ct + 1) * 128], identb[:, :])) for ct in range(4)], r=[srck, 'identb'], w=[kj])
        V(lambda: nc.vector.tensor_copy(dst[:, 0:4, c0:c0 + 128], pbb[j][:, 0:512].rearrange('p (a b) -> p a b', b=128)), r=[kj], w=[(dstk, c0)])

    def rope_tile(K_, kk, tab, tabk, tt):
        rt, rtk = usPt[1][:, 0:512], ('usP', 1); ru, ruk = usSt[1][:, 0:512], ('usS', 1); B_, bk = kbfR.next()
        x3 = K_[:].rearrange('p (m d) -> p m d', d=64)
        cosb = tab[:, tt, 0, :].unsqueeze(1).to_broadcast([128, 8, 64])
        V(lambda: nc.vector.tensor_tensor(rt.rearrange('p (m d) -> p m d', d=64), x3, cosb, ALU.mult), r=[kk, tabk], w=[rtk])
        x5 = K_[:].rearrange('p (m a r i) -> p m a r i', m=8, a=2, r=2, i=16)
        u5 = ru.rearrange('p (m a r i) -> p m a r i', m=8, a=2, r=2, i=16)
        s5 = tab[:, tt, 1, :].rearrange('p (a r i) -> p a r i', a=2, r=2)
        for r_ in (0, 1):
            P(lambda: nc.gpsimd.tensor_tensor(u5[:, :, :, r_, :], x5[:, :, :, 1 - r_, :], s5[:, :, r_, :].unsqueeze(1).to_broadcast([128, 8, 2, 16]), ALU.mult), r=[kk, tabk], w=[ruk + (r_,)])
        V(lambda: nc.vector.tensor_tensor(B_[:], rt, ru, ALU.add), r=[rtk, ruk + (0,), ruk + (1,)], w=[bk])
        return B_, bk

    def evac_kst(i, k):
        K_, kk = kstR.next()
        A(lambda: nc.scalar.copy(K_[:], pb[i][:, :]), r=[k], w=[kk])
        return K_, kk

    def cast_bf(K_, kk):
        B_, bk = kbfR.next()
        V(lambda: nc.vector.tensor_copy(B_[:], K_[:]), r=[kk], w=[bk])
        return B_, bk

    units3 = []
    cur = {}

    def u_acq(blk):
        def f():
            cur['slot'], cur['sk'] = acquire(blk)
        return f

    def unit_q_p(ct, ch):
        def s0():
            i, k = proj_fm(cur['slot'], cur['sk'], ct, hTP, ch * 512, 512)
            alt(lambda: nc.scalar.copy(QTP[:, ct, ch * 512:(ch + 1) * 512], pb[i][:, :]),
                lambda: nc.vector.tensor_copy(QTP[:, ct, ch * 512:(ch + 1) * 512], pb[i][:, :]), r=[k], w=[('QTP', ct, ch)])
        return [s0]

    def unit_rope(hT, t0, tab, tabk, tt, dst, dstk, c0):
        st_ = {}

        def s0():
            i, k = proj_tm(cur['slot'], cur['sk'], hT, t0, 128)
            K_, kk = evac_kst(i, k)
            st_['b'] = rope_tile(K_, kk, tab, tabk, tt)

        def s1():
            to_fm(st_['b'][0], st_['b'][1], dst, dstk, c0)
        return [s0, s1]

    def unit_k_p(tt):
        st_ = {}

        def s0():
            i, k = proj_tm(cur['slot'], cur['sk'], hTP, tt * 128, 128)
            K_, kk = evac_kst(i, k)
            kb.dma('sp', nkD[tt * 128:(tt + 1) * 128, :], K_[:], r=[kk], w=[('nk', tt)])
            st_['b'] = cast_bf(K_, kk)

        def s1():
            to_fm(st_['b'][0], st_['b'][1], KTP, 'KTP', tt * 128)
        return [s0, s1]

    def unit_ctx_k(kt):
        st_ = {}

        def s0():
            K_, kk = kstR.next()
            kb.dma('sp', K_[:], ckD[kt * 128:(kt + 1) * 128, :], w=[kk])
            st_['b'] = cast_bf(K_, kk)

        def s1():
            to_fm(st_['b'][0], st_['b'][1], KTS, 'KTS', kt * 128)
        return [s0, s1]

    def unit_v_p(tt):
        def s0():
            i, k = proj_tm(cur['slot'], cur['sk'], hTP, tt * 128, 128)
            K_, kk = evac_kst(i, k)
            kb.dma('sp', nvD[tt * 128:(tt + 1) * 128, :], K_[:], r=[kk], w=[('nv', tt)])
            V(lambda: nc.vector.tensor_copy(VP[:, tt, :, 0:128], K_[:].rearrange('p (a b) -> p a b', b=128)), r=[kk], w=[('VP', tt)])
        return [s0]

    def unit_v_s(tt):
        def s0():
            i, k = proj_tm(cur['slot'], cur['sk'], hTS, tt * 128, 128)
            V(lambda: nc.vector.tensor_copy(VS[:, 2 + tt, :, 0:128], pb[i][:, :].rearrange('p (a b) -> p a b', b=128)), r=[k], w=[('VS', 2 + tt)])
        return [s0]

    def unit_ctx_v(kt):
        def s0():
            K_, kk = kstR.next()
            kb.dma('sp', K_[:], cvD[kt * 128:(kt + 1) * 128, :], w=[kk])
            V(lambda: nc.vector.tensor_copy(VS[:, kt, :, 0:128], K_[:].rearrange('p (a b) -> p a b', b=128)), r=[kk], w=[('VS', kt)])
        return [s0]

    units3.append([u_acq(12)])
    for tt in range(2):
        units3.append(unit_rope(hTO, 1 + tt * 128, ropeO, 'ropeO', tt, QTO, 'QTO', tt * 128))
    for kt in range(2):
        units3.append(unit_ctx_k(kt))
    for kt in range(2):
        units3.append(unit_ctx_v(kt))
    for ct in range(4):
        for ch in range(2):
            units3.append(unit_q_p(ct, ch))
    units3.append([u_acq(13)])
    for tt in range(8):
        units3.append(unit_k_p(tt))
        units3.append(unit_rope(hTS, tt * 128, ropeS, 'ropeS', tt, KTS, 'KTS', 256 + tt * 128))
    units3.append([u_acq(14)])
    for tt in range(8):
        units3.append(unit_v_p(tt))
        units3.append(unit_v_s(tt))

    usPv = [t_[:, 0:1032].rearrange('p (s t) -> p s t', t=258) for t_ in usPt]
    rotP = [0]; rotS = [0]

    def conv3(ul, um, ur, usk, accv, acck, outv, ctg, outk):
        A(lambda: nc.scalar.activation(accv, um, AF.Identity, bias=convp[:, ctg, 3:4], scale=convp[:, ctg, 1:2]), r=[usk, 'convp'], w=[acck])
        V(lambda: nc.vector.scalar_tensor_tensor(accv, ul, convp[:, ctg, 0:1], accv, ALU.mult, ALU.add), r=[usk, acck, 'convp'], w=[acck])
        V(lambda: nc.vector.scalar_tensor_tensor(outv, ur, convp[:, ctg, 2:3], accv, ALU.mult, ALU.add), r=[usk, acck, 'convp'], w=[outk])

    def to_tm(C_, ck_, dst, dstk, ct):
        j, kj = tbank()
        kb.mmg([(lambda tt=tt: nc.tensor.transpose(pbb[j][:, tt * 128:(tt + 1) * 128], C_[:, tt * 128:(tt + 1) * 128], identb[:, :])) for tt in range(8)], r=[ck_, 'identb'], w=[kj])
        V(lambda: nc.vector.tensor_copy(dst[:, 0:8, ct * 128:(ct + 1) * 128], pbb[j][:, :].rearrange('p (a b) -> p a b', b=128)), r=[kj], w=[(dstk, ct)])

    def zero_pads():
        for q in range(2):
            V(lambda: nc.vector.memset(usPt[q][:], 0.0), w=[('usP', q)])
            V(lambda: nc.vector.memset(usSt[q][:], 0.0), w=[('usS', q), ('usS', q, 0), ('usS', q, 1)])

    def hy_P(slot, sk, ct, ctg, outv, outk):
        q = rotP[0] % 2; rotP[0] += 1
        usP = usPv[q]
        for ch in range(2):
            i, k = proj_fm(slot, sk, ct, hTP, ch * 512, 512)
            A(lambda: nc.scalar.copy(usP[:, 2 * ch:2 * ch + 2, 1:257], pb[i][:, :].rearrange('p (s t) -> p s t', t=256)), r=[k], w=[('usP', q)])
        ac_, ak = accR.next()
        a3 = ac_[:].rearrange('p (s t) -> p s t', t=256)
        conv3(usP[:, :, 0:256], usP[:, :, 1:257], usP[:, :, 2:258], ('usP', q), a3, ak, outv, ctg, outk)

    def hy_S(slot, sk, ct, ctg, outv, outk):
        q = rotS[0] % 2; rotS[0] += 1
        us = usSt[q]
        for ch in range(2):
            i, k = proj_fm(slot, sk, ct, hTS, ch * 512, 512)
            A(lambda: nc.scalar.copy(us[:, 1 + ch * 512:1 + (ch + 1) * 512], pb[i][:, :]), r=[k], w=[('usS', q)])
        ac_, ak = accR.next()
        conv3(us[:, 0:1024], us[:, 1:1025], us[:, 2:1026], ('usS', q), ac_[:], ak, outv, ctg, outk)

    def unit_hy(which, ct, ctg, dst, dstk):
        st_ = {}

        def s0():
            C_, ck_ = cvoR.next()
            st_['c'] = (C_, ck_)
            if which == 'P':
                hy_P(cur['slot'], cur['sk'], ct, ctg, C_[:].rearrange('p (s t) -> p s t', t=256), ck_)
            else:
                hy_S(cur['slot'], cur['sk'], ct, ctg, C_[:], ck_)

        def s1():
            to_tm(st_['c'][0], st_['c'][1], dst, dstk, ct)
        return [s0, s1]

    def unit_x2(ct):
        ctg = 8 + ct

        def s0():
            hy_P(cur['slot'], cur['sk'], ct, ctg, x2P[:, ct, :].rearrange('p (s t) -> p s t', t=256), ('x2P', ct))

        def s1():
            i, k = proj_fm(cur['slot'], cur['sk'], ct, hTO, 0, 258)
            A(lambda: nc.scalar.copy(usO[:], pb[i][:, 0:258]), r=[k], w=['usO'])
            V(lambda: nc.vector.tensor_scalar(usO[:, 0:1], usO[:, 0:1], hmask[:, 0:1], None, ALU.mult), r=['usO', 'hmask'], w=['usO'])
            V(lambda: nc.vector.tensor_scalar(usO[:, 257:258], usO[:, 257:258], hmask[:, 1:2], None, ALU.mult), r=['usO', 'hmask'], w=['usO'])
            ac_, ak = accR.next()
            conv3(usO[:, 0:256], usO[:, 1:257], usO[:, 2:258], 'usO', ac_[:, 0:256], ak, x2O[:, ct, :], ctg, ('x2O', ct))
        return [lambda: (s0(), s1())]

    units3.append([lambda: (u_acq(15)(), zero_pads())])
    for ct in range(4):
        units3.append(unit_x2(ct))
    for bi, (dP, dPk, dS, dSk) in enumerate(((vP, 'vP', vS, 'vS'), (x1P, 'x1P', x1S, 'x1S'))):
        units3.append([u_acq(16 + bi)])
        for ct in range(4):
            units3.append(unit_hy('P', ct, bi * 4 + ct, dP, dPk))
            units3.append(unit_hy('S', ct, bi * 4 + ct, dS, dSk))
    pipeline(units3, 1)
    kb.barrier()
    S_t3.close(); S_h.close()
    dump('QTO', QTO[:], [128, 4, 256], 'x')
    dump('KTS', KTS[:, :, 0:384], [128, 4, 384], 'x')
    dump('vS', vS[:, 0:2, :], [128, 2, 512], 'x')
    dump('x2O', x2O[:], [128, 4, 256], 'x')
    dump('x1P', x1P[:, 0:2, :], [128, 2, 512], 'x')
    if stop_after <= 3:
        kb.finish()
        return kb

    with ExitStack() as sc:
        EP = [[kb.sb(f'EP{s_}_{m}', [128, 2, 256], BF16, sc) for m in range(8)] for s_ in range(2)]
        EO = [kb.sb(f'EO_{m}', [128, 10, 256], BF16, sc) for m in range(4)]
        on = [kb.sb(f'on{q}', [128, 8, 128], F32, sc) for q in range(2)]
        araw = [kb.sb(f'araw{q}', [128, 4, 128], F32, sc) for q in range(2)]
        an = [kb.sb(f'an{q}', [128, 4, 128], F32, sc) for q in range(2)]; anb = [kb.sb(f'anb{q}', [128, 4, 128], BF16, sc) for q in range(2)]
        sq = [kb.sb(f'sq{q}', [128, 2, 128], BF16, sc) for q in range(2)]
        rz = [kb.sb(f'rz{q}', [128, 8], F32, sc) for q in range(2)]; ss = [kb.sb(f'ss{q}', [128, 8], F32, sc) for q in range(2)]

        def unit_attg(gi, QT, q0, KT, k0, nkt, Vg, vt0, mix, tok0, Eset, eid, maps=tuple(range(8)), final=True):
            def sA():
                for m in maps:
                    h = m // 2
                    pr = slice((m % 2) * 64, (m % 2) * 64 + 64)
                    for kp in range(0, nkt, 2):
                        i, k = bank()
                        for kt in (kp, kp + 1):
                            T(lambda: MM(pb[i][:, (kt - kp) * 256:(kt - kp + 1) * 256], lhsT=KT[pr, h, k0 + kt * 128:k0 + (kt + 1) * 128], rhs=QT[pr, h, q0:q0 + 256], start=True, stop=True), w=[k])
                        A(lambda: nc.scalar.activation(Eset[m - maps[0]][:, kp:kp + 2, :], pb[i][:, :].rearrange('p (a b) -> p a b', b=256), AF.Exp, scale=0.125), r=[k], w=[('E', eid, m - maps[0], kp)])

            def sB():
                for qt in range(2):
                    bks = []
                    for grp in [maps[g0:g0 + 3] for g0 in range(0, len(maps), 3)]:
                        i, k = bank()
                        for li, m in enumerate(grp):
                            h = m // 2
                            kb.mmg([(lambda kt=kt: MM(pb[i][:, li * 129:(li + 1) * 129], lhsT=Eset[m - maps[0]][:, kt, qt * 128:(qt + 1) * 128], rhs=Vg[:, vt0 + kt, h, 0:129], start=(kt == 0), stop=(kt == nkt - 1))) for kt in range(nkt)],
                                   r=[('E', eid, m - maps[0], kp) for kp in range(0, nkt, 2)], w=[k])
                        bks.append((i, k, grp))
                    for (i, k, grp) in bks:
                        n_ = len(grp); m0 = grp[0]
                        pv = pb[i][:, 0:n_ * 129].rearrange('p (a b) -> p a b', b=129)
                        V(lambda: nc.vector.reciprocal(rz[qt][:, m0:m0 + n_], pv[:, :, 128]), r=[k], w=[('rz', qt)])
                        V(lambda: nc.vector.tensor_tensor(on[qt][:, m0:m0 + n_, :], pv[:, :, 0:128], rz[qt][:, m0:m0 + n_].unsqueeze(2).to_broadcast([128, n_, 128]), ALU.mult), r=[k, ('rz', qt)], w=[('on', qt)])
                    if not final:
                        continue
                    onv = on[qt][:].rearrange('p (h two) e -> p h two e', two=2)
                    V(lambda: nc.vector.scalar_tensor_tensor(araw[qt][:], onv[:, :, 1, :], nlam[:, 0:1], onv[:, :, 0, :], ALU.mult, ALU.add), r=[('on', qt)], w=[('araw', qt)])
                    V(lambda: nc.vector.memset(ss[qt][:], 0.0), w=[('ss', qt, hh) for hh in range(4)] + [('ssl', qt)])
                if not final:
                    return
                for qt in range(2):
                    for hh in range(4):
                        A(lambda: nc.scalar.activation(sq[qt][:, hh % 2, :], araw[qt][:, hh, :], AF.Square, accum_out=ss[qt][:, hh:hh + 1]), r=[('araw', qt), ('ss', qt, hh)], w=[('ss', qt, hh), ('sq', qt, hh % 2)])
                    A(lambda: nc.scalar.activation(ss[qt][:, 4:8], ss[qt][:, 0:4], AF.Ln, bias=epst[:, 0:1], scale=1.0 / 128.0), r=[('ss', qt, hh) for hh in range(4)], w=[('ssl', qt)])
                    A(lambda: nc.scalar.activation(ss[qt][:, 4:8], ss[qt][:, 4:8], AF.Exp, scale=-0.5), r=[('ssl', qt)], w=[('ssl', qt)])
                for qt in range(2):
                    V(lambda: nc.vector.tensor_tensor(an[qt][:], araw[qt][:], ss[qt][:, 4:8].unsqueeze(2).to_broadcast([128, 4, 128]), ALU.mult), r=[('araw', qt), ('ssl', qt)], w=[('an', qt)])
                    P(lambda: nc.gpsimd.tensor_tensor(anb[qt][:], an[qt][:], sublnbc[:, :].unsqueeze(1).to_broadcast([128, 4, 128]), ALU.mult), r=[('an', qt)], w=[('anb', qt)])
                for qt in range(2):
                    j, kj = tbank()
                    kb.mmg([(lambda hh=hh: nc.tensor.transpose(pbb[j][:, hh * 128:(hh + 1) * 128], anb[qt][:, hh, :], identb[:, :])) for hh in range(4)], r=[('anb', qt)], w=[kj])
                    V(lambda: nc.vector.tensor_copy(mix[:, 0:4, tok0 + qt * 128:tok0 + (qt + 1) * 128], pbb[j][:, 0:512].rearrange('p (a b) -> p a b', b=128)), r=[kj], w=[('mixatt', tok0, qt)])
            return [sA, sB]

        unitsA = [unit_attg(b, QTP, b * 256, KTP, b * 256, 2, VP, b * 2, mixP, b * 256, EP[b % 2], b % 2) for b in range(4)]
        unitsA.append(unit_attg(4, QTO, 0, KTS, 0, 10, VS, 0, mixO, 0, EO, 2, maps=(0, 1, 2, 3), final=False))
        pipeline(unitsA, 1, oldest_first=True)
        pipeline([unit_attg(5, QTO, 0, KTS, 0, 10, VS, 0, mixO, 0, EO, 2, maps=(4, 5, 6, 7), final=True)], 1)
        dump('attP', mixP[:, 0:4, 0:256], [128, 4, 256], 'x')
        dump('attO', mixO[:, 0:4, 0:256], [128, 4, 256], 'x')
        kb.barrier()
    S_at.close()
    if stop_after <= 4:
        kb.finish()
        return kb
    def hyena_conv(NT, cf, bfm, bft, cfi, bfti, u1_tile, x1_tile, x2v, mixv, PQ, sc, tagp, stages=False, dk=None):
        dk = dk or {}
        kcf = dk.get('cf', []); kbf = dk.get('bf', []); kbft = dk.get('bft', []); kcfi = dk.get('cfi', []); kbfti = dk.get('bfti', [])
        Rb = kb.sb(tagp + 'Rb', [128, NT, 512], BF16, sc); Sb = kb.sb(tagp + 'Sb', [128, NT, 512], BF16, sc)
        zt_ = kb.sb(tagp + 'z', [128, NT, 512], BF16, sc)
        t1 = kb.sb(tagp + tagp + 't1', [128, 512], F32, sc); t2 = kb.sb(tagp + tagp + 't2', [128, 512], F32, sc)
        t3 = kb.sb(tagp + tagp + 't3', [128, 512], F32, sc); t4 = kb.sb(tagp + tagp + 't4', [128, 512], F32, sc)

        def fwd_pw(u_tile, ukeys, o):
            Pq, Qq, pn = PQ[o]
            kpq = dk.get(('PQ', o), [])
            for kt in range(NT):
                ia, ka = bank(); ib, kbk = bank()
                kb.mmg([(lambda st=st: MM(pb[ia][:, :], lhsT=cf[:, st, kt * 128:(kt + 1) * 128], rhs=u_tile(st), start=(st == 0), stop=(st == NT - 1))) for st in range(NT)], r=ukeys + kcf, w=[ka])
                kb.mmg([(lambda st=st: MM(pb[ib][:, :], lhsT=bfm[:, st, kt * 128:(kt + 1) * 128], rhs=u_tile(st), start=(st == 0), stop=(st == NT - 1))) for st in range(NT)], r=ukeys + kbf, w=[kbk])
                V(lambda: nc.vector.tensor_tensor(t1[:], pb[ia][:, :], Pq[:, kt, :], ALU.mult), r=[ka] + kpq, w=[tagp + 't1'])
                V(lambda: nc.vector.tensor_tensor(t2[:], pb[ib][:, :], Qq[:, kt, :], ALU.mult), r=[kbk] + kpq, w=[tagp + 't2'])
                P(lambda: nc.gpsimd.tensor_tensor(Rb[:, kt, :], t1[:], t2[:], ALU.subtract), r=[tagp + 't1', tagp + 't2'], w=[(tagp, 'Rb', kt)])
                V(lambda: nc.vector.tensor_tensor(t3[:], pb[ia][:, :], Qq[:, kt, :], ALU.mult), r=[ka], w=[tagp + 't3'])
                V(lambda: nc.vector.tensor_tensor(t4[:], pb[ib][:, :], Pq[:, kt, :], ALU.mult), r=[kbk], w=[tagp + 't4'])
                P(lambda: nc.gpsimd.tensor_tensor(Sb[:, kt, :], t3[:], t4[:], ALU.add), r=[tagp + 't3', tagp + 't4'], w=[(tagp, 'Sb', kt)])
                if kt == 0:
                    V(lambda: nc.vector.tensor_tensor(Sb[0:1, 0, :], pb[ib][0:1, :], pn[0:1, :], ALU.mult), r=[kbk, (tagp, 'Sb', 0)] + kpq, w=[(tagp, 'Sb', 0)])
        rs_keys = [(tagp, 'Rb', kt) for kt in range(NT)] + [(tagp, 'Sb', kt) for kt in range(NT)]
        def stA():
            fwd_pw(u1_tile, [], 0)

        def stB():
            inv1()

        def stC():
            fwd_pw(lambda st: zt_[:, st, :], [(tagp, 'z', tt) for tt in range(NT)], 1)

        def stD():
            inv2()

        def inv1():
          for tt in range(NT):
            i, k = bank()
            kb.mmg([(lambda kt=kt: MM(pb[i][:, :], lhsT=cf[:, kt, tt * 128:(tt + 1) * 128], rhs=Rb[:, kt, :], start=(kt == 0), stop=False)) for kt in range(NT)]
                   + [(lambda kt=kt: MM(pb[i][:, :], lhsT=bft[:, kt, tt * 128:(tt + 1) * 128], rhs=Sb[:, kt, :], start=False, stop=(kt == NT - 1))) for kt in range(NT)], r=rs_keys + kcf + kbft, w=[k])
            V(lambda: nc.vector.tensor_tensor(zt_[:, tt, :], pb[i][:, :], x1_tile(tt), ALU.mult), r=[k], w=[(tagp, 'z', tt)])

        def inv2():
          for pair in range(2):
            i, k = bank()
            for c2 in range(2):
                ct = pair * 2 + c2
                kb.mmg([(lambda kt=kt: MM(pb[i][:, c2 * 256:(c2 + 1) * 256], lhsT=Rb[:, kt, ct * 128:(ct + 1) * 128], rhs=cfi[:, kt, 0:256], start=(kt == 0), stop=False)) for kt in range(NT)]
                       + [(lambda kt=kt: MM(pb[i][:, c2 * 256:(c2 + 1) * 256], lhsT=Sb[:, kt, ct * 128:(ct + 1) * 128], rhs=bfti[:, kt, 0:256], start=False, stop=(kt == NT - 1))) for kt in range(NT)], r=rs_keys + kcfi + kbfti, w=[k])
            V(lambda: nc.vector.tensor_tensor(mixv[:, pair * 2:pair * 2 + 2, :], pb[i][:, :].rearrange('p (a b) -> p a b', b=256), x2v[:, pair * 2:pair * 2 + 2, :], ALU.mult), r=[k], w=[('hyout', tagp, pair)])

        if stages:
            return [stA, stB, stC, stD]
        stA(); stB(); stC(); stD()

    def load_spectra(n, sc, tagp):
        NT = n // 128
        PQ = []
        for o in (0, 1):
            Pq = kb.sb(f'{tagp}Pq{o}', [128, NT, 512], BF16, sc); Qq = kb.sb(f'{tagp}Qq{o}', [128, NT, 512], BF16, sc)
            pn = kb.sb(f'{tagp}pn{o}', [1, 512], F32, sc)
            kb.dma('sp', Pq[:], spP[(n, o)], w=[(tagp, 'Pq', o)]); kb.dma('sp', Qq[:], spQ[(n, o)], w=[(tagp, 'Qq', o)])
            kb.dma('sp', pn[:], spN[(n, o)], w=[(tagp, 'pn', o)])
            PQ.append((Pq, Qq, pn))
        return PQ

    def load_dft(n, sc, tagp, srcs):
        NT = n // 128
        out = []
        for nm, srcD in srcs:
            t_ = kb.sb(f'{tagp}{nm}', [128, NT, srcD.shape[1]], BF16, sc)
            kb.dma('sp', t_[:], srcD.rearrange('(st p) k -> p st k', p=128), w=[(tagp, nm)])
            out.append(t_)
        return out

    with ExitStack() as sc:
        cf, bfm, bft = load_dft(256, sc, 'd256', (('cf', cfD[256]), ('bf', bfD[256]), ('bft', bftD[256])))
        PQ = load_spectra(256, sc, 'p')
        dkP = {'cf': [('d256', 'cf')], 'bf': [('d256', 'bf')], 'bft': [('d256', 'bft')], 'cfi': [('d256', 'cf')], 'bfti': [('d256', 'bft')],
               ('PQ', 0): [('p', 'Pq', 0), ('p', 'Qq', 0), ('p', 'pn', 0)], ('PQ', 1): [('p', 'Pq', 1), ('p', 'Qq', 1), ('p', 'pn', 1)]}
        pipeline([hyena_conv(2, cf, bfm, bft, cf, bft,
                             lambda st, b=b: vP[:, b * 2 + st, :], lambda tt, b=b: x1P[:, b * 2 + tt, :],
                             x2P[:, :, b * 256:(b + 1) * 256], mixP[:, 4:8, b * 256:(b + 1) * 256], PQ, sc, f'hp{b}', stages=True, dk=dkP) for b in range(4)], 1)
        kb.barrier()
    S_hyP.close()
    with ExitStack() as sc:
        def ld(nm, srcD):
            t_ = kb.sb('d1k' + nm, [128, 8, srcD.shape[1]], BF16, sc)
            kb.dma('sp', t_[:], srcD.rearrange('(st p) k -> p st k', p=128), w=[('d1k', nm)])
            return t_

        def ldsp(o):
            Pq = kb.sb(f'sPq{o}', [128, 8, 512], BF16, sc); Qq = kb.sb(f'sQq{o}', [128, 8, 512], BF16, sc)
            pn = kb.sb(f'spn{o}', [1, 512], F32, sc)
            kb.dma('sp', Pq[:], spP[(1024, o)], w=[('s', 'PQ', o)]); kb.dma('sp', Qq[:], spQ[(1024, o)], w=[('s', 'PQ', o)])
            kb.dma('sp', pn[:], spN[(1024, o)], w=[('s', 'PQ', o)])
            return (Pq, Qq, pn)

        cf = ld('cf', cfD[1024]); bfm = ld('bf', bfD[1024]); PQ0 = ldsp(0)
        bft = ld('bft', bftD[1024]); PQ1 = ldsp(1); cfo = ld('cfo', cfoD); bfto = ld('bfto', bftoD)
        dk = {'cf': [('d1k', 'cf')], 'bf': [('d1k', 'bf')], 'bft': [('d1k', 'bft')], 'cfi': [('d1k', 'cfo')], 'bfti': [('d1k', 'bfto')],
              ('PQ', 0): [('s', 'PQ', 0)], ('PQ', 1): [('s', 'PQ', 1)]}
        hyena_conv(8, cf, bfm, bft, cfo, bfto, lambda st: vS[:, st, :], lambda tt: x1S[:, tt, :], x2O[:, :, :], mixO[:, 4:8, 0:256], [PQ0, PQ1], sc, 'hs', dk=dk)
        kb.barrier()
    S_hy.close()
    dump('hyP', mixP[:, 4:8, 0:256], [128, 4, 256], 'x')
    dump('hyO', mixO[:, 4:8, 0:256], [128, 4, 256], 'x')
    if stop_after <= 5:
        kb.finish()
        return kb

    xmidD = nc.dram_tensor('xmid_scratch', [1280, 1024], F32).ap()
    S7a = ExitStack()
    actT0 = kb.sb('actT0', [128, 22, 512], BF16, S7a)
    ringB = [kb.sb(f'ringB{i}', [128, 8, 512], BF16, S7a) for i in range(3)]
    sg = [kb.sb(f'sg{i}', [128, 512], F32, S7a) for i in range(2)]
    S6 = ExitStack()
    lnbc = {nm: kb.sb('bc_' + nm, [128, 1024], F32, S6) for nm in ('ln1g', 'ln1b')}
    for nm, srcD in (('ln1g', ln1gD), ('ln1b', ln1bD)):
        kb.dma('sp', lnbc[nm][:], srcD.partition_broadcast(128), w=[nm])
    NB6 = 6
    xt6 = [kb.sb(f'xt6_{i}', [128, 1024], F32, S6) for i in range(NB6)]
    y6 = [kb.sb(f'y6_{i}', [128, 1024], F32, S6) for i in range(NB6)]
    xn6 = [kb.sb(f'xn6_{i}', [128, 1024], BF16, S6) for i in range(NB6)]

    def ln_stats(src, srck, S_, M, col0, tag):
        V(lambda: nc.vector.bn_stats(S_[:, 0:6], src[:, 0:512]), r=[srck], w=[tag + 'st0'])
        V(lambda: nc.vector.bn_stats(S_[:, 6:12], src[:, 512:1024]), r=[srck], w=[tag + 'st1'])
        V(lambda: nc.vector.bn_aggr(M[:, col0:col0 + 2], S_[:, :]), r=[tag + 'st0', tag + 'st1'], w=[tag + 'mv'])
        ln_rstd(M[:, col0 + 1:col0 + 2], M[:, col0 + 3:col0 + 4], M[:, col0 + 2:col0 + 3], 128, tag + 'mv', tag + 'rs')
        V(lambda: nc.vector.scalar_tensor_tensor(M[:, col0 + 2:col0 + 3], M[:, col0:col0 + 1], -1.0, M[:, col0 + 3:col0 + 4], ALU.mult, ALU.mult), r=[tag + 'mv', tag + 'rs'], w=[tag + 'nmr'])
        return M[:, col0 + 3:col0 + 4], M[:, col0 + 2:col0 + 3], [tag + 'rs', tag + 'nmr']

    tiles = [(tt, 0, xpD[tt * 128:(tt + 1) * 128, :], mixP, tt * 128) for tt in range(8)] + [(8 + tt, 1, xoD[1 + tt * 128:1 + (tt + 1) * 128, :], mixO, tt * 128) for tt in range(2)]
    acquire(18)
    wo = [ring[18 % NR], ring[19 % NR]]; wok = [('ring', 18 % NR), ('ring', 19 % NR)]
    st6 = [kb.sb(f'st6b_{i}', [128, 24], F32, S6) for i in range(NB6)]
    mv6 = [kb.sb(f'mv6b_{i}', [128, 8], F32, S6) for i in range(NB6)]

    def unit_p6(tile, cvi, xsrc, mix, m0):
        b = tile % NB6
        Y = y6[b]
        S_, M = st6[b], mv6[b]
        st_ = {}

        def sL():
            kb.dma('pool', xt6[b][:], xsrc, w=[('xt6', b)])

        def s0():
            for half in range(2):
                i, k = bank()
                kb.mmg([(lambda kc=kc: MM(pb[i][:, :], lhsT=mix[:, kc, m0:m0 + 128], rhs=wo[half][:, kc, :], start=(kc == 0), stop=(kc == 7))) for kc in range(8)], r=[wok[half]], w=[k, ('mixrd', tile)])
                V(lambda: nc.vector.tensor_tensor(Y[:, half * 512:(half + 1) * 512], pb[i][:, :], gbc[(cvi, 0)][:, half * 512:(half + 1) * 512], ALU.mult), r=[k], w=[('y6', b)])
            V(lambda: nc.vector.scalar_tensor_tensor(Y[:], xt6[b][:], ALPHA, Y[:], ALU.mult, ALU.add), r=[('xt6', b), ('y6', b)], w=[('y6', b)])
            st_['a'] = ln_stats(Y, ('y6', b), S_[:, 0:12], M, 0, f'p6a{b}')

        def s1a():
            sc_, bi_, ks = st_['a']
            A(lambda: nc.scalar.activation(Y[:], Y[:], AF.Identity, bias=bi_, scale=sc_), r=[('y6', b)] + ks, w=[('y6', b)])
            P(lambda: nc.gpsimd.tensor_tensor(Y[:], Y[:], lnbc['ln1g'][:], ALU.mult), r=[('y6', b), 'ln1g'], w=[('y6', b)])
            P(lambda: nc.gpsimd.tensor_tensor(Y[:], Y[:], lnbc['ln1b'][:], ALU.add), r=[('y6', b), 'ln1b'], w=[('y6', b)])
            kb.dma('sp', xmidD[tile * 128:(tile + 1) * 128, :], Y[:], r=[('y6', b)], w=[('xmidD', tile)])

        def s1b():
            st_['b'] = ln_stats(Y, ('y6', b), S_[:, 12:24], M, 4, f'p6b{b}')

        def s2a():
            sc_, bi_, ks = st_['b']
            A(lambda: nc.scalar.activation(xn6[b][:], Y[:], AF.Identity, bias=bi_, scale=sc_), r=[('y6', b)] + ks, w=[('xn6', b)])
            st_['ev'] = transpose_mod(xn6[b], ('xn6', b), 128, mix, 'h2T%d' % tile, m0, cvi, 32, 24, defer=True)

        def s2b():
            st_['ev']()
        return [sL, s0, s1a, s1b, s2a, s2b]

    wupv = wupD.rearrange('(kc p) c -> p kc c', p=128)
    plan2 = []
    for g in range(6):
        ncol = 512 if g < 5 else 256
        plan2.append((wupv[:, :, g * 512:g * 512 + ncol], ncol))
        plan2.append((wupv[:, :, DFF + g * 512:DFF + g * 512 + ncol], ncol))

    class WRing:
        def __init__(self, slots, nblocks=None):
            self.slots = slots; self.issued = 0; self.nblocks = len(plan2) if nblocks is None else nblocks

        def issue_to(self, k):
            n_ = len(self.slots)
            while self.issued < min(self.nblocks, k):
                j = self.issued
                src, ncol = plan2[j]
                t_, key = self.slots[j % n_]
                kb.dma('pool', t_[:, :, 0:ncol], src, w=[key])
                self.issued += 1

        def acquire(self, i):
            self.issue_to(i + len(self.slots))

        def get(self, j):
            return self.slots[j % len(self.slots)]

    wslot = {'r0': (ring[0], ('wslot', 'r0')), 'r1': (ring[1], ('wslot', 'r1')), 'r2': (ring[2], ('wslot', 'r2')),
             'b0': (ringB[0], ('wslot', 'b0')), 'b1': (ringB[1], ('wslot', 'b1')), 'b2': (ringB[2], ('wslot', 'b2'))}

    def ffn_unit(wr, g, cti, chunk, ci, dst, rkeys, resident=False):
        def f():
            if cti == 0 and not resident:
                wr.acquire(2 * g)
            wg, wgk = wr.get(2 * g); wu, wuk = wr.get(2 * g + 1)
            hsrc, h0, nt = chunk
            j = g * 4 + cti
            ig, kg = bank(); iu, ku = bank()
            kb.mmg([(lambda kc=kc: MM(pb[ig][:, 0:nt], lhsT=wg[:, kc, cti * 128:(cti + 1) * 128], rhs=hsrc[:, kc, h0:h0 + nt], start=(kc == 0), stop=(kc == 7))) for kc in range(8)], r=[wgk] + rkeys, w=[kg])
            kb.mmg([(lambda kc=kc: MM(pb[iu][:, 0:nt], lhsT=wu[:, kc, cti * 128:(cti + 1) * 128], rhs=hsrc[:, kc, h0:h0 + nt], start=(kc == 0), stop=(kc == 7))) for kc in range(8)], r=[wuk] + rkeys, w=[ku])
            sb_ = (j * 3 + ci) % 2
            A(lambda: nc.scalar.activation(sg[sb_][:, 0:nt], pb[ig][:, 0:nt], AF.Silu), r=[kg], w=[('sg', sb_)])
            V(lambda: nc.vector.tensor_tensor(dst(j), pb[iu][:, 0:nt], sg[sb_][:, 0:nt], ALU.mult), r=[ku, ('sg', sb_)], w=[('actT', j, ci)])
        return f

    pipeline([unit_p6(*t) for t in tiles[0:4]], 1, order=[0, 1, 5, 4, 2, 3])
    ringX = WRing([wslot[n_] for n_ in ('r2', 'b0', 'b1', 'b2')])
    h2k0 = [('h2T%d' % t_, kc) for t_ in range(4) for kc in range(8)]
    ffn0 = [ffn_unit(ringX, g, cti, (mixP, 0, 512), 0, (lambda j: actT0[:, j, 0:512]), h2k0) for g in range(6) for cti in range(plan2[2 * g][1] // 128)]
    it0 = iter(ffn0)
    for _ in pipeline_gen([unit_p6(*t) for t in tiles[4:]], 1, order=[0, 5, 4, 2, 3, 1]):
        for _q in range(2):
            f_ = next(it0, None)
            if f_ is not None:
                f_()
    for f_ in it0:
        f_()
    kb.barrier()
    S6.close()
    if stop_after <= 6:
        kb.finish()
        return kb

    S7 = ExitStack()
    actT12 = kb.sb('actT12', [128, 22, 768], BF16, S7)
    wdn = kb.sb('wdn', [128, 22, 1024], BF16, S7)
    wdv = wdnD.rearrange('(j p) c -> p j c', p=128)
    st7 = [kb.sb(f'st7_{i}', [128, 12], F32, S7) for i in range(4)]
    mv7 = [kb.sb(f'mv7_{i}', [128, 8], F32, S7) for i in range(4)]
    ringY = WRing([wslot[n_] for n_ in ('r0', 'r1', 'r2', 'b0', 'b1', 'b2')], nblocks=8)
    ringY.issue_to(2)

    def pass2(wr, g, resident):
        for cti in range(plan2[2 * g][1] // 128):
            ffn_unit(wr, g, cti, (mixP, 512, 512), 1, (lambda j: actT12[:, j, 0:512]), [], resident=resident)()
            ffn_unit(wr, g, cti, (mixO, 0, 256), 2, (lambda j: actT12[:, j, 512:768]), [], resident=resident)()

    pass2(ringX, 4, True)
    pass2(ringX, 5, True)
    for g in range(4):
        pass2(ringY, g, False)
        if g == 1:
            for q in range(4):
                j0, j1 = (0, 6, 12, 18)[q], (6, 12, 18, 22)[q]
                kb.dma('pool', wdn[:, j0:j1, :], wdv[:, j0:j1, :], w=[('wdn', q)])
    kb.barrier()
    def f32v(t_, idx):
        return t_[:].rearrange('p a b -> p (a b)').bitcast(F32)[:, idx * 1024:(idx + 1) * 1024]
    lnbc = {'ln2g': f32v(ringB[0], 0), 'ln2b': f32v(ringB[0], 1)}
    for nm, srcD in (('ln2g', ln2gD), ('ln2b', ln2bD)):
        kb.dma('sp', lnbc[nm], srcD.partition_broadcast(128), w=[nm])
    xt7 = [f32v(ringB[1], 0), f32v(ringB[1], 1)]
    y7 = [gbc[(0, 0)][:], gbc[(1, 0)][:], f32v(ringB[2], 0)]

    def act_tile(j, tile):
        return actT0[:, j, tile * 128:(tile + 1) * 128] if tile < 4 else actT12[:, j, (tile - 4) * 128:(tile - 3) * 128]

    def unit_p7(tile, cvi, xsrc, mix, m0):
        b = tile % 3
        bx = tile % 2
        Y = y7[b]
        st_ = {}

        def sL():
            kb.dma('pool', xt7[bx], xmidD[tile * 128:(tile + 1) * 128, :], w=[('xt7', bx)])

        def s0():
            for half in range(2):
                i, k = bank()
                kb.mmg([(lambda j=j: MM(pb[i][:, :], lhsT=act_tile(j, tile), rhs=wdn[:, j, half * 512:(half + 1) * 512], start=(j == 0), stop=(j == 21))) for j in range(22)], r=[('wdn', q) for q in range(4)], w=[k])
                V(lambda: nc.vector.tensor_tensor(Y[:, half * 512:(half + 1) * 512], pb[i][:, :], gbc[(cvi, 1)][:, half * 512:(half + 1) * 512], ALU.mult), r=[k], w=[('y7', b)])
            V(lambda: nc.vector.scalar_tensor_tensor(Y, xt7[bx], ALPHA, Y, ALU.mult, ALU.add), r=[('y7', b), ('xt7', bx)], w=[('y7', b)])
            st_['a'] = ln_stats(Y, ('y7', b), st7[b], mv7[b], 0, f'p7{b}')

        def s1a():
            sc_, bi_, ks = st_['a']
            A(lambda: nc.scalar.activation(Y, Y, AF.Identity, bias=bi_, scale=sc_), r=[('y7', b)] + ks, w=[('y7', b)])
            P(lambda: nc.gpsimd.tensor_tensor(Y, Y, lnbc['ln2g'], ALU.mult), r=[('y7', b), 'ln2g'], w=[('y7', b)])

        def s1():
            V(lambda: nc.vector.tensor_tensor(Y, Y, lnbc['ln2b'], ALU.add), r=[('y7', b), 'ln2b'], w=[('y7', b)])
            if tile < 8:
                kb.dma('sp', ypD[tile * 128:(tile + 1) * 128, :], Y, r=[('y7', b)], w=[('yp', tile)])
            else:
                kb.dma('sp', yoD[(tile - 8) * 128:(tile - 7) * 128, :], Y, r=[('y7', b)], w=[('yo', tile)])
        return [sL, s0, s1a, s1]

    pipeline([unit_p7(*t) for t in tiles], 1, order=[0, 3, 2, 1])
    kb.finish()
    return kb


_NC_CACHE = {}


def _in_maps(inp):
    c = _consts()
    f = lambda a: np.ascontiguousarray(np.asarray(a, dtype=np.float32))
    shared = {
        'mod_w': f(inp['mod_w'][0]), 'mod_b': f(inp['mod_b'][0]), 'w_in': f(inp['w_in'][0]),
        'lq1': f(inp['da_lq1'][0]), 'lk1': f(inp['da_lk1'][0]), 'lq2': f(inp['da_lq2'][0]), 'lk2': f(inp['da_lk2'][0]),
        'subln': f(inp['da_subln'][0]), 'conv_w': f(inp['hy_conv_w'][0]), 'conv_b': f(inp['hy_conv_b'][0]),
        'hw1': f(inp['hy_w1'][0]), 'hb1': f(inp['hy_b1'][0]), 'hw2': f(inp['hy_w2'][0]), 'hb2': f(inp['hy_b2'][0]),
        'hfreq': f(inp['hy_freq'][0]), 'hw3': f(inp['hy_w3'][0]), 'hdecay': f(np.asarray(inp['hy_decay'][0]).reshape(-1)),
        'hskip': f(np.asarray(inp['hy_skip'][0]).reshape(-1)), 'w_out': f(inp['w_out'][0]),
        'ln1_g': f(inp['ln1_g'][0]), 'ln1_b': f(inp['ln1_b'][0]), 'w_up': f(inp['w_up'][0]), 'w_down': f(inp['w_down'][0]),
        'ln2_g': f(inp['ln2_g'][0]), 'ln2_b': f(inp['ln2_b'][0]),
        'identb': c['identb'], 'identf': c['identf'], 'sel2': c['sel2'],
    }
    for n in (256, 1024):
        for nm in ('cf', 'bf', 'bft', 'wk', 'zt', 'nt'):
            shared[f'{nm}{n}'] = c[f'{nm}{n}']
    rope = c['rope']
    shared['ropeS'] = np.ascontiguousarray(rope.reshape(8, 128, 2, 64).transpose(1, 0, 2, 3))
    xp = f(inp['x_prompt']); xs = f(inp['x_sample']); ck = f(inp['cache_k']); cv = f(inp['cache_v'])
    cc = f(inp['c']); cctx = f(inp['c_ctx'])
    maps = []
    for core in range(NCORE):
        b, j = core // 4, core % 4
        m = dict(shared)
        m['xp'] = np.ascontiguousarray(xp[4 * core:4 * core + 4].reshape(1024, 1024))
        m['xs'] = np.ascontiguousarray(xs[b])
        xo = np.zeros((258, 1024), np.float32)
        lo, hi = 256 * j - 1, 256 * j + 257
        slo, shi = max(lo, 0), min(hi, 1024)
        xo[slo - lo:shi - lo] = xs[b, slo:shi]
        m['xo'] = xo
        hm = np.ones((128, 2), np.float32)
        if lo < 0:
            hm[:, 0] = 0.0
        if hi > 1024:
            hm[:, 1] = 0.0
        m['hmask'] = hm
        m['ck'] = np.ascontiguousarray(ck[b, 0].reshape(256, 512))
        m['cv'] = np.ascontiguousarray(cv[b, 0].reshape(256, 512))
        m['cvec'] = np.ascontiguousarray(np.stack([cctx, cc[b]], axis=0))
        m['cfo'] = np.ascontiguousarray(c['cf1024'][:, 256 * j:256 * j + 256])
        m['bfto'] = np.ascontiguousarray(c['bft1024'][:, 256 * j:256 * j + 256])
        m['ropeO'] = np.ascontiguousarray(rope[256 * j:256 * j + 256].reshape(2, 128, 2, 64).transpose(1, 0, 2, 3))
        maps.append(m)
    return maps


def kernel(**inp):
    if 'nc' not in _NC_CACHE:
        _NC_CACHE['nc'] = build().nc
    nc = _NC_CACHE['nc']
    maps = _in_maps(inp)
    res = run_bass_kernel_spmd(nc, maps, core_ids=list(range(NCORE)))
    R = res.results
    y_prompt = np.concatenate([R[c]['yp'].reshape(4, 256, 1024) for c in range(NCORE)], axis=0).astype(np.float32)
    y_sample = np.stack([np.concatenate([R[4 * b + j]['yo'] for j in range(4)], axis=0) for b in range(2)], axis=0).astype(np.float32)
    nk = np.concatenate([R[c]['nk'].reshape(4, 1, 256, 8, 64) for c in range(NCORE)], axis=0).astype(np.float32)
    nv = np.concatenate([R[c]['nv'].reshape(4, 1, 256, 4, 128) for c in range(NCORE)], axis=0).astype(np.float32)
    return (y_prompt, y_sample, nk, nv)
```
